# Optimizing a Trainium2 kernel written in Bass

```python
import jax
import jax.numpy as jnp
from jax import lax
import numpy as np

D_MODEL = 1024
BATCH = 4
SEQ = 4096
DEPTH = 1

GRID_W = 64
CTX_LEN = 256
GM_WIDTH = 1024
GM_GROUPS = 8
GM_GROUP_DIM = GM_WIDTH // GM_GROUPS
GM_CHUNK = 2 * GRID_W
GDN_HEADS = 8
GDN_DK = 128
GDN_DV = 128
GDN_KD = GDN_HEADS * GDN_DK
GDN_VD = GDN_HEADS * GDN_DV
GDN_CONV = 3
GDN_CHUNK = 64
EPS = 1e-6

OFF_Q = 0
OFF_K = OFF_Q + GDN_KD
OFF_V = OFF_K + GDN_KD
OFF_A = OFF_V + GDN_VD
OFF_B = OFF_A + 2 * GDN_HEADS
OFF_ZB = OFF_B + 2 * GDN_HEADS
OFF_UA = OFF_ZB + GDN_VD
OFF_VA = OFF_UA + GM_WIDTH
OFF_ZA = OFF_VA + GM_WIDTH
OFF_G = OFF_ZA + GM_WIDTH
IN_COLS = OFF_G + 2 * D_MODEL

kernel_name = 'hybrid_gmlp_gdeltanet_prefix_dit'


def _rmsnorm(x, g):
    xf = x.astype(jnp.float32)
    y = xf * lax.rsqrt(jnp.mean(xf * xf, axis=-1, keepdims=True) + EPS)
    return (y * g.astype(jnp.float32)).astype(x.dtype)


def _layernorm(x, g, b):
    xf = x.astype(jnp.float32)
    xc = xf - jnp.mean(xf, axis=-1, keepdims=True)
    y = xc * lax.rsqrt(jnp.mean(xc * xc, axis=-1, keepdims=True) + EPS)
    return (y * g.astype(jnp.float32) + b.astype(jnp.float32)).astype(x.dtype)


def _l2norm(x):
    return x * lax.rsqrt(jnp.sum(x * x, axis=-1, keepdims=True) + EPS)


def _heads(u, d):
    return u.reshape(u.shape[:-1] + (u.shape[-1] // d, d))


def _modulation(cond, w_mod, b_mod):
    m = jax.nn.silu(cond) @ w_mod + b_mod
    m = m.reshape((-1, 1, 3, D_MODEL))
    return m[:, :, 0], m[:, :, 1], m[:, :, 2]


def _projector(h, w_in, full):
    if full:
        p = h @ w_in
        return lambda lo, hi: p[..., lo:hi]
    return lambda lo, hi: h @ w_in[:, lo:hi]


def _conv_silu(u, w):
    pad = w.shape[0] // 2
    y = lax.conv_general_dilated(u, w[:, None, :].astype(u.dtype), window_strides=(1,),
                                 padding=[(pad, pad)], dimension_numbers=('NWC', 'WIO', 'NWC'),
                                 feature_group_count=u.shape[-1])
    return jax.nn.silu(y)


def _gdn_side(col, w_conv, a_log, dt_bias, with_q):
    lo = OFF_Q if with_q else OFF_K
    qkv = _conv_silu(col(lo, OFF_A), w_conv[:, lo - OFF_Q:]).astype(jnp.float32)
    k = _l2norm(_heads(qkv[..., -(GDN_KD + GDN_VD):-GDN_VD], GDN_DK))
    v = _heads(qkv[..., -GDN_VD:], GDN_DV)
    q = _l2norm(_heads(qkv[..., :GDN_KD], GDN_DK)) if with_q else None
    ab = col(OFF_A, OFF_ZB).astype(jnp.float32)
    ab = ab.reshape(ab.shape[:-1] + (2, 2, GDN_HEADS))
    g = -jnp.exp(a_log.astype(jnp.float32)) * jax.nn.softplus(ab[..., 0, :, :] + dt_bias.astype(jnp.float32))
    beta = jax.nn.sigmoid(ab[..., 1, :, :])
    return q, k, v, g, beta


def _chunks(t):
    t = jnp.moveaxis(t, 2, 1)
    t = t.reshape(t.shape[:2] + (-1, GDN_CHUNK) + t.shape[3:])
    return jnp.moveaxis(t, 2, 0)


def _unchunk(t):
    n, b, h, cl, d = t.shape
    return t.transpose(1, 0, 3, 2, 4).reshape(b, n * cl, h, d)


def _gdn_direction(q, k, v, beta, g, s0):
    kc, vc, bc, gc = (_chunks(t) for t in (k, v, beta, g))
    gcum = jnp.cumsum(gc, axis=-1)
    idx = jnp.arange(GDN_CHUNK)
    incl = idx[:, None] >= idx[None, :]
    decay = jnp.exp(jnp.where(incl, gcum[..., :, None] - gcum[..., None, :], -jnp.inf))
    kb = kc * bc[..., None]
    a_mat = jnp.where(idx[:, None] > idx[None, :],
                      jnp.einsum('nbhcd,nbhsd->nbhcs', kb, kc) * decay, 0.0)
    rhs = jnp.concatenate([vc * bc[..., None], kb * jnp.exp(gcum)[..., None]], axis=-1)
    sol = lax.linalg.triangular_solve(a_mat + jnp.eye(GDN_CHUNK, dtype=jnp.float32), rhs,
                                      left_side=True, lower=True)
    u_val, w_key = sol[..., :GDN_DV], sol[..., GDN_DV:]
    g_last = gcum[..., -1]
    k_tail = kc * jnp.exp(g_last[..., None] - gcum)[..., None]

    def advance(S, u_c, w_c, kt_c, gl_c):
        v_new = u_c - jnp.einsum('bhcd,bhde->bhce', w_c, S)
        S_new = S * jnp.exp(gl_c)[..., None, None] + jnp.einsum('bhcd,bhce->bhde', kt_c, v_new)
        return v_new, S_new

    if q is None:
        def state_step(S, xs):
            _, S_new = advance(S, *xs)
            return S_new, None
        s_final, _ = lax.scan(state_step, s0, (u_val, w_key, k_tail, g_last))
        return None, s_final

    qc = _chunks(q) * (GDN_DK ** -0.5)
    attn = jnp.einsum('nbhcd,nbhsd->nbhcs', qc, kc) * decay
    q_dec = qc * jnp.exp(gcum)[..., None]

    def out_step(S, xs):
        u_c, w_c, kt_c, gl_c, qd_c, at_c = xs
        v_new, S_new = advance(S, u_c, w_c, kt_c, gl_c)
        o = jnp.einsum('bhcd,bhde->bhce', qd_c, S) + jnp.einsum('bhcs,bhse->bhce', at_c, v_new)
        return S_new, o

    s_final, o = lax.scan(out_step, s0, (u_val, w_key, k_tail, g_last, q_dec, attn))
    return _unchunk(o), s_final


def _gdn_bidir(q, k, v, g, beta, s0_f, s0_b):
    rev = lambda t: None if t is None else jnp.flip(t, axis=1)
    o_f, s_f = _gdn_direction(q, k, v, beta[:, :, 0], g[:, :, 0], s0_f)
    o_b, s_b = _gdn_direction(rev(q), rev(k), rev(v), rev(beta[:, :, 1]), rev(g[:, :, 1]), s0_b)
    o = None if q is None else o_f + rev(o_b)
    return o, s_f, s_b


def _gmlp(pu, pv, pz, ln_g, ln_b, w_sp, b_sp, n_chunks):
    u = jax.nn.gelu(pu)
    v = _layernorm(jax.nn.gelu(pv), ln_g, ln_b)
    b, l, _ = v.shape
    v = v.reshape(b, n_chunks, GM_CHUNK, GM_GROUPS, GM_GROUP_DIM)
    s = jnp.einsum('gpq,bnqgc->bnpgc', w_sp, v) + jnp.transpose(b_sp)[:, :, None]
    return u * s.reshape(b, l, GM_WIDTH) * jax.nn.silu(pz)


def _stream_out(col, o_gdn, n_chunks, g_onorm, gm_ln_g, gm_ln_b, w_sp, b_sp, w_pa, w_pb, w_out):
    zb_raw = col(OFF_ZB, OFF_UA)
    z_b = _heads(zb_raw, GDN_DV).astype(jnp.float32)
    y_b = _rmsnorm(o_gdn, g_onorm) * jax.nn.silu(z_b)
    y_b = y_b.reshape(y_b.shape[:-2] + (GDN_VD,)).astype(zb_raw.dtype)
    y_a = _gmlp(col(OFF_UA, OFF_VA), col(OFF_VA, OFF_ZA), col(OFF_ZA, OFF_G),
                gm_ln_g, gm_ln_b, w_sp, b_sp, n_chunks)
    gates = jax.nn.sigmoid(col(OFF_G, IN_COLS))
    merged = gates[..., :D_MODEL] * (y_a @ w_pa) + gates[..., D_MODEL:] * (y_b @ w_pb)
    return merged @ w_out


def _hybrid_layer(x, ctx, mod_x, mod_c, n_chunks, update_ctx, g_pre, g_post, w_in, w_conv,
                  a_log, dt_bias, g_onorm, gm_ln_g, gm_ln_b, w_sp, b_sp, w_pa, w_pb, w_out):
    shift_x, scale_x, gate_x = mod_x
    shift_c, scale_c, gate_c = mod_c
    s_zero = jnp.zeros((ctx.shape[0], GDN_HEADS, GDN_DK, GDN_DV), jnp.float32)
    branch_w = (g_onorm, gm_ln_g, gm_ln_b, w_sp, b_sp, w_pa, w_pb, w_out)
    h_c = _rmsnorm(ctx, g_pre) * (1 + scale_c) + shift_c
    col_c = _projector(h_c, w_in, update_ctx)
    q_c, k_c, v_c, g_c, beta_c = _gdn_side(col_c, w_conv, a_log, dt_bias, update_ctx)
    o_c, s_fwd, s_bwd = _gdn_bidir(q_c, k_c, v_c, g_c, beta_c, s_zero, s_zero)
    h_x = _rmsnorm(x, g_pre) * (1 + scale_x) + shift_x
    col_x = _projector(h_x, w_in, True)
    q_x, k_x, v_x, g_x, beta_x = _gdn_side(col_x, w_conv, a_log, dt_bias, True)
    o_x, _, _ = _gdn_bidir(q_x, k_x, v_x, g_x, beta_x, s_fwd, s_bwd)
    x = x + gate_x * _rmsnorm(_stream_out(col_x, o_x, n_chunks, *branch_w), g_post)
    if update_ctx:
        ctx = ctx + gate_c * _rmsnorm(_stream_out(col_c, o_c, ctx.shape[1] // GM_CHUNK, *branch_w), g_post)
    return x, ctx


def setup_inputs(seed: int = 0) -> dict:
    key = jax.random.key(seed)
    ks = jax.random.split(key, 20)
    nrm = lambda k, shape, s: s * jax.random.normal(k, shape, jnp.float32)
    d = D_MODEL
    x = nrm(ks[0], (BATCH, SEQ, d), 1.0)
    c = nrm(ks[1], (BATCH, d), 1.0)
    ctx = nrm(ks[2], (BATCH, CTX_LEN, d), 1.0)
    c_ctx = nrm(ks[3], (d,), 1.0)
    w_mod = nrm(ks[4], (DEPTH, d, 3 * d), 0.5 * d ** -0.5)
    b_mod = nrm(ks[5], (DEPTH, 3 * d), 0.02)
    g_pre = 1.0 + nrm(ks[6], (DEPTH, d), 0.05)
    g_post = 1.0 + nrm(ks[7], (DEPTH, d), 0.05)
    w_in = nrm(ks[8], (DEPTH, d, IN_COLS), d ** -0.5)
    w_conv = nrm(ks[9], (DEPTH, GDN_CONV, OFF_A), GDN_CONV ** -0.5)
    a_log = jnp.log(jax.random.uniform(ks[10], (DEPTH, 2, GDN_HEADS), jnp.float32, 1.0, 16.0))
    dt = jnp.exp(jax.random.uniform(ks[11], (DEPTH, 2, GDN_HEADS), jnp.float32,
                                    float(np.log(1e-3)), float(np.log(1e-1))))
    dt_bias = dt + jnp.log(-jnp.expm1(-dt))
    g_onorm = 1.0 + nrm(ks[12], (DEPTH, GDN_DV), 0.05)
    gm_ln_g = 1.0 + nrm(ks[13], (DEPTH, GM_WIDTH), 0.05)
    gm_ln_b = nrm(ks[14], (DEPTH, GM_WIDTH), 0.02)
    w_sp = nrm(ks[15], (DEPTH, GM_GROUPS, GM_CHUNK, GM_CHUNK), GM_CHUNK ** -0.5)
    b_sp = 1.0 + nrm(ks[16], (DEPTH, GM_GROUPS, GM_CHUNK), 0.02)
    w_pa = nrm(ks[17], (DEPTH, GM_WIDTH, d), GM_WIDTH ** -0.5)
    w_pb = nrm(ks[18], (DEPTH, GDN_VD, d), GDN_VD ** -0.5)
    w_out = nrm(ks[19], (DEPTH, d, d), d ** -0.5)
    return {'x': x, 'c': c, 'ctx': ctx, 'c_ctx': c_ctx, 'w_mod': w_mod, 'b_mod': b_mod,
            'g_pre': g_pre, 'g_post': g_post, 'w_in': w_in, 'w_conv': w_conv, 'a_log': a_log,
            'dt_bias': dt_bias, 'g_onorm': g_onorm, 'gm_ln_g': gm_ln_g, 'gm_ln_b': gm_ln_b,
            'w_sp': w_sp, 'b_sp': b_sp, 'w_pa': w_pa, 'w_pb': w_pb, 'w_out': w_out}


def reference(x, c, ctx, c_ctx, w_mod, b_mod, g_pre, g_post, w_in, w_conv, a_log, dt_bias,
              g_onorm, gm_ln_g, gm_ln_b, w_sp, b_sp, w_pa, w_pb, w_out):
    rows = x.shape[1] // GRID_W
    n_chunks = rows // (GM_CHUNK // GRID_W)
    for i in range(DEPTH):
        mod_x = _modulation(c, w_mod[i], b_mod[i])
        mod_c = _modulation(c_ctx, w_mod[i], b_mod[i])
        x, ctx = _hybrid_layer(x, ctx, mod_x, mod_c, n_chunks, i + 1 < DEPTH, g_pre[i], g_post[i],
                               w_in[i], w_conv[i], a_log[i], dt_bias[i], g_onorm[i], gm_ln_g[i],
                               gm_ln_b[i], w_sp[i], b_sp[i], w_pa[i], w_pb[i], w_out[i])
    return x
```

```python
from contextlib import ExitStack
import numpy as np
import concourse.bass as bass
import concourse.mybir as mybir
from concourse.bass_utils import run_bass_kernel_spmd

F32 = mybir.dt.float32
BF16 = mybir.dt.bfloat16
AF = mybir.ActivationFunctionType
ALU = mybir.AluOpType
AX = mybir.AxisListType

D = 1024
NOWN = 2048
NTO = 16
NCTX = 256
EPS = 1e-6
OFF_A = 3072
OFF_ZB = 3104
OFF_UA = 4128
OFF_VA = 5152
OFF_ZA = 6176
OFF_G = 7200
EPOCH = 8000
NEGV = -30000.0

C_ID = 0
C_LE = 128
C_GT = 256
C_GE = 384
C_LT = 512
C_NEGU = 640
C_NEGL = 1152
C_NM = 1664
NCONST = C_NM + 14 * 128


class _Op:
    __slots__ = ("eng", "dma", "grp", "gseq", "deps", "sig", "signo", "waits")

    def __init__(self, eng, dma, grp):
        self.eng = eng
        self.dma = dma
        self.grp = grp
        self.gseq = 0
        self.deps = []
        self.sig = False
        self.signo = 0
        self.waits = []


class Prog:
    ENGS = ("pe", "dve", "act", "pool", "sp")

    def __init__(self, nc, plan=None):
        self.nc = nc
        self.plan = plan
        self.dry = plan is None
        self.ops = []
        self.last_w = {}
        self.readers = {}
        self.shared_w = {}
        self.grp_count = {}
        self.idx = 0
        self.last_on = {}
        self.last_dma = {}
        self.pending = {}
        self.serial_dep = None
        self.engs = {"pe": nc.tensor, "dve": nc.vector, "act": nc.scalar,
                     "pool": nc.gpsimd, "sp": nc.sync}

    def _record(self, eng, reads, writes, dma, grp):
        op = _Op(eng, dma, grp)
        oid = len(self.ops)
        deps = {}
        for k in reads:
            w = self.last_w.get(k)
            if w is not None:
                deps[w] = True
            for w in self.shared_w.get(k, ()):
                deps[w] = True
        for k in writes:
            if isinstance(k, tuple) and len(k) == 2 and k[0] == "~":
                kk = k[1]
                w = self.last_w.get(kk)
                if w is not None and w not in deps:
                    deps[w] = False
                for r in self.readers.get(kk, ()):
                    if r not in deps:
                        deps[r] = False
                continue
            w = self.last_w.get(k)
            if w is not None and w not in deps:
                deps[w] = False
            for w in self.shared_w.get(k, ()):
                if w not in deps:
                    deps[w] = False
            for r in self.readers.get(k, ()):
                if r not in deps:
                    deps[r] = False
        for k in reads:
            self.readers.setdefault(k, []).append(oid)
        for k in writes:
            if isinstance(k, tuple) and len(k) == 2 and k[0] == "~":
                self.shared_w.setdefault(k[1], []).append(oid)
                continue
            self.last_w[k] = oid
            self.readers[k] = []
            self.shared_w[k] = []
        if eng in self.pending:
            for did in self.pending.pop(eng):
                deps.setdefault(did, True)
        if self.serial_dep is not None:
            deps.setdefault(self.serial_dep, True)
            self.serial_dep = None
        op.deps = list(deps.items())
        if dma:
            self.grp_count[grp] = self.grp_count.get(grp, 0) + 1
            op.gseq = self.grp_count[grp]
            self.last_dma[grp] = oid
        else:
            self.last_on[eng] = oid
        self.ops.append(op)

    def barrier(self):
        if self.dry:
            snap = list(self.last_on.values()) + list(self.last_dma.values())
            self.pending = {e: list(snap) for e in self.ENGS}

    def analyze(self):
        ops = self.ops

        def need(op, d, is_raw):
            if d.eng != op.eng:
                return True
            if op.eng == "pe":
                return False
            return True

        for op in ops:
            best = {}
            for (did, is_raw) in op.deps:
                d = ops[did]
                if not d.dma and need(op, d, is_raw):
                    if best.get(d.eng, -1) < did:
                        best[d.eng] = did
            for did in best.values():
                ops[did].sig = True
        cnt = {e: 0 for e in self.ENGS}
        for op in ops:
            if op.sig and not op.dma:
                cnt[op.eng] += 1
                op.signo = cnt[op.eng]
        waited = {e: {} for e in self.ENGS}
        for op in ops:
            wl = waited[op.eng]
            ne, ng = {}, {}
            for (did, is_raw) in op.deps:
                d = ops[did]
                if d.dma:
                    ng[d.grp] = max(ng.get(d.grp, 0), d.gseq)
                elif need(op, d, is_raw) and d.sig:
                    ne[d.eng] = max(ne.get(d.eng, 0), d.signo)
            for g, s in ng.items():
                if wl.get(("g", g), 0) < s:
                    op.waits.append(("g", g, s))
                    wl[("g", g)] = s
            for e, s in ne.items():
                if wl.get(e, 0) < s:
                    op.waits.append(("e", e, s))
                    wl[e] = s
        return {"ops": ops, "cnt": cnt, "grp": dict(self.grp_count)}

    def setup_sems(self, stack):
        nc = self.nc
        self.sems = {}
        for e, c in self.plan["cnt"].items():
            n = max(1, (c + EPOCH - 1) // EPOCH)
            self.sems[e] = [stack.enter_context(nc.semaphore(f"s_{e}_{i}")) for i in range(n)]
        self.gsem = {}
        for i, g in enumerate(self.plan["grp"]):
            self.gsem[g] = stack.enter_context(nc.semaphore(f"g{i}"))

    def _emit(self, eng, fn):
        op = self.plan["ops"][self.idx]
        assert op.eng == eng, (self.idx, op.eng, eng)
        E = self.engs[eng]
        for (kind, key, s) in op.waits:
            if kind == "g":
                E.wait_ge(self.gsem[key], 16 * s)
            else:
                E.wait_ge(self.sems[key][(s - 1) // EPOCH], (s - 1) % EPOCH + 1)
        ins = fn(E)
        if op.dma:
            ins.then_inc(self.gsem[op.grp], 16)
        elif op.sig:
            ins.then_inc(self.sems[eng][(op.signo - 1) // EPOCH], 1)
        self.idx += 1

    def op(self, eng, fn, reads=(), writes=()):
        if self.dry:
            self._record(eng, reads, writes, False, None)
        else:
            self._emit(eng, fn)

    def dma(self, out, in_, reads=(), writes=(), grp=None, eng="sp", serial=False):
        if self.dry:
            if serial and grp in self.last_dma:
                self.serial_dep = self.last_dma[grp]
            self._record(eng, reads, writes, True, grp)
        else:
            self._emit(eng, lambda e: e.dma_start(out=out, in_=in_))

    def finish(self, groups):
        if not self.dry:
            E = self.engs["sp"]
            for g in groups:
                E.wait_ge(self.gsem[g], 16 * self.plan["grp"][g])


def bcast_mid(ap2d, n):
    a = ap2d.ap
    return bass.AP(ap2d.tensor, ap2d.offset, [list(a[0]), [0, n], list(a[1])])


def bcast_last(ap2d, n):
    return ap2d.unsqueeze(2).to_broadcast([ap2d.shape[0], ap2d.shape[1], n])


class Builder:
    def __init__(self, nc, P, debug=None):
        self.nc = nc
        self.P = P
        self.debug = debug
        self.dbg_groups = []
        self.bg = None
        self.hti = 0
        self.spawned = []
        self.gsfx = ""
        self.psi = 0
        self.ps_live = set()

    def mm(self, out, lhsT, rhs, start, stop, r, w, skip=False):
        if skip:
            self.P.op("pe", lambda e: e.matmul(out, lhsT=lhsT, rhs=rhs, start=start, stop=stop, skip_group_check=True), r, w)
        else:
            self.P.op("pe", lambda e: e.matmul(out, lhsT=lhsT, rhs=rhs, start=start, stop=stop), r, w)

    def tr(self, out, in_, r, w):
        idb = self.id_bf
        self.P.op("pe", lambda e: e.transpose(out, in_, idb[:]), tuple(r) + ("idbf",), w)

    def act(self, out, in_, func, r, w, scale=None, bias=None, accum=None):
        kw = {}
        if scale is not None:
            kw["scale"] = scale
        if bias is not None:
            kw["bias"] = bias
        if accum is not None:
            kw["accum_out"] = accum
        self.P.op("act", lambda e: e.activation(out=out, in_=in_, func=func, **kw), r, w)

    def tt(self, eng, out, in0, in1, op, r, w):
        self.P.op(eng, lambda e: e.tensor_tensor(out=out, in0=in0, in1=in1, op=op), r, w)

    def ts(self, eng, out, in0, s1, s2, op0, op1, r, w):
        if s2 is None:
            self.P.op(eng, lambda e: e.tensor_scalar(out=out, in0=in0, scalar1=s1, scalar2=None, op0=op0), r, w)
        else:
            self.P.op(eng, lambda e: e.tensor_scalar(out=out, in0=in0, scalar1=s1, scalar2=s2, op0=op0, op1=op1), r, w)

    def stt(self, out, in0, scalar, in1, op0, op1, r, w):
        self.P.op("dve", lambda e: e.scalar_tensor_tensor(out=out, in0=in0, scalar=scalar, in1=in1, op0=op0, op1=op1), r, w)

    def cp(self, eng, out, in_, r, w):
        if eng == "act":
            self.P.op("act", lambda e: e.copy(out=out, in_=in_), r, w)
        else:
            self.P.op(eng, lambda e: e.tensor_copy(out=out, in_=in_), r, w)

    def memset(self, eng, ap, val, w):
        self.P.op(eng, lambda e: e.memset(ap, val), (), w)

    def ps(self):
        for j in range(len(self.psb)):
            i = (self.psi + j) % len(self.psb)
            if i not in self.ps_live:
                self.psi = i + 1
                return self.psb[i], ("ps", i)
        raise RuntimeError("no free PSUM bank")

    def dump(self, name, ap, keys):
        if not self.debug:
            return
        t = self.nc.dram_tensor("dbg_" + name, list(ap.shape), ap.dtype, kind="ExternalOutput").ap()
        self.P.dma(t, ap, tuple(keys), (), grp="dbg_" + name)
        self.dbg_groups.append("dbg_" + name)

    def sb(self, st, name, shape, dt=F32, side="left"):
        self.uid = getattr(self, "uid", 0) + 1
        return st.enter_context(self.nc.sbuf_tensor(f"{name}_{self.uid}", shape, dt, side=side))

    def rsqrt_small(self, out, in_, scale, r, w):
        k = out.shape[1]
        self.ts("dve", out, in_, float(scale), EPS, ALU.mult, ALU.add, r, w)
        self.tt("pool", out, out, self.negh[:, 0:k], ALU.pow, tuple(w) + ("negh",), w)

    def load_w(self, dst, name, dram, col0, ncols, dcol0=0):
        CH = 256
        c = 0
        while c < ncols:
            n = min(CH, ncols - c)
            src = dram[:, col0 + c:col0 + c + n].rearrange("(kc p) c -> p kc c", p=128)
            keys = tuple({(name, (dcol0 + c + j) // 128) for j in range(0, n, 128)} | {(name, (dcol0 + c + n - 1) // 128)})
            grp = f"wq{self.wq % 8}"
            self.wq += 1
            self.P.dma(dst[:, :, dcol0 + c:dcol0 + c + n], src, (), keys, grp=grp, eng="pool", serial=True)
            c += n

    @staticmethod
    def wk(name, c0, c1):
        return tuple((name, j) for j in range(c0 // 128, (c1 - 1) // 128 + 1))

    def build(self, io):
        nc, P = self.nc, self.P
        with ExitStack() as g:
            sb = lambda name, shape, dt=F32: self.sb(g, name, shape, dt, side="right")
            self.pt = g.enter_context(nc.psum_tensor("pt", [128, 8, 128], BF16))
            self.psb = [g.enter_context(nc.psum_tensor(f"psb{i}", [128, 512], F32)) for i in range(7)]
            self.id_bf = sb("id_bf", [128, 128], BF16)
            self.ones_bf = sb("ones_bf", [128, 128], BF16)
            self.negd = sb("negd", [128, 128], BF16)
            self.cst = sb("cst", [128, 4])
            self.negh = sb("negh", [128, 16])
            self.fpar = sb("fpar", [128, 104])
            self.gate_bc = sb("gate_bc", [128, 1024])
            self.hTo = sb("hTo", [128, 8, NOWN], BF16)
            self.hTh = sb("hTh", [128, 8, 1], BF16)
            self.wq = 0
            self.sml = sb("sml", [128, 64])
            self.tmpA = sb("tmpA", [128, 1024])
            self.tmpB = sb("tmpB", [128, 1024])
            self.xi = 0

            self.memset("pool", self.cst[:, 0:1], EPS, ("cst",))
            self.memset("pool", self.cst[:, 1:2], float(np.log(128.0 ** -0.5)), ("cst",))
            self.memset("pool", self.ones_bf[:], 1.0, ("ones",))
            self.memset("pool", self.negh[:], -0.5, ("negh",))
            P.dma(self.fpar[:], io["fparams"], (), ("fpar",), grp="fpar")

            son = ExitStack()
            with son:
                with ExitStack() as s1:
                    self.phase_gdn(s1, son, io)
                P.barrier()
                with ExitStack() as s3:
                    self.Bg = self.sb(s3, "Bg", [128, 8, NOWN], BF16)
                    self.phase3a(io)
                    self.dump("Bg", self.Bg[:], tuple(("Bg", b) for b in range(4)))
                    P.barrier()
                    son.close()
                    self.phase3b(s3, io)
        P.finish(["yout0", "yout1"] + self.dbg_groups)

    def alloc_x(self, st, n=2):
        self.nx = n
        self.xst = [self.sb(st, f"xst{i}", [128, 1024]) for i in range(n)]
        self.xbf = [self.sb(st, f"xbf{i}", [128, 1024], BF16) for i in range(n)]

    def phase_gdn(self, st, son, io):
        nc, P = self.nc, self.P
        sb = lambda name, shape, dt=F32: self.sb(st, name, shape, dt)
        self.cf = sb("cf", [128, C_NEGU])
        self.negb = sb("negb", [128, 2, 512], BF16)
        self.nm = sb("nm", [128, 14, 128], BF16)
        GN = ("gall", "beta", "gam", "bgam", "eta", "egl")
        own = {n: sb(n + "_o", [128, NTO, 16]) for n in GN}
        self.Sst = sb("Sst", [128, 16, 128])
        self.modfm = sb("modfm", [128, 16, 2])
        self.scale1 = sb("scale1", [128, 8, 2])
        P.dma(self.cf[:], io["consts"][:, 0:C_NEGU], (), ("cf",), grp="cf")
        self.cp("dve", self.id_bf[:], self.cf[:, C_ID:C_ID + 128], ("cf",), ("idbf",))
        self.ts("dve", self.negd[:], self.cf[:, C_ID:C_ID + 128], NEGV, None, ALU.mult, None, ("cf",), ("negd",))
        with ExitStack() as s0:
            nmst = self.sb(s0, "nmst", [128, NCONST - C_NEGU])
            P.dma(nmst[:], io["consts"][:, C_NEGU:NCONST], (), ("nmst",), grp="nmst")
            self.cp("dve", self.negb[:].rearrange("p a b -> p (a b)"), nmst[:, 0:1024], ("nmst",), ("negb",))
            self.cp("dve", self.nm[:].rearrange("p a b -> p (a b)"), nmst[:, 1024:], ("nmst",), ("nm",))
            self.phase0(s0, io)
        P.barrier()
        self.dump("gate_bc", self.gate_bc[:], ("gate_bc",))
        self.dump("scale1", self.scale1[:], ("scale1",))
        self.dump("modfm", self.modfm[:], ("modfm",))
        with ExitStack() as s1:
            self.hTx = self.sb(s1, "hTx", [128, 8, NOWN], BF16)
            self.hTc = self.sb(s1, "hTc", [128, 8, NCTX], BF16)
            for n in GN:
                setattr(self, n, self.sb(s1, n, [128, 34, 16]))
            self.gsfx = ""
            with ExitStack() as sx:
                self.alloc_x(sx, 4)
                self.build_hT(io)
            P.barrier()
            with ExitStack() as sab:
                self.ab_stage(sab, io)
            P.barrier()
            for n in GN:
                self.cp("pool", own[n][:], getattr(self, n)[:, 0:NTO, :], (n,), (n + "_o",))
            self.dump("hTo", self.hTo[:], tuple(("hTo", b) for b in range(4)))
            self.dump("hTx", self.hTx[:], tuple(("hTx", b) for b in range(4)))
            self.dump("hTc", self.hTc[:], (("hTc", 0),))
            for nm_ in ("gall", "beta", "gam", "eta", "egl", "bgam"):
                self.dump(nm_, getattr(self, nm_)[:], (nm_,))
            self.memset("pool", self.Sst[:], 0.0, ("Sst",) + tuple(("S", c) for c in range(16)))
            with ExitStack() as s2:
                self.alloc_head_bufs(s2, False)
                for h in range(8):
                    self.head_pass1(h, io)
                self.dump("Sst", self.Sst[:], tuple(("S", c) for c in range(16)))
        P.barrier()
        self.on_store = self.sb(son, "on_store", [128, NTO, 1024], BF16, side="right")
        for n in GN:
            setattr(self, n, own[n])
        self.gsfx = "_o"
        with ExitStack() as s2:
            self.alloc_head_bufs(s2, True)
            self.gon = self.sb(s2, "gon", [128, 128])
            P.dma(self.gon[:], io["r_gon"], (), ("gon",), grp="gon")
            self.pass2_all(io)
            self.dump("on_store", self.on_store[:], ("on_store",))
        P.barrier()

    def phase0(self, st, io):
        P = self.P
        sb = lambda name, shape, dt=F32: self.sb(st, name, shape, dt)
        cv = sb("cv", [128, 16])
        sc = sb("sc", [128, 16])
        screp = sb("screp", [128, 8, 128])
        wm = [sb(f"wm{i}", [128, 8, 512]) for i in range(6)]
        rg = sb("rg", [128, 1024])
        rb = sb("rb", [128, 1024])
        P.dma(cv[:], io["cvec"], (), ("cv",), grp="cv")
        P.dma(rg[:], io["r_gpost"], (), ("rg",), grp="rg")
        P.dma(rb[:], io["r_bmodg"], (), ("rb",), grp="rb")
        self.act(sc[:], cv[:], AF.Silu, ("cv",), ("sc",))
        sc3 = sc[:].rearrange("p (k j) -> p k j", j=2)
        self.cp("dve", screp[:], sc3[:, :, 0:1].to_broadcast([128, 8, 128]), ("sc",), ("screp",))
        pm, pmk = self.ps()
        for piece in range(6):
            w = wm[piece]
            wkey = ("wm", piece)
            src = io["w_mod"][:, piece * 512:(piece + 1) * 512].rearrange("(kc p) c -> p kc c", p=128)
            P.dma(w[:], src, (), (wkey,), grp=f"wm{piece}")
            if piece < 4:
                for i in range(4):
                    blk = piece * 4 + i
                    for kc in range(8):
                        self.mm(pm[:, blk * 2:blk * 2 + 2], w[:, kc, i * 128:(i + 1) * 128], sc[:, kc * 2:kc * 2 + 2],
                                kc == 0, kc == 7, (wkey, "sc"), (pmk,))
            else:
                pg, pgk = self.ps()
                for kc in range(8):
                    self.mm(pg[:], screp[:, kc, :], w[:, kc, :], kc == 0, kc == 7, (wkey, "screp"), (pgk,))
                c0 = (piece - 4) * 512
                self.tt("dve", self.tmpA[:, 0:512], pg[:], rb[:, c0:c0 + 512], ALU.add, (pgk, "rb"), ("tmpA",))
                self.tt("dve", self.gate_bc[:, c0:c0 + 512], self.tmpA[:, 0:512], rg[:, c0:c0 + 512], ALU.mult,
                        ("tmpA", "rg"), ("gate_bc",))
        self.tt("dve", self.modfm[:], pm[:, 0:32].rearrange("p (b j) -> p b j", j=2),
                bcast_last(self.fpar[:, 8:24], 2), ALU.add, (pmk, "fpar"), ("modfm",))
        self.ts("dve", self.scale1[:], self.modfm[:, 8:16, :], 1.0, None, ALU.add, None, ("modfm",), ("scale1",))
        self.tt("dve", self.scale1[:], self.scale1[:], bcast_last(self.fpar[:, 0:8], 2), ALU.mult,
                ("scale1", "fpar"), ("scale1",))

    def hT_stageA(self, src_rows):
        P = self.P
        i = self.xi % self.nx
        self.xi += 1
        xs, xb = self.xst[i], self.xbf[i]
        sm = self.sml
        P.dma(xs[:], src_rows, (), (("xst", i),), grp=f"xst{i}")
        c = 2 * i
        self.act(xb[:], xs[:], AF.Square, (("xst", i),), (("xbf", i), ("sml", c)), accum=sm[:, c:c + 1])
        self.rsqrt_small(sm[:, c + 1:c + 2], sm[:, c:c + 1], 1.0 / D, (("sml", c),), (("sml", c + 1),))
        self.ts("dve", xb[:], xs[:], sm[:, c + 1:c + 2], None, ALU.mult, None,
                (("xst", i), ("sml", c + 1)), (("xbf", i),))
        return i

    def hT_stageB(self, i, dst, dkey, tok0, j):
        xb = self.xbf[i]
        q = self.hti % 2
        self.hti += 1
        def bank(n):
            if n == 0:
                return self.pt, "pt"
            return self.psb[3 + n][:].bitcast(BF16).rearrange("p (a b) -> p a b", b=128), ("ps", 3 + n)
        bA, kA = bank(2 * q)
        bB, kB = bank(2 * q + 1)
        for kc in range(8):
            b_, k_ = (bA, kA) if kc % 2 == 0 else (bB, kB)
            self.tr(b_[:, kc // 2, :], xb[:, kc * 128:(kc + 1) * 128], (("xbf", i),), (k_,))
        for kc in range(8):
            if kc % 2 == 0:
                self.act(dst[:, kc, tok0:tok0 + 128], bA[:, kc // 2, :], AF.Identity,
                         (kA, "scale1", "modfm"), (("~", dkey),),
                         scale=self.scale1[:, kc, j:j + 1], bias=self.modfm[:, kc, j:j + 1])
            else:
                self.ts("dve", dst[:, kc, tok0:tok0 + 128], bB[:, kc // 2, :], self.scale1[:, kc, j:j + 1],
                        self.modfm[:, kc, j:j + 1], ALU.mult, ALU.add, (kB, "scale1", "modfm"), (("~", dkey),))

    def build_hT(self, io):
        tiles = []
        for t in range(NTO):
            tiles.append((io["x_own"][t * 128:(t + 1) * 128, :], self.hTo, ("hTo", t // 4), t * 128, 0))
        for t in range(NTO):
            tiles.append((io["x_oth"][t * 128:(t + 1) * 128, :], self.hTx, ("hTx", t // 4), t * 128, 0))
        for t in range(2):
            tiles.append((io["ctx"][t * 128:(t + 1) * 128, :], self.hTc, ("hTc", 0), t * 128, 1))
        pend = None
        for (src, dst, dkey, tok0, j) in tiles:
            i = self.hT_stageA(src)
            if pend is not None:
                self.hT_stageB(*pend)
            pend = (i, dst, dkey, tok0, j)
        self.hT_stageB(*pend)
        self.cp("pool", self.hTh[:], self.hTx[:, :, 0:1], (("hTx", 0),), ("hTh",))

    def hT_of(self, T):
        if T < 16:
            return self.hTo, ("hTo", T // 4), T * 128
        if T < 32:
            return self.hTx, ("hTx", (T - 16) // 4), (T - 16) * 128
        return self.hTc, ("hTc", 0), (T - 32) * 128

    def ab_stage(self, st, io):
        P = self.P
        sb = lambda name, shape, dt=F32: self.sb(st, name, shape, dt)
        wab = sb("wab", [128, 8, 32], BF16)
        ab = sb("ab", [128, 34, 32])
        rdtb = sb("rdtb", [128, 16])
        rneg = sb("rneg", [128, 16])
        P.dma(wab[:], io["w_ab"].rearrange("(kc p) c -> p kc c", p=128), (), ("wab",), grp="wab", eng="pool")
        P.dma(rdtb[:], io["r_dtb"], (), ("rdtb",), grp="rdtb")
        P.dma(rneg[:], io["r_alog"], (), ("rneg",), grp="rneg")
        for b0 in range(0, 34, 16):
            nb = min(16, 34 - b0)
            pa, pak = self.ps()
            for i in range(nb):
                T = b0 + i
                hT, hk, t0 = self.hT_of(T)
                for kc in range(8):
                    self.mm(pa[:, i * 32:(i + 1) * 32], hT[:, kc, t0:t0 + 128], wab[:, kc, :], kc == 0, kc == 7,
                            (hk, "wab"), (pak,))
            self.cp("act", ab[:, b0:b0 + nb, :], pa[:, 0:nb * 32].rearrange("p (t c) -> p t c", c=32), (pak,), ("ab",))
        self.act(rneg[:], rneg[:], AF.Exp, ("rneg",), ("rneg",))
        self.ts("dve", rneg[:], rneg[:], -1.0, None, ALU.mult, None, ("rneg",), ("rneg",))
        g = self.gall
        self.tt("dve", g[:], ab[:, :, 0:16], bcast_mid(rdtb[:], 34), ALU.add, ("ab", "rdtb"), ("gall",))
        self.act(g[:], g[:], AF.Exp, ("gall",), ("gall",))
        self.act(g[:], g[:], AF.Ln, ("gall",), ("gall",), bias=1.0)
        self.tt("dve", g[:], g[:], bcast_mid(rneg[:], 34), ALU.mult, ("gall", "rneg"), ("gall",))
        self.act(self.beta[:], ab[:, :, 16:32], AF.Sigmoid, ("ab",), ("beta",))
        cf = self.cf
        for d in range(2):
            rhs = g[:, :, d * 8:(d + 1) * 8]
            incl = cf[:, C_LE:C_LE + 128] if d == 0 else cf[:, C_GE:C_GE + 128]
            strict = cf[:, C_GT:C_GT + 128] if d == 0 else cf[:, C_LT:C_LT + 128]
            p1, k1 = self.ps()
            self.mm(p1[:, 0:272], incl, rhs, True, True, ("cf", "gall"), (k1,))
            self.act(self.gam[:, :, d * 8:(d + 1) * 8], p1[:, 0:272].rearrange("p (t c) -> p t c", c=8), AF.Exp,
                     (k1,), ("gam",))
            p2, k2 = self.ps()
            self.mm(p2[:, 0:272], strict, rhs, True, True, ("cf", "gall"), (k2,))
            self.act(self.eta[:, :, d * 8:(d + 1) * 8], p2[:, 0:272].rearrange("p (t c) -> p t c", c=8), AF.Exp,
                     (k2,), ("eta",))
            p3, k3 = self.ps()
            self.mm(p3[:, 0:272], incl, rhs, True, False, ("cf", "gall"), (k3,))
            self.mm(p3[:, 0:272], strict, rhs, False, True, ("cf", "gall"), (k3,))
            self.act(self.egl[:, :, d * 8:(d + 1) * 8], p3[:, 0:272].rearrange("p (t c) -> p t c", c=8), AF.Exp,
                     (k3,), ("egl",))
        self.tt("dve", self.bgam[:], self.beta[:], self.gam[:], ALU.mult, ("beta", "gam"), ("bgam",))

    def alloc_head_bufs(self, st, full):
        sb = lambda name, shape, dt=F32: self.sb(st, name, shape, dt)
        self.whs = [sb(f"wh{i}", [128, 8, 384], BF16) for i in range(2)]
        self.praw = sb("praw", [128, NOWN + 2])
        self.sqb = sb("sqb", [128, 1024], BF16)
        ntk = NOWN if full else NOWN + NCTX
        self.kTs = [sb(f"kT{i}", [128, ntk], BF16) for i in range(2)]
        if full:
            self.qTs = [sb(f"qT{i}", [128, NOWN], BF16) for i in range(2)]
        self.ktoks = [sb(f"ktok{i}", [128, ntk // 128, 128], BF16) for i in range(2)]
        self.vtoks = [sb(f"vtok{i}", [128, ntk // 128, 128], BF16) for i in range(2)]
        NS = 2
        self.gm = [sb(f"gm{t}", [128, 4, 128]) for t in range(NS)]
        self.Eb = self.gm
        self.Am = [sb(f"Am{t}", [128, 4, 128], BF16) for t in range(NS)]
        self.Vm = [sb(f"Vm{t}", [128, 4, 128], BF16) for t in range(NS)]
        self.Rm = [[sb(f"Rm{t}{i}", [128, 4, 128], BF16) for i in range(2)] for t in range(NS)]
        self.Rt = [[sb(f"Rt{t}{i}", [128, 4, 128], BF16) for i in range(2)] for t in range(NS)]
        self.Xb = [sb(f"Xb{t}", [128, 4, 256], BF16) for t in range(NS)]
        self.U = [[sb(f"U{d}{p}", [128, 4, 128]) for p in range(2)] for d in range(2)]
        self.WT = [[sb(f"WT{d}{p}", [128, 4, 128], BF16) for p in range(2)] for d in range(2)]
        self.KT = [[sb(f"KT{d}{p}", [128, 4, 128], BF16) for p in range(2)] for d in range(2)]
        if full:
            self.AT = [[sb(f"AT{d}{p}", [128, 4, 128], BF16) for p in range(2)] for d in range(2)]
        self.vnew = [sb(f"vnew{d}", [128, 128], BF16) for d in range(2)]
        self.Sb = [sb(f"Sb{d}", [128, 128], BF16) for d in range(2)]
        if full:
            self.oacc = sb("oacc", [128, NTO, 128])
            self.nsq = self.gm[0]

    def proj_seg(self, h, seg, io):
        P = self.P
        ntok = seg["ntok"]
        hT = seg["hT"]
        loff = seg["loff"]
        hp = h % 2
        wh = self.whs[hp]
        for kind in seg["kinds"]:
            ki = "qkv".index(kind)
            cb = ki * 8 + h
            wcol = lambda tap: self.fpar[:, 32 + tap * 24 + cb:32 + tap * 24 + cb + 1]
            wkeys = self.wk(f"wh{hp}", ki * 128, ki * 128 + 128)
            nblk = (ntok + 511) // 512
            for b in range(nblk):
                n = min(512, ntok - b * 512)
                bk = yield from self.acq(1)
                pp, pk = bk[0]
                for kc in range(8):
                    self.mm(pp[:, 0:n], wh[:, kc, ki * 128:(ki + 1) * 128], hT[:, kc, b * 512:b * 512 + n],
                            kc == 0, kc == 7, wkeys + (seg["hkey"](b),), (pk,))
                yield
                self.cp("act", self.praw[:, 1 + b * 512:1 + b * 512 + n], pp[:, 0:n], (pk,), ("praw",))
                self.rel(bk)
            for side, col in (("left", 0), ("right", ntok + 1)):
                src = seg[side]
                if src is None:
                    self.memset("pool", self.praw[:, col:col + 1], 0.0, ("praw",))
                else:
                    ht, hk, c = src
                    bk = yield from self.acq(1)
                    pp, pk = bk[0]
                    for kc in range(8):
                        self.mm(pp[:, 0:1], wh[:, kc, ki * 128:(ki + 1) * 128], ht[:, kc, c:c + 1],
                                kc == 0, kc == 7, wkeys + (hk,), (pk,))
                    yield
                    self.cp("act", self.praw[:, col:col + 1], pp[:, 0:1], (pk,), ("praw",))
                    self.rel(bk)
            for o in range(0, ntok, 1024):
                n = min(1024, ntok - o)
                acc = self.tmpA
                self.act(acc[:, 0:n], self.praw[:, 1 + o:1 + o + n], AF.Identity, ("praw", "fpar"), ("tmpA",), scale=wcol(1))
                yield
                self.stt(acc[:, 0:n], self.praw[:, o:o + n], wcol(0), acc[:, 0:n], ALU.mult, ALU.add,
                         ("praw", "fpar", "tmpA"), ("tmpA",))
                yield
                self.stt(acc[:, 0:n], self.praw[:, 2 + o:2 + o + n], wcol(2), acc[:, 0:n], ALU.mult, ALU.add,
                         ("praw", "fpar", "tmpA"), ("tmpA",))
                yield
                if kind == "v":
                    self.act(self.sqb[:, 0:n], acc[:, 0:n], AF.Silu, ("tmpA",), ("sqb",))
                    yield
                    src_bf = self.sqb
                    soff = 0
                else:
                    self.act(acc[:, 0:n], acc[:, 0:n], AF.Silu, ("tmpA",), ("tmpA",))
                    yield
                    self.tt("pool", self.sqb[:, 0:n], acc[:, 0:n], acc[:, 0:n], ALU.mult, ("tmpA",), ("sqb",))
                    yield
                    dstT = self.kTs[hp] if kind == "k" else self.qTs[hp]
                    dkey = ("kT", hp) if kind == "k" else ("qT", hp)
                    for c0 in range(0, n, 512):
                        m = min(512, n - c0)
                        bk = yield from self.acq(1)
                        pp, pk = bk[0]
                        self.mm(pp[:, 0:m], self.ones_bf[:], self.sqb[:, c0:c0 + m], True, True, ("ones", "sqb"), (pk,))
                        yield
                        rin = self.tmpB
                        self.act(rin[:, 0:m], pp[:, 0:m], AF.Ln, (pk, "cst"), ("tmpB",), bias=self.cst[:, 0:1])
                        self.rel(bk)
                        if kind == "q":
                            self.act(rin[:, 0:m], rin[:, 0:m], AF.Exp, ("tmpB", "cst"), ("tmpB",), scale=-0.5,
                                     bias=self.cst[:, 1:2])
                        else:
                            self.act(rin[:, 0:m], rin[:, 0:m], AF.Exp, ("tmpB",), ("tmpB",), scale=-0.5)
                        yield
                        self.tt("dve", dstT[:, loff + o + c0:loff + o + c0 + m], acc[:, c0:c0 + m], rin[:, 0:m], ALU.mult,
                                ("tmpA", "tmpB"), (dkey,))
                        yield
                    src_bf = dstT
                    soff = loff + o
                if kind in ("k", "v"):
                    dtok = self.ktoks[hp] if kind == "k" else self.vtoks[hp]
                    dk2 = ("ktok", hp) if kind == "k" else ("vtok", hp)
                    skey = "sqb" if kind == "v" else ("kT", hp)
                    nt = n // 128
                    lt0 = (loff + o) // 128
                    for t in range(nt):
                        self.tr(self.pt[:, t, :], src_bf[:, soff + t * 128:soff + (t + 1) * 128], (skey,), ("pt",))
                    yield
                    self.cp("act", dtok[:, lt0:lt0 + nt, :], self.pt[:, 0:nt, :], ("pt",), (dk2,))
                    yield

    def load_head_w(self, h, io, kinds):
        for kind in kinds:
            ki = "qkv".index(kind)
            self.load_w(self.whs[h % 2], f"wh{h % 2}", io["w_in"], ki * 1024 + h * 128, 128, ki * 128)

    def acq(self, n):
        while True:
            free = []
            for j in range(len(self.psb)):
                i = (self.psi + j) % len(self.psb)
                if i not in self.ps_live:
                    free.append(i)
            if len(free) >= n:
                take = free[:n]
                self.psi = take[-1] + 1
                for i in take:
                    self.ps_live.add(i)
                return [(self.psb[i], ("ps", i)) for i in take]
            yield

    def rel(self, banks):
        for (_, k) in banks:
            self.ps_live.discard(k[1])

    def gdn_pre(self, h, d, T0, nb, l0, full, par, ts):
        cf = self.cf
        col = d * 8 + h
        hp = h % 2
        G = self.gsfx
        N = nb * 128
        if d == 0:
            lh_nat, rm_nat, neg_nat = C_LE, C_GT, 0
            lh_tr, rm_tr, neg_tr = C_GT, C_LE, 1
        else:
            lh_nat, rm_nat, neg_nat = C_GE, C_LT, 1
            lh_tr, rm_tr, neg_tr = C_LT, C_GE, 0
        g_bc = bcast_last(self.gall[:, T0:T0 + nb, col], 128)
        beta_bc = bcast_last(self.beta[:, T0:T0 + nb, col], 128)
        bgam_bc = bcast_last(self.bgam[:, T0:T0 + nb, col], 128)
        eta_bc = bcast_last(self.eta[:, T0:T0 + nb, col], 128)
        kTi = lambda i: self.kTs[hp][:, (l0 + i) * 128:(l0 + i + 1) * 128]
        qTi = lambda i: self.qTs[hp][:, (l0 + i) * 128:(l0 + i + 1) * 128]
        idf = cf[:, C_ID:C_ID + 128]
        idb = self.id_bf
        gm, Eb_, Am, Vm, Xb = self.gm[ts], self.Eb[ts], self.Am[ts], self.Vm[ts], self.Xb[ts]
        Rm, Rt = self.Rm[ts], self.Rt[ts]
        kgm, kEb, kAm, kVm, kXb = ("gm", ts), ("gm", ts), ("Am", ts), ("Vm", ts), ("Xb", ts)
        v3 = lambda p: p[:, 0:N].rearrange("p (a b) -> p a b", b=128)
        self.tt("pool", gm[:, 0:nb, :], bcast_mid(cf[:, rm_nat:rm_nat + 128], nb), g_bc, ALU.mult, ("cf", "gall" + G), (kgm,))
        self.tt("pool", Xb[:, 0:nb, 0:128], self.vtoks[hp][:, l0:l0 + nb, :], beta_bc, ALU.mult, (("vtok", hp), "beta" + G), (kXb,))
        self.tt("pool", Xb[:, 0:nb, 128:256], self.ktoks[hp][:, l0:l0 + nb, :], bgam_bc, ALU.mult, (("ktok", hp), "bgam" + G), (kXb,))
        bk = yield from self.acq(2)
        pD, kD = bk[0]
        pG, kG = bk[1]
        self.mm(pD[:, 0:N], cf[:, lh_nat:lh_nat + 128], gm[:, 0:nb, :].rearrange("p a b -> p (a b)"), True, False,
                ("cf", kgm), (kD,))
        self.mm(pD[:, 0:N], idb[:], self.negb[:, neg_nat, 0:N], False, False, ("idbf", "negb"), (kD,))
        self.mm(pD[:, 0:N], idb[:], bcast_mid(self.negd[:], nb), False, True, ("idbf", "negd"), (kD,))
        for i in range(nb):
            self.mm(pG[:, i * 128:(i + 1) * 128], kTi(i), kTi(i), True, True, (("kT", hp),), (kG,))
        yield
        Eb = Eb_[:, 0:nb, :]
        self.act(Eb, v3(pD), AF.Exp, (kD,), (kEb,))
        yield
        self.tt("pool", Eb, Eb, beta_bc, ALU.mult, (kEb, "beta" + G), (kEb,))
        yield
        self.tt("dve", Am[:, 0:nb, :], v3(pG), Eb, ALU.mult, (kG, kEb), (kAm,))
        self.rel(bk)
        if full:
            self.spawned.append(self.gdn_at(h, d, T0, nb, l0, par, ts))
        yield
        for k in range(7):
            cur = (k + 1) % 2
            nxt = k % 2
            if k == 0:
                Rc = lambda i: idb[:]
                Rtc = lambda i: idb[:]
                rk, rtk = "idbf", "idbf"
            else:
                Rc = (lambda c: lambda i: Rm[c][:, i, :])(cur)
                Rtc = (lambda c: lambda i: Rt[c][:, i, :])(cur)
                rk, rtk = ("Rm", ts, cur), ("Rt", ts, cur)
                if k == 1:
                    Rc = lambda i: Vm[:, i, :]
                    rk = kVm
            bk = yield from self.acq(1)
            pZ, kZ = bk[0]
            for i in range(nb):
                self.mm(pZ[:, i * 128:(i + 1) * 128], Am[:, i, :], Rc(i), i == 0, False, (kAm, rk), (kZ,), skip=True)
            self.mm(pZ[:, 0:N], idb[:], bcast_mid(idb[:], nb), False, True, ("idbf",), (kZ,), skip=True)
            yield
            self.tt("dve", Vm[:, 0:nb, :], v3(pZ), bcast_mid(self.nm[:, d * 7 + k, :], nb), ALU.mult, (kZ, "nm"), (kVm,))
            self.rel(bk)
            bk = yield from self.acq(2 if k < 6 else 1)
            pR, kR = bk[0]
            if k > 0:
                for i in range(nb):
                    self.mm(pR[:, i * 128:(i + 1) * 128], Rtc(i), Vm[:, i, :], True, True, (rtk, kVm), (kR,))
            if k < 6:
                pR2, kR2 = bk[1]
                for i in range(nb):
                    self.mm(pR2[:, i * 128:(i + 1) * 128], Vm[:, i, :], Rtc(i), True, True, (kVm, rtk), (kR2,))
            yield
            if k > 0:
                self.cp("act", Rm[nxt][:, 0:nb, :], v3(pR), (kR,), (("Rm", ts, nxt),))
            if k < 6:
                self.cp("dve" if k % 2 == 0 else "act", Rt[nxt][:, 0:nb, :], v3(pR2), (kR2,), (("Rt", ts, nxt),))
            self.rel(bk)
            if k == 3:
                yield "MID"
                self.tt("pool", self.KT[d][par][:, 0:nb, :], self.ktoks[hp][:, l0:l0 + nb, :], eta_bc, ALU.mult,
                        (("ktok", hp), "eta" + G), (("KT", d, par),))
        Rf = Rm[0]
        rfk = ("Rm", ts, 0)
        bk = yield from self.acq(2)
        pU, kU = bk[0]
        for i in range(nb):
            self.mm(pU[:, i * 128:(i + 1) * 128], Rf[:, i, :], Xb[:, i, 0:128], True, True, (rfk, kXb), (kU,))
        pW, kW = bk[1]
        for i in range(nb):
            self.mm(pW[:, i * 128:(i + 1) * 128], Xb[:, i, 128:256], Rf[:, i, :], True, True, (rfk, kXb), (kW,))
        yield
        self.cp("act", self.U[d][par][:, 0:nb, :], v3(pU), (kU,), (("U", d, par),))
        self.cp("dve", self.WT[d][par][:, 0:nb, :], v3(pW), (kW,), (("WT", d, par),))
        self.rel(bk)
        yield

    def gdn_at(self, h, d, T0, nb, l0, par, ts):
        cf = self.cf
        col = d * 8 + h
        hp = h % 2
        G = self.gsfx
        N = nb * 128
        if d == 0:
            lh_tr, rm_tr, neg_tr = C_GT, C_LE, 1
        else:
            lh_tr, rm_tr, neg_tr = C_LT, C_GE, 0
        g_bc = bcast_last(self.gall[:, T0:T0 + nb, col], 128)
        idb = self.id_bf
        gm = self.gm[ts]
        kgm = ("gm", ts)
        Eb = gm[:, 0:nb, :]
        v3 = lambda p: p[:, 0:N].rearrange("p (a b) -> p a b", b=128)
        self.tt("pool", gm[:, 0:nb, :], bcast_mid(cf[:, rm_tr:rm_tr + 128], nb), g_bc, ALU.mult, ("cf", "gall" + G), (kgm,))
        bk = yield from self.acq(2)
        pD2, kD2 = bk[0]
        pQ, kQ = bk[1]
        self.mm(pD2[:, 0:N], cf[:, lh_tr:lh_tr + 128], gm[:, 0:nb, :].rearrange("p a b -> p (a b)"), True, False,
                ("cf", kgm), (kD2,))
        self.mm(pD2[:, 0:N], idb[:], self.negb[:, neg_tr, 0:N], False, True, ("idbf", "negb"), (kD2,))
        for i in range(nb):
            self.mm(pQ[:, i * 128:(i + 1) * 128], self.kTs[hp][:, (l0 + i) * 128:(l0 + i + 1) * 128],
                    self.qTs[hp][:, (l0 + i) * 128:(l0 + i + 1) * 128], True, True, (("kT", hp), ("qT", hp)), (kQ,))
        yield
        self.act(Eb, v3(pD2), AF.Exp, (kD2,), (kgm,))
        yield
        self.tt("dve", self.AT[d][par][:, 0:nb, :], v3(pQ), Eb, ALU.mult, (kQ, kgm), (("AT", d, par),))
        self.rel(bk)
        yield

    def gdn_step(self, h, d, T, i, par, l, full):
        col = d * 8 + h
        hp = h % 2
        G = self.gsfx
        S = self.Sst[:, col, :]
        sk = ("S", col)
        Sb = self.Sb[d]
        sbk = ("Sb", d)
        vn = self.vnew[d]
        vk = ("vnew", d)
        bk = yield from self.acq(1)
        pw, kw = bk[0]
        self.mm(pw[:, 0:128], self.WT[d][par][:, i, :], Sb[:], True, True, (("WT", d, par), sbk), (kw,))
        yield
        self.tt("dve", vn[:], self.U[d][par][:, i, :], pw[:, 0:128], ALU.subtract, (("U", d, par), kw), (vk,))
        self.rel(bk)
        bk = yield from self.acq(2 if full else 1)
        pS, kS = bk[0]
        self.mm(pS[:, 0:128], self.KT[d][par][:, i, :], vn[:], True, True, (("KT", d, par), vk), (kS,))
        if full:
            po, ko = bk[1]
            self.mm(po[:, 0:128], self.qTs[hp][:, l * 128:(l + 1) * 128], Sb[:], True, True, (("qT", hp), sbk), (ko,))
            self.mm(po[:, 128:256], self.AT[d][par][:, i, :], vn[:], True, True, (("AT", d, par), vk), (ko,))
        yield
        self.stt(S, S, self.egl[:, T, col:col + 1], pS[:, 0:128], ALU.mult, ALU.add, (sk, "egl" + G, kS), (sk,))
        yield
        self.cp("pool", Sb[:], S, (sk,), (sbk,))
        if full:
            ok = ("oacc", l)
            self.stt(self.oacc[:, l, :], po[:, 0:128], self.gam[:, T, col:col + 1], self.oacc[:, l, :], ALU.mult, ALU.add,
                     (ko, "gam" + G, ok), (ok,))
            self.tt("dve", self.oacc[:, l, :], po[:, 128:256], self.oacc[:, l, :], ALU.add, (ko, ok), (ok,))
        self.rel(bk)
        yield

    def chain_batches(self, d, T_lo, n):
        out = []
        if d == 0:
            t = T_lo
            while t < T_lo + n:
                nb = min(4, T_lo + n - t)
                out.append((t, nb, list(range(nb))))
                t += nb
        else:
            t = T_lo + n
            while t > T_lo:
                nb = min(4, t - T_lo)
                out.append((t - nb, nb, list(range(nb - 1, -1, -1))))
                t -= nb
        return out

    def bg_step(self):
        if self.bg is not None:
            try:
                next(self.bg)
            except StopIteration:
                self.bg = None

    def bg_drain(self):
        while self.bg is not None:
            self.bg_step()

    def run_round(self, gens):
        gens = list(gens)
        while gens:
            alive = []
            for g_ in gens:
                try:
                    next(g_)
                    alive.append(g_)
                except StopIteration:
                    pass
            gens = alive + self.spawned
            self.spawned = []
            self.bg_step()

    def run_staggered(self, h, chain, full, extra=None):
        d, T_lo, n, lbase = chain
        bat = [(T0, nb, order, T0 - lbase) for (T0, nb, order) in self.chain_batches(d, T_lo, n)]
        nbt = len(bat)
        pre = {}

        def start(bi):
            T0, nb, order, l0 = bat[bi]
            pre[bi] = self.gdn_pre(h, d, T0, nb, l0, full, bi % 2, bi % 2)

        for r in range(-1, nbt):
            entries = []
            if r + 1 < nbt:
                if r + 1 not in pre:
                    start(r + 1)
                entries.append([pre[r + 1], False])
            if r + 2 < nbt:
                start(r + 2)
                entries.append([pre[r + 2], True])
            if r >= 0:
                T0, nb, order, l0 = bat[r]
                entries.append([self.scan_batch(h, d, T0, nb, order, l0, r % 2, full), False])
            if extra is not None:
                for mk in extra(r, nbt):
                    entries.append([mk, False])
            while entries:
                alive = []
                for e in entries:
                    try:
                        v = next(e[0])
                        if v == "MID" and e[1]:
                            continue
                        alive.append(e)
                    except StopIteration:
                        pass
                entries = alive
                self.bg_step()

    def scan_batch(self, h, d, T0, nb, order, l0, par, full):
        for i in order:
            yield from self.gdn_step(h, d, T0 + i, i, par, l0 + i, full)

    def run_chains(self, h, chains, full):
        sched = []
        for ci, (d, T_lo, n, lbase) in enumerate(chains):
            sched.append([(d, T0, nb, order, T0 - lbase, ci) for (T0, nb, order) in self.chain_batches(d, T_lo, n)])
        nround = max(len(s) for s in sched)
        for r in range(-1, nround):
            gens = []
            for s in sched:
                if r + 1 < len(s):
                    d, T0, nb, order, l0, ci = s[r + 1]
                    gens.append(self.gdn_pre(h, d, T0, nb, l0, full, (r + 1) % 2, ci))
            for s in sched:
                if 0 <= r < len(s):
                    d, T0, nb, order, l0, ci = s[r]
                    gens.append(self.scan_batch(h, d, T0, nb, order, l0, r % 2, full))
            self.run_round(gens)

    def set_state(self, h, d):
        col = d * 8 + h
        self.cp("act", self.Sb[d][:], self.Sst[:, col, :], (("S", col), "Sst"), (("Sb", d),))

    def proj_pass1(self, h, io):
        seg = dict(ntok=NCTX, hT=self.hTc, hkey=lambda b: ("hTc", 0), left=None, right=None, loff=NOWN, kinds="kv")
        yield from self.proj_seg(h, seg, io)
        seg = dict(ntok=NOWN, hT=self.hTx, hkey=lambda b: ("hTx", b), left=(self.hTo, ("hTo", 3), NOWN - 1), right=None,
                   loff=0, kinds="kv")
        yield from self.proj_seg(h, seg, io)
        if h + 1 < 8:
            self.load_head_w(h + 1, io, "kv")

    def head_pass1(self, h, io):
        if h == 0:
            self.load_head_w(0, io, "kv")
            self.bg = self.proj_pass1(0, io)
        self.bg_drain()
        self.bg = self.proj_pass1(h + 1, io) if h + 1 < 8 else None
        hp = h % 2
        if h == 0:
            self.dump("p1_kT", self.kTs[hp][:], (("kT", hp),))
            self.dump("p1_ktok", self.ktoks[hp][:], (("ktok", hp),))
            self.dump("p1_vtok", self.vtoks[hp][:], (("vtok", hp),))
        self.set_state(h, 0)
        self.set_state(h, 1)
        def extra(r, nbt):
            if r == nbt - 2:
                return [self.gdn_pre(h, 0, 32, 2, 16, False, 0, nbt % 2)]
            if r == nbt - 1:
                return [self.scan_batch(h, 0, 32, 2, [0, 1], 16, 0, False)]
            return []
        self.run_staggered(h, (1, 16, 18, 16), False, extra)

    def proj_pass2(self, h, io):
        seg = dict(ntok=NOWN, hT=self.hTo, hkey=lambda b: ("hTo", b), left=None, right=(self.hTh, "hTh", 0),
                   loff=0, kinds="qkv")
        yield from self.proj_seg(h, seg, io)
        if h + 1 < 8:
            self.load_head_w(h + 1, io, "qkv")

    def pass2_all(self, io):
        chains = [(0, 0, 16, 0), (1, 0, 16, 0)]
        sched = []
        for ci, (d, T_lo, n, lbase) in enumerate(chains):
            sched.append([(d, T0, nb, order, T0 - lbase, ci) for (T0, nb, order) in self.chain_batches(d, T_lo, n)])
        nround = len(sched[0])

        def pre_gens(h, bi):
            return [self.gdn_pre(h, d, T0, nb, l0, True, bi % 2, ci) for (d, T0, nb, order, l0, ci) in
                    (s_[bi] for s_ in sched)]

        def scan_gens(h, bi):
            return [self.scan_batch(h, d, T0, nb, order, l0, bi % 2, True) for (d, T0, nb, order, l0, ci) in
                    (s_[bi] for s_ in sched)]

        self.load_head_w(0, io, "qkv")
        self.bg = self.proj_pass2(0, io)
        self.bg_drain()
        self.bg = self.proj_pass2(1, io)
        self.run_round(pre_gens(0, 0))
        for h in range(8):
            hp = h % 2
            if h == 0:
                self.dump("p2_qT", self.qTs[hp][:], (("qT", hp),))
                self.dump("p2_kT", self.kTs[hp][:], (("kT", hp),))
                self.dump("p2_vtok", self.vtoks[hp][:], (("vtok", hp),))
            self.memset("pool", self.oacc[:], 0.0, tuple(("oacc", l) for l in range(NTO)))
            self.set_state(h, 0)
            self.set_state(h, 1)
            for r in range(nround):
                gens = []
                if r + 1 < nround:
                    gens += pre_gens(h, r + 1)
                elif h + 1 < 8:
                    self.bg_drain()
                    gens += pre_gens(h + 1, 0)
                gens += scan_gens(h, r)
                self.run_round(gens)
            self.bg = self.proj_pass2(h + 2, io) if h + 2 < 8 else None
            if h == 0:
                self.dump("p2_oacc", self.oacc[:], tuple(("oacc", l) for l in range(NTO)))
            self.head_norm(h)

    def head_norm(self, h):
        okeys = tuple(("oacc", l) for l in range(NTO))
        ssq = self.sml[:, 8:24]
        sq = self.nsq[:]
        for qt in range(4):
            o2 = self.oacc[:, qt * 4:(qt + 1) * 4, :]
            self.tt("dve", sq, o2, o2, ALU.mult, okeys, (("gm", 0),))
            self.P.op("dve", (lambda sq, qt: lambda e: e.tensor_reduce(
                out=self.sml[:, 8 + qt * 4:12 + qt * 4], in_=sq, axis=AX.X, op=ALU.add))(sq, qt),
                (("gm", 0),), (("sml", "ssq"),))
        self.rsqrt_small(ssq, ssq, 1.0 / 128, (("sml", "ssq"),), (("sml", "ssq"),))
        self.tt("dve", self.oacc[:], self.oacc[:], bcast_last(ssq, 128), ALU.mult, okeys + (("sml", "ssq"),), okeys)
        self.tt("pool", self.on_store[:, :, h * 128:(h + 1) * 128], self.oacc[:], bcast_mid(self.gon[:], NTO), ALU.mult,
                okeys + ("gon",), ("on_store",))

    def phase3a(self, io):
        nc, P = self.nc, self.P
        Bg = self.Bg
        hTo = self.hTo
        with ExitStack() as s:
            sb = lambda name, shape, dt=F32: self.sb(s, name, shape, dt)
            w_zb = sb("w_zb", [128, 8, 1024], BF16)
            w_gb = sb("w_gb", [128, 8, 1024], BF16)
            w_pb = sb("w_pb", [128, 8, 1024], BF16)
            ybb = sb("ybb", [128, 1024], BF16)
            ybT = sb("ybT", [128, 8, 512], BF16)
            self.load_w(w_zb, "w_zb", io["w_in"], OFF_ZB, 1024)
            self.load_w(w_gb, "w_gb", io["w_in"], OFF_G + 1024, 1024)
            self.load_w(w_pb, "w_pb", io["w_pb"], 0, 1024)
            ybb2 = sb("ybb2", [128, 1024], BF16)
            gbuf = sb("gbuf", [128, 8, 512])
            ybbs = [ybb, ybb2]
            souts = [self.tmpA, self.tmpB]
            def ZB(t):
                blk, tt_ = divmod(t, 4)
                so, yb = souts[t % 2], ybbs[t % 2]
                sok, ybk = ("so", t % 2), ("ybb", t % 2)
                for half in range(2):
                    pz, kz = self.ps()
                    for kc in range(8):
                        self.mm(pz[:], hTo[:, kc, t * 128:(t + 1) * 128], w_zb[:, kc, half * 512:(half + 1) * 512],
                                kc == 0, kc == 7, (("hTo", blk),) + self.wk("w_zb", half * 512, half * 512 + 512), (kz,))
                    self.act(so[:, half * 512:(half + 1) * 512], pz[:], AF.Silu, (kz,), (sok,))
                self.tt("pool", yb[:], self.on_store[:, t, :], so[:], ALU.mult, ("on_store", sok), (ybk,))

            def MIDP(tp):
                bp, tq = divmod(tp, 4)
                yb, ybk = ybbs[tp % 2], ("ybb", tp % 2)
                for kc in range(8):
                    self.tr(self.pt[:, kc, :], yb[:, kc * 128:(kc + 1) * 128], (ybk,), ("pt",))
                self.cp("dve", ybT[:, :, tq * 128:(tq + 1) * 128], self.pt[:], ("pt",), ("ybT",))
                if tq == 3:
                    for mb in range(8):
                        pB, kB = self.ps()
                        for kc in range(8):
                            self.mm(pB[:], w_pb[:, kc, mb * 128:(mb + 1) * 128], ybT[:, kc, :],
                                    kc == 0, kc == 7, ("ybT",) + self.wk("w_pb", mb * 128, mb * 128 + 128), (kB,))
                        self.tt("dve", Bg[:, mb, bp * 512:(bp + 1) * 512], pB[:], gbuf[:, mb, :], ALU.mult,
                                (kB, ("gbuf", mb)), (("Bg", bp),))

            def GATES(t):
                blk, tt_ = divmod(t, 4)
                for mb in (2 * tt_, 2 * tt_ + 1):
                    pg, kg = self.ps()
                    for kc in range(8):
                        self.mm(pg[:], w_gb[:, kc, mb * 128:(mb + 1) * 128], hTo[:, kc, blk * 512:(blk + 1) * 512],
                                kc == 0, kc == 7, (("hTo", blk),) + self.wk("w_gb", mb * 128, mb * 128 + 128), (kg,))
                    self.act(gbuf[:, mb, :], pg[:], AF.Sigmoid, (kg,), (("gbuf", mb),))

            for t in range(17):
                odd = (t % 4) % 2 == 1
                if t < 16 and odd:
                    GATES(t)
                if t < 16:
                    ZB(t)
                if t >= 1:
                    MIDP(t - 1)
                if t < 16 and not odd:
                    GATES(t)

    def phase3b(self, st, io):
        nc, P = self.nc, self.P
        Bg = self.Bg
        hTo = self.hTo
        yaT = self.sb(st, "yaT", [128, 8, NOWN], BF16)
        with ExitStack() as s:
            sb = lambda name, shape, dt=F32: self.sb(s, name, shape, dt)
            w_ua = sb("w_ua", [128, 8, 1024], BF16)
            w_va = sb("w_va", [128, 8, 1024], BF16)
            w_za = sb("w_za", [128, 8, 1024], BF16)
            wsp = sb("wsp", [128, 8, 128], BF16)
            rlg = sb("rlg", [128, 1024])
            rlb = sb("rlb", [128, 1024])
            rbs = sb("rbs", [128, 1024])
            uz = sb("uz", [128, 8, 512], BF16)
            gvs = [sb(f"gv{i}", [128, 1024]) for i in range(2)]
            vlns = [sb(f"vln{i}", [128, 1024], BF16) for i in range(2)]
            self.load_w(w_va, "w_va", io["w_in"], OFF_VA, 1024)
            self.load_w(w_ua, "w_ua", io["w_in"], OFF_UA, 1024)
            self.load_w(w_za, "w_za", io["w_in"], OFF_ZA, 1024)
            P.dma(gvs[1][:], io["w_spT"], (), (("gv", 1),), grp="wspst")
            self.cp("dve", wsp[:].rearrange("p a b -> p (a b)"), gvs[1][:], (("gv", 1),), ("wsp",))
            P.dma(rlg[:], io["r_lng"], (), ("rlg",), grp="rlg")
            P.dma(rlb[:], io["r_lnb"], (), ("rlb",), grp="rlb")
            P.dma(rbs[:], io["r_bsp"], (), ("rbs",), grp="rbs")
            sm = self.sml
            tu = self.tmpA[:, 0:512]
            tz = self.tmpB[:, 0:512]
            stmp = self.tmpA[:, 512:1024]

            def U(blk):
                hk = ("hTo", blk)
                for cbk in range(8):
                    pu, ku = self.ps()
                    for kc in range(8):
                        self.mm(pu[:], w_ua[:, kc, cbk * 128:(cbk + 1) * 128], hTo[:, kc, blk * 512:(blk + 1) * 512],
                                kc == 0, kc == 7, (hk,) + self.wk("w_ua", cbk * 128, cbk * 128 + 128), (ku,))
                    pz, kz = self.ps()
                    for kc in range(8):
                        self.mm(pz[:], w_za[:, kc, cbk * 128:(cbk + 1) * 128], hTo[:, kc, blk * 512:(blk + 1) * 512],
                                kc == 0, kc == 7, (hk,) + self.wk("w_za", cbk * 128, cbk * 128 + 128), (kz,))
                    if cbk % 2 == 0:
                        self.act(tu, pu[:], AF.Gelu_apprx_tanh, (ku,), ("tu",))
                        self.act(tz, pz[:], AF.Silu, (kz,), ("tz",))
                    else:
                        self.act(tz, pz[:], AF.Silu, (kz,), ("tz",))
                        self.act(tu, pu[:], AF.Gelu_apprx_tanh, (ku,), ("tu",))
                    self.tt("pool", uz[:, cbk, :], tu, tz, ALU.mult, ("tu", "tz"), ("uz",))

            def V(t):
                blk = t // 4
                hk = ("hTo", blk)
                q = t % 2
                gv, vln = gvs[q], vlns[q]
                gk, vk = ("gv", q), ("vln", q)
                c0 = 24 + 8 * q
                K = lambda j: ("sml", c0 + j)
                C = lambda j: sm[:, c0 + j:c0 + j + 1]
                for half in range(2):
                    pv, kv = self.ps()
                    for kc in range(8):
                        self.mm(pv[:], hTo[:, kc, t * 128:(t + 1) * 128], w_va[:, kc, half * 512:(half + 1) * 512],
                                kc == 0, kc == 7, (hk,) + self.wk("w_va", half * 512, half * 512 + 512), (kv,))
                    self.act(gv[:, half * 512:(half + 1) * 512], pv[:], AF.Gelu_apprx_tanh, (kv,), (gk, K(half)), accum=C(half))
                self.act(yaT[:, :, t * 128:(t + 1) * 128], gv[:].rearrange("p (a b) -> p a b", b=128), AF.Square,
                         (gk,), (("yaT", blk), K(2)), accum=C(2))
                self.tt("dve", C(3), C(0), C(1), ALU.add, (K(0), K(1)), (K(3),))
                self.ts("dve", C(3), C(3), 1.0 / 1024, None, ALU.mult, None, (K(3),), (K(3),))
                self.tt("dve", C(5), C(3), C(3), ALU.mult, (K(3),), (K(5),))
                self.ts("dve", C(4), C(2), 1.0 / 1024, None, ALU.mult, None, (K(2),), (K(4),))
                self.tt("dve", C(4), C(4), C(5), ALU.subtract, (K(4), K(5)), (K(4),))
                self.rsqrt_small(C(4), C(4), 1.0, (K(4),), (K(4),))
                self.ts("dve", gv[:], gv[:], C(3), C(4), ALU.subtract, ALU.mult, (gk, K(3), K(4)), (gk,))
                self.tt("pool", gv[:], gv[:], rlg[:], ALU.mult, (gk, "rlg"), (gk,))
                self.tt("pool", vln[:], gv[:], rlb[:], ALU.add, (gk, "rlb"), (vk,))

            def S(t):
                blk, tt_ = divmod(t, 4)
                q = t % 2
                vln, vk = vlns[q], ("vln", q)
                for gh in range(2):
                    pS, kS = self.ps()
                    for gi in range(4):
                        g_ = gh * 4 + gi
                        self.mm(pS[:, gi * 128:(gi + 1) * 128], vln[:, g_ * 128:(g_ + 1) * 128], wsp[:, g_, :], True, True,
                                (vk, "wsp"), (kS,))
                    self.tt("dve", stmp, pS[:], rbs[:, gh * 512:(gh + 1) * 512], ALU.add, (kS, "rbs"), ("stmp",))
                    self.tt("pool", yaT[:, gh * 4:(gh + 1) * 4, t * 128:(t + 1) * 128],
                            stmp.rearrange("p (a b) -> p a b", b=128),
                            uz[:, gh * 4:(gh + 1) * 4, tt_ * 128:(tt_ + 1) * 128], ALU.mult, ("stmp", "uz"), (("yaT", blk),))

            V(0)
            for blk in range(4):
                U(blk)
                for tt_ in range(4):
                    t = blk * 4 + tt_
                    if t + 1 < 16:
                        V(t + 1)
                    S(t)
        P.barrier()
        self.dump("yaT", yaT[:], tuple(("yaT", b) for b in range(4)))
        with ExitStack() as s:
            sb = lambda name, shape, dt=F32: self.sb(s, name, shape, dt)
            w_ga = sb("w_ga", [128, 8, 1024], BF16)
            w_pa = sb("w_pa", [128, 8, 1024], BF16)
            w_out = sb("w_out", [128, 8, 1024], BF16)
            mT = sb("mT", [128, 8, 512], BF16)
            ob = [sb(f"ob{i}", [128, 1024]) for i in range(2)]
            self.alloc_x(s)
            self.load_w(w_ga, "w_ga", io["w_in"], OFF_G, 1024)
            self.load_w(w_pa, "w_pa", io["w_pa"], 0, 1024)
            self.load_w(w_out, "w_out", io["w_out"], 0, 1024)
            sm = self.sml
            for blk in range(4):
                hk = ("hTo", blk)
                for mb in range(8):
                    pg, kg = self.ps()
                    for kc in range(8):
                        self.mm(pg[:], w_ga[:, kc, mb * 128:(mb + 1) * 128], hTo[:, kc, blk * 512:(blk + 1) * 512],
                                kc == 0, kc == 7, (hk,) + self.wk("w_ga", mb * 128, mb * 128 + 128), (kg,))
                    self.act(self.tmpB[:, 0:512], pg[:], AF.Sigmoid, (kg,), ("tmpB",))
                    pA, kA = self.ps()
                    for kc in range(8):
                        self.mm(pA[:], w_pa[:, kc, mb * 128:(mb + 1) * 128], yaT[:, kc, blk * 512:(blk + 1) * 512],
                                kc == 0, kc == 7, (("yaT", blk),) + self.wk("w_pa", mb * 128, mb * 128 + 128), (kA,))
                    self.tt("dve", self.tmpB[:, 512:1024], pA[:], self.tmpB[:, 0:512], ALU.mult, (kA, "tmpB"), ("tmpB2",))
                    self.tt("pool", mT[:, mb, :], self.tmpB[:, 512:1024], Bg[:, mb, blk * 512:(blk + 1) * 512], ALU.add,
                            ("tmpB2", ("Bg", blk)), ("mT",))
                for tt_ in range(4):
                    t = blk * 4 + tt_
                    i = self.xi % self.nx
                    self.xi += 1
                    xs = self.xst[i]
                    P.dma(xs[:], io["x_own"][t * 128:(t + 1) * 128, :], (), (("xst", i),), grp=f"xst{i}")
                    pos = []
                    for half in range(2):
                        po, ko = self.ps()
                        for kc in range(8):
                            self.mm(po[:], mT[:, kc, tt_ * 128:(tt_ + 1) * 128], w_out[:, kc, half * 512:(half + 1) * 512],
                                    kc == 0, kc == 7, ("mT",) + self.wk("w_out", half * 512, half * 512 + 512), (ko,))
                        self.act(self.tmpA[:, half * 512:(half + 1) * 512], po[:], AF.Square, (ko,),
                                 ("tmpA", ("sml", 32 + half)), accum=sm[:, 32 + half:33 + half])
                        pos.append((po, ko))
                    self.tt("dve", sm[:, 34:35], sm[:, 32:33], sm[:, 33:34], ALU.add, (("sml", 32), ("sml", 33)), (("sml", 34),))
                    self.rsqrt_small(sm[:, 34:35], sm[:, 34:35], 1.0 / 1024, (("sml", 34),), (("sml", 34),))
                    o_ = ob[t % 2]
                    okey = ("ob", t % 2)
                    for half in range(2):
                        po, ko = pos[half]
                        self.stt(o_[:, half * 512:(half + 1) * 512], po[:], sm[:, 34:35],
                                 self.gate_bc[:, half * 512:(half + 1) * 512], ALU.mult, ALU.mult,
                                 (ko, ("sml", 34), "gate_bc"), (okey,))
                    self.tt("pool", o_[:], o_[:], xs[:], ALU.add, (okey, ("xst", i)), (okey,))
                    P.dma(io["y"][t * 128:(t + 1) * 128, :], o_[:], (okey,), (), grp=f"yout{t % 2}")


IN_SPECS = [
    ("x_own", [NOWN, D]), ("x_oth", [NOWN, D]), ("ctx", [NCTX, D]), ("cvec", [128, 16]),
    ("w_mod", [D, 3 * D]), ("w_in", [D, 9248]), ("w_ab", [D, 32]),
    ("w_pa", [D, D]), ("w_pb", [D, D]), ("w_out", [D, D]), ("w_spT", [128, 1024]),
    ("fparams", [128, 104]), ("r_dtb", [128, 16]), ("r_alog", [128, 16]),
    ("r_gpost", [128, D]), ("r_bmodg", [128, D]), ("r_lng", [128, D]), ("r_lnb", [128, D]),
    ("r_bsp", [128, D]), ("r_gon", [128, 128]), ("consts", [128, NCONST]),
]


def _program(nc, plan, debug=None):
    io = {}
    for name, shape in IN_SPECS:
        io[name] = nc.dram_tensor(name, shape, F32, kind="ExternalInput").ap()
    io["y"] = nc.dram_tensor("y", [NOWN, D], F32, kind="ExternalOutput").ap()
    P = Prog(nc, plan)
    with ExitStack() as st:
        if plan is not None:
            P.setup_sems(st)
        Builder(nc, P, debug).build(io)
    return P


def build_nc(debug=None):
    nc0 = bass.Bass("TRN2", target_bir_lowering=False)
    P0 = _program(nc0, None, debug)
    plan = P0.analyze()
    nc = bass.Bass("TRN2", target_bir_lowering=False)
    P1 = _program(nc, plan, debug)
    assert P1.idx == len(plan["ops"])
    return nc, plan


def make_consts():
    c = np.zeros((128, NCONST), np.float32)
    i = np.arange(128)
    p, f = i[:, None], i[None, :]
    c[:, C_ID:C_ID + 128] = (p == f)
    c[:, C_LE:C_LE + 128] = (p <= f)
    c[:, C_GT:C_GT + 128] = (p > f)
    c[:, C_GE:C_GE + 128] = (p >= f)
    c[:, C_LT:C_LT + 128] = (p < f)
    c[:, C_NEGU:C_NEGU + 512] = np.tile(np.where(p < f, NEGV, 0.0), (1, 4))
    c[:, C_NEGL:C_NEGL + 512] = np.tile(np.where(p > f, NEGV, 0.0), (1, 4))
    for k in range(7):
        b = 1 << k
        blk = i // (2 * b)
        half = (i // b) % 2
        same = blk[:, None] == blk[None, :]
        mF = same & (half[:, None] == 0) & (half[None, :] == 1)
        mB = same & (half[:, None] == 1) & (half[None, :] == 0)
        c[:, C_NM + k * 128:C_NM + (k + 1) * 128] = -mF.astype(np.float32) + (p == f)
        c[:, C_NM + (7 + k) * 128:C_NM + (8 + k) * 128] = -mB.astype(np.float32) + (p == f)
    return c


def fm(v, n):
    return np.ascontiguousarray(np.asarray(v, np.float32).reshape(n, 128).T)


def rows(v):
    v = np.asarray(v, np.float32).reshape(1, -1)
    return np.ascontiguousarray(np.broadcast_to(v, (128, v.shape[1])))


def make_in_maps(x, c, ctx, c_ctx, w_mod, b_mod, g_pre, g_post, w_in, w_conv, a_log, dt_bias,
                 g_onorm, gm_ln_g, gm_ln_b, w_sp, b_sp, w_pa, w_pb, w_out):
    f32 = lambda a: np.ascontiguousarray(np.asarray(a, np.float32))
    w_mod0, b_mod0, w_in0 = f32(w_mod[0]), f32(b_mod[0]), f32(w_in[0])
    w_pa0, w_pb0, w_out0 = f32(w_pa[0]), f32(w_pb[0]), f32(w_out[0])
    consts = make_consts()
    maps = []
    for r in range(8):
        b, half = r // 2, r % 2
        rev = half == 1
        xs = np.asarray(x[b], np.float32)
        cx = np.asarray(ctx[b], np.float32)
        if rev:
            xs = xs[::-1]
            cx = cx[::-1]
        wc = np.asarray(w_conv[0], np.float32)
        al = np.asarray(a_log[0], np.float32)
        db = np.asarray(dt_bias[0], np.float32)
        wsp = np.asarray(w_sp[0], np.float32)
        bsp = np.asarray(b_sp[0], np.float32)
        wab = w_in0[:, OFF_A:OFF_ZB]
        if rev:
            wc = wc[::-1]
            al = al[::-1]
            db = db[::-1]
            wsp = wsp[:, ::-1, ::-1]
            bsp = bsp[:, ::-1]
            wab = wab[:, [8, 9, 10, 11, 12, 13, 14, 15, 0, 1, 2, 3, 4, 5, 6, 7,
                          24, 25, 26, 27, 28, 29, 30, 31, 16, 17, 18, 19, 20, 21, 22, 23]]
        cvec = np.zeros((128, 16), np.float32)
        cb_fm = fm(c[b], 8)
        cc_fm = fm(c_ctx, 8)
        cvec[:, 0::2] = cb_fm
        cvec[:, 1::2] = cc_fm
        fpar = np.zeros((128, 104), np.float32)
        fpar[:, 0:8] = fm(g_pre[0], 8)
        fpar[:, 8:32] = fm(b_mod0, 24)
        for tap in range(3):
            fpar[:, 32 + tap * 24:32 + (tap + 1) * 24] = fm(wc[tap], 24)
        m = {
            "x_own": f32(xs[:NOWN]), "x_oth": f32(xs[NOWN:]), "ctx": f32(cx), "cvec": cvec,
            "w_mod": w_mod0, "w_in": w_in0, "w_ab": f32(wab), "w_pa": w_pa0, "w_pb": w_pb0, "w_out": w_out0,
            "w_spT": f32(np.transpose(wsp, (2, 0, 1)).reshape(128, 1024)),
            "fparams": fpar, "r_dtb": rows(db.reshape(-1)), "r_alog": rows(al.reshape(-1)),
            "r_gpost": rows(g_post[0]), "r_bmodg": rows(b_mod0[2048:]), "r_lng": rows(gm_ln_g[0]),
            "r_lnb": rows(gm_ln_b[0]), "r_bsp": rows(bsp.reshape(-1)), "r_gon": rows(g_onorm[0]),
            "consts": consts,
        }
        maps.append(m)
    return maps


_CACHE = {}


def kernel(**inputs):
    if "nc" not in _CACHE:
        _CACHE["nc"] = build_nc()[0]
    nc = _CACHE["nc"]
    maps = make_in_maps(**inputs)
    res = run_bass_kernel_spmd(nc, maps, core_ids=list(range(8)))
    out = np.empty((4, 4096, D), np.float32)
    for r in range(8):
        b, half = r // 2, r % 2
        y = np.asarray(res.results[r]["y"], np.float32)
        if half == 0:
            out[b, :NOWN] = y
        else:
            out[b, ::-1][:NOWN] = y
    return out
```

```python
from contextlib import ExitStack
import numpy as np
import concourse.bass as bass
import concourse.mybir as mybir
from concourse.bass_utils import run_bass_kernel_spmd

F32 = mybir.dt.float32
BF16 = mybir.dt.bfloat16
AF = mybir.ActivationFunctionType
ALU = mybir.AluOpType
AX = mybir.AxisListType

D = 1024
NOWN = 2048
NTO = 16
NCTX = 256
EPS = 1e-6
OFF_A = 3072
OFF_ZB = 3104
OFF_UA = 4128
OFF_VA = 5152
OFF_ZA = 6176
OFF_G = 7200
EPOCH = 8000
NEGV = -30000.0

C_ID = 0
C_LE = 128
C_GT = 256
C_GE = 384
C_LT = 512
C_NEGU = 640
C_NEGL = 1152
C_NM = 1664
NCONST = C_NM + 14 * 128


class _Op:
    __slots__ = ("eng", "dma", "grp", "gseq", "deps", "sig", "signo", "waits")

    def __init__(self, eng, dma, grp):
        self.eng = eng
        self.dma = dma
        self.grp = grp
        self.gseq = 0
        self.deps = []
        self.sig = False
        self.signo = 0
        self.waits = []


class Prog:
    ENGS = ("pe", "dve", "act", "pool", "sp")

    def __init__(self, nc, plan=None):
        self.nc = nc
        self.plan = plan
        self.dry = plan is None
        self.ops = []
        self.last_w = {}
        self.readers = {}
        self.shared_w = {}
        self.grp_count = {}
        self.idx = 0
        self.last_on = {}
        self.last_dma = {}
        self.pending = {}
        self.serial_dep = None
        self.engs = {"pe": nc.tensor, "dve": nc.vector, "act": nc.scalar,
                     "pool": nc.gpsimd, "sp": nc.sync}

    def _record(self, eng, reads, writes, dma, grp):
        op = _Op(eng, dma, grp)
        oid = len(self.ops)
        deps = {}
        for k in reads:
            w = self.last_w.get(k)
            if w is not None:
                deps[w] = True
            for w in self.shared_w.get(k, ()):
                deps[w] = True
        for k in writes:
            if isinstance(k, tuple) and len(k) == 2 and k[0] == "~":
                kk = k[1]
                w = self.last_w.get(kk)
                if w is not None and w not in deps:
                    deps[w] = False
                for r in self.readers.get(kk, ()):
                    if r not in deps:
                        deps[r] = False
                continue
            w = self.last_w.get(k)
            if w is not None and w not in deps:
                deps[w] = False
            for w in self.shared_w.get(k, ()):
                if w not in deps:
                    deps[w] = False
            for r in self.readers.get(k, ()):
                if r not in deps:
                    deps[r] = False
        for k in reads:
            self.readers.setdefault(k, []).append(oid)
        for k in writes:
            if isinstance(k, tuple) and len(k) == 2 and k[0] == "~":
                self.shared_w.setdefault(k[1], []).append(oid)
                continue
            self.last_w[k] = oid
            self.readers[k] = []
            self.shared_w[k] = []
        if eng in self.pending:
            for did in self.pending.pop(eng):
                deps.setdefault(did, True)
        if self.serial_dep is not None:
            deps.setdefault(self.serial_dep, True)
            self.serial_dep = None
        op.deps = list(deps.items())
        if dma:
            self.grp_count[grp] = self.grp_count.get(grp, 0) + 1
            op.gseq = self.grp_count[grp]
            self.last_dma[grp] = oid
        else:
            self.last_on[eng] = oid
        self.ops.append(op)

    def barrier(self):
        if self.dry:
            snap = list(self.last_on.values()) + list(self.last_dma.values())
            self.pending = {e: list(snap) for e in self.ENGS}

    def analyze(self):
        ops = self.ops

        def need(op, d, is_raw):
            if d.eng != op.eng:
                return True
            if op.eng == "pe":
                return False
            return True

        for op in ops:
            best = {}
            for (did, is_raw) in op.deps:
                d = ops[did]
                if not d.dma and need(op, d, is_raw):
                    if best.get(d.eng, -1) < did:
                        best[d.eng] = did
            for did in best.values():
                ops[did].sig = True
        cnt = {e: 0 for e in self.ENGS}
        for op in ops:
            if op.sig and not op.dma:
                cnt[op.eng] += 1
                op.signo = cnt[op.eng]
        waited = {e: {} for e in self.ENGS}
        for op in ops:
            wl = waited[op.eng]
            ne, ng = {}, {}
            for (did, is_raw) in op.deps:
                d = ops[did]
                if d.dma:
                    ng[d.grp] = max(ng.get(d.grp, 0), d.gseq)
                elif need(op, d, is_raw) and d.sig:
                    ne[d.eng] = max(ne.get(d.eng, 0), d.signo)
            for g, s in ng.items():
                if wl.get(("g", g), 0) < s:
                    op.waits.append(("g", g, s))
                    wl[("g", g)] = s
            for e, s in ne.items():
                if wl.get(e, 0) < s:
                    op.waits.append(("e", e, s))
                    wl[e] = s
        return {"ops": ops, "cnt": cnt, "grp": dict(self.grp_count)}

    def setup_sems(self, stack):
        nc = self.nc
        self.sems = {}
        for e, c in self.plan["cnt"].items():
            n = max(1, (c + EPOCH - 1) // EPOCH)
            self.sems[e] = [stack.enter_context(nc.semaphore(f"s_{e}_{i}")) for i in range(n)]
        self.gsem = {}
        for i, g in enumerate(self.plan["grp"]):
            self.gsem[g] = stack.enter_context(nc.semaphore(f"g{i}"))

    def _emit(self, eng, fn):
        op = self.plan["ops"][self.idx]
        assert op.eng == eng, (self.idx, op.eng, eng)
        E = self.engs[eng]
        for (kind, key, s) in op.waits:
            if kind == "g":
                E.wait_ge(self.gsem[key], 16 * s)
            else:
                E.wait_ge(self.sems[key][(s - 1) // EPOCH], (s - 1) % EPOCH + 1)
        ins = fn(E)
        if op.dma:
            ins.then_inc(self.gsem[op.grp], 16)
        elif op.sig:
            ins.then_inc(self.sems[eng][(op.signo - 1) // EPOCH], 1)
        self.idx += 1

    def op(self, eng, fn, reads=(), writes=()):
        if self.dry:
            self._record(eng, reads, writes, False, None)
        else:
            self._emit(eng, fn)

    def dma(self, out, in_, reads=(), writes=(), grp=None, eng="sp", serial=False):
        if self.dry:
            if serial and grp in self.last_dma:
                self.serial_dep = self.last_dma[grp]
            self._record(eng, reads, writes, True, grp)
        else:
            self._emit(eng, lambda e: e.dma_start(out=out, in_=in_))

    def finish(self, groups):
        if not self.dry:
            E = self.engs["sp"]
            for g in groups:
                E.wait_ge(self.gsem[g], 16 * self.plan["grp"][g])


def bcast_mid(ap2d, n):
    a = ap2d.ap
    return bass.AP(ap2d.tensor, ap2d.offset, [list(a[0]), [0, n], list(a[1])])


def bcast_last(ap2d, n):
    return ap2d.unsqueeze(2).to_broadcast([ap2d.shape[0], ap2d.shape[1], n])


class Builder:
    def __init__(self, nc, P, debug=None):
        self.nc = nc
        self.P = P
        self.debug = debug
        self.dbg_groups = []
        self.bg = None
        self.hti = 0
        self.spawned = []
        self.gsfx = ""
        self.psi = 0
        self.ps_live = set()

    def mm(self, out, lhsT, rhs, start, stop, r, w, skip=False):
        if skip:
            self.P.op("pe", lambda e: e.matmul(out, lhsT=lhsT, rhs=rhs, start=start, stop=stop, skip_group_check=True), r, w)
        else:
            self.P.op("pe", lambda e: e.matmul(out, lhsT=lhsT, rhs=rhs, start=start, stop=stop), r, w)

    def tr(self, out, in_, r, w):
        idb = self.id_bf
        self.P.op("pe", lambda e: e.transpose(out, in_, idb[:]), tuple(r) + ("idbf",), w)

    def act(self, out, in_, func, r, w, scale=None, bias=None, accum=None):
        kw = {}
        if scale is not None:
            kw["scale"] = scale
        if bias is not None:
            kw["bias"] = bias
        if accum is not None:
            kw["accum_out"] = accum
        self.P.op("act", lambda e: e.activation(out=out, in_=in_, func=func, **kw), r, w)

    def tt(self, eng, out, in0, in1, op, r, w):
        self.P.op(eng, lambda e: e.tensor_tensor(out=out, in0=in0, in1=in1, op=op), r, w)

    def ts(self, eng, out, in0, s1, s2, op0, op1, r, w):
        if s2 is None:
            self.P.op(eng, lambda e: e.tensor_scalar(out=out, in0=in0, scalar1=s1, scalar2=None, op0=op0), r, w)
        else:
            self.P.op(eng, lambda e: e.tensor_scalar(out=out, in0=in0, scalar1=s1, scalar2=s2, op0=op0, op1=op1), r, w)

    def stt(self, out, in0, scalar, in1, op0, op1, r, w):
        self.P.op("dve", lambda e: e.scalar_tensor_tensor(out=out, in0=in0, scalar=scalar, in1=in1, op0=op0, op1=op1), r, w)

    def cp(self, eng, out, in_, r, w):
        if eng == "act":
            self.P.op("act", lambda e: e.copy(out=out, in_=in_), r, w)
        else:
            self.P.op(eng, lambda e: e.tensor_copy(out=out, in_=in_), r, w)

    def memset(self, eng, ap, val, w):
        self.P.op(eng, lambda e: e.memset(ap, val), (), w)

    def ps(self):
        for j in range(len(self.psb)):
            i = (self.psi + j) % len(self.psb)
            if i not in self.ps_live:
                self.psi = i + 1
                return self.psb[i], ("ps", i)
        raise RuntimeError("no free PSUM bank")

    def dump(self, name, ap, keys):
        if not self.debug:
            return
        t = self.nc.dram_tensor("dbg_" + name, list(ap.shape), ap.dtype, kind="ExternalOutput").ap()
        self.P.dma(t, ap, tuple(keys), (), grp="dbg_" + name)
        self.dbg_groups.append("dbg_" + name)

    def sb(self, st, name, shape, dt=F32, side="left"):
        self.uid = getattr(self, "uid", 0) + 1
        return st.enter_context(self.nc.sbuf_tensor(f"{name}_{self.uid}", shape, dt, side=side))

    def rsqrt_small(self, out, in_, scale, r, w):
        k = out.shape[1]
        self.ts("dve", out, in_, float(scale), EPS, ALU.mult, ALU.add, r, w)
        self.tt("pool", out, out, self.negh[:, 0:k], ALU.pow, tuple(w) + ("negh",), w)

    def load_w(self, dst, name, dram, col0, ncols, dcol0=0):
        CH = 256
        c = 0
        while c < ncols:
            n = min(CH, ncols - c)
            src = dram[:, col0 + c:col0 + c + n].rearrange("(kc p) c -> p kc c", p=128)
            keys = tuple({(name, (dcol0 + c + j) // 128) for j in range(0, n, 128)} | {(name, (dcol0 + c + n - 1) // 128)})
            grp = f"wq{self.wq % 8}"
            self.wq += 1
            self.P.dma(dst[:, :, dcol0 + c:dcol0 + c + n], src, (), keys, grp=grp, eng="pool", serial=True)
            c += n

    @staticmethod
    def wk(name, c0, c1):
        return tuple((name, j) for j in range(c0 // 128, (c1 - 1) // 128 + 1))

    def build(self, io):
        nc, P = self.nc, self.P
        with ExitStack() as g:
            sb = lambda name, shape, dt=F32: self.sb(g, name, shape, dt, side="right")
            self.pt = g.enter_context(nc.psum_tensor("pt", [128, 8, 128], BF16))
            self.psb = [g.enter_context(nc.psum_tensor(f"psb{i}", [128, 512], F32)) for i in range(7)]
            self.id_bf = sb("id_bf", [128, 128], BF16)
            self.ones_bf = sb("ones_bf", [128, 128], BF16)
            self.negd = sb("negd", [128, 128], BF16)
            self.cst = sb("cst", [128, 4])
            self.negh = sb("negh", [128, 16])
            self.fpar = sb("fpar", [128, 104])
            self.gate_bc = sb("gate_bc", [128, 1024])
            self.hTo = sb("hTo", [128, 8, NOWN], BF16)
            self.hTh = sb("hTh", [128, 8, 1], BF16)
            self.wq = 0
            self.sml = sb("sml", [128, 64])
            self.tmpA = sb("tmpA", [128, 1024])
            self.tmpB = sb("tmpB", [128, 1024])
            self.xi = 0

            self.memset("pool", self.cst[:, 0:1], EPS, ("cst",))
            self.memset("pool", self.cst[:, 1:2], float(np.log(128.0 ** -0.5)), ("cst",))
            self.memset("pool", self.ones_bf[:], 1.0, ("ones",))
            self.memset("pool", self.negh[:], -0.5, ("negh",))
            P.dma(self.fpar[:], io["fparams"], (), ("fpar",), grp="fpar")

            son = ExitStack()
            with son:
                with ExitStack() as s1:
                    self.phase_gdn(s1, son, io)
                P.barrier()
                with ExitStack() as s3:
                    self.Bg = self.sb(s3, "Bg", [128, 8, NOWN], BF16)
                    self.phase3a(io)
                    self.dump("Bg", self.Bg[:], tuple(("Bg", b) for b in range(4)))
                    P.barrier()
                    son.close()
                    self.phase3b(s3, io)
        P.finish(["yout0", "yout1"] + self.dbg_groups)

    def alloc_x(self, st, n=2):
        self.nx = n
        self.xst = [self.sb(st, f"xst{i}", [128, 1024]) for i in range(n)]
        self.xbf = [self.sb(st, f"xbf{i}", [128, 1024], BF16) for i in range(n)]

    def phase_gdn(self, st, son, io):
        nc, P = self.nc, self.P
        sb = lambda name, shape, dt=F32: self.sb(st, name, shape, dt)
        self.cf = sb("cf", [128, C_NEGU])
        self.negb = sb("negb", [128, 2, 512], BF16)
        self.nm = sb("nm", [128, 14, 128], BF16)
        GN = ("gall", "beta", "gam", "bgam", "eta", "egl")
        own = {n: sb(n + "_o", [128, NTO, 16]) for n in GN}
        self.Sst = sb("Sst", [128, 16, 128])
        self.modfm = sb("modfm", [128, 16, 2])
        self.scale1 = sb("scale1", [128, 8, 2])
        P.dma(self.cf[:], io["consts"][:, 0:C_NEGU], (), ("cf",), grp="cf")
        self.cp("dve", self.id_bf[:], self.cf[:, C_ID:C_ID + 128], ("cf",), ("idbf",))
        self.ts("dve", self.negd[:], self.cf[:, C_ID:C_ID + 128], NEGV, None, ALU.mult, None, ("cf",), ("negd",))
        with ExitStack() as s0:
            nmst = self.sb(s0, "nmst", [128, NCONST - C_NEGU])
            P.dma(nmst[:], io["consts"][:, C_NEGU:NCONST], (), ("nmst",), grp="nmst")
            self.cp("dve", self.negb[:].rearrange("p a b -> p (a b)"), nmst[:, 0:1024], ("nmst",), ("negb",))
            self.cp("dve", self.nm[:].rearrange("p a b -> p (a b)"), nmst[:, 1024:], ("nmst",), ("nm",))
            self.phase0(s0, io)
        P.barrier()
        self.dump("gate_bc", self.gate_bc[:], ("gate_bc",))
        self.dump("scale1", self.scale1[:], ("scale1",))
        self.dump("modfm", self.modfm[:], ("modfm",))
        with ExitStack() as s1:
            self.hTx = self.sb(s1, "hTx", [128, 8, NOWN], BF16)
            self.hTc = self.sb(s1, "hTc", [128, 8, NCTX], BF16)
            for n in GN:
                setattr(self, n, self.sb(s1, n, [128, 34, 16]))
            self.gsfx = ""
            with ExitStack() as sx:
                self.alloc_x(sx, 4)
                self.build_hT(io)
            P.barrier()
            with ExitStack() as sab:
                self.ab_stage(sab, io)
            P.barrier()
            for n in GN:
                self.cp("pool", own[n][:], getattr(self, n)[:, 0:NTO, :], (n,), (n + "_o",))
            self.dump("hTo", self.hTo[:], tuple(("hTo", b) for b in range(4)))
            self.dump("hTx", self.hTx[:], tuple(("hTx", b) for b in range(4)))
            self.dump("hTc", self.hTc[:], (("hTc", 0),))
            for nm_ in ("gall", "beta", "gam", "eta", "egl", "bgam"):
                self.dump(nm_, getattr(self, nm_)[:], (nm_,))
            self.memset("pool", self.Sst[:], 0.0, ("Sst",) + tuple(("S", c) for c in range(16)))
            with ExitStack() as s2:
                self.alloc_head_bufs(s2, False)
                for h in range(8):
                    self.head_pass1(h, io)
                self.dump("Sst", self.Sst[:], tuple(("S", c) for c in range(16)))
        P.barrier()
        self.on_store = self.sb(son, "on_store", [128, NTO, 1024], BF16, side="right")
        for n in GN:
            setattr(self, n, own[n])
        self.gsfx = "_o"
        with ExitStack() as s2:
            self.alloc_head_bufs(s2, True)
            self.gon = self.sb(s2, "gon", [128, 128])
            P.dma(self.gon[:], io["r_gon"], (), ("gon",), grp="gon")
            self.pass2_all(io)
            self.dump("on_store", self.on_store[:], ("on_store",))
        P.barrier()

    def phase0(self, st, io):
        P = self.P
        sb = lambda name, shape, dt=F32: self.sb(st, name, shape, dt)
        cv = sb("cv", [128, 16])
        sc = sb("sc", [128, 16])
        screp = sb("screp", [128, 8, 128])
        wm = [sb(f"wm{i}", [128, 8, 512]) for i in range(6)]
        rg = sb("rg", [128, 1024])
        rb = sb("rb", [128, 1024])
        rows = sb("rows", [2, 2048])
        P.dma(cv[:], io["cvec"], (), ("cv",), grp="cv")
        P.dma(rg[:], io["r_gpost"], (), ("rg",), grp="rg")
        P.dma(rb[:], io["r_bmodg"], (), ("rb",), grp="rb")
        self.act(sc[:], cv[:], AF.Silu, ("cv",), ("sc",))
        sc3 = sc[:].rearrange("p (k j) -> p k j", j=2)
        self.cp("dve", screp[:], sc3[:, :, 0:1].to_broadcast([128, 8, 128]), ("sc",), ("screp",))
        pm, pmk = self.ps()
        for piece in range(6):
            w = wm[piece]
            wkey = ("wm", piece)
            src = io["w_mod"][:, piece * 512:(piece + 1) * 512].rearrange("(kc p) c -> p kc c", p=128)
            P.dma(w[:], src, (), (wkey,), grp=f"wm{piece}")
            if piece < 4:
                pr, prk = self.ps()
                for kc in range(8):
                    self.mm(pr[0:2, :], sc[:, kc * 2:kc * 2 + 2], w[:, kc, :], kc == 0, kc == 7, (wkey, "sc"), (prk,))
                self.cp("act", rows[0:2, piece * 512:(piece + 1) * 512], pr[0:2, :], (prk,), (("rows", piece),))
                for i in range(4):
                    blk = piece * 4 + i
                    self.mm(pm[:, blk * 2:blk * 2 + 2], rows[0:2, blk * 128:(blk + 1) * 128],
                            self.cf[0:2, C_ID:C_ID + 2], True, True, (("rows", piece), "cf"), (pmk,))
            else:
                pg, pgk = self.ps()
                for kc in range(8):
                    self.mm(pg[:], screp[:, kc, :], w[:, kc, :], kc == 0, kc == 7, (wkey, "screp"), (pgk,))
                c0 = (piece - 4) * 512
                self.tt("dve", self.tmpA[:, 0:512], pg[:], rb[:, c0:c0 + 512], ALU.add, (pgk, "rb"), ("tmpA",))
                self.tt("dve", self.gate_bc[:, c0:c0 + 512], self.tmpA[:, 0:512], rg[:, c0:c0 + 512], ALU.mult,
                        ("tmpA", "rg"), ("gate_bc",))
        self.tt("dve", self.modfm[:], pm[:, 0:32].rearrange("p (b j) -> p b j", j=2),
                bcast_last(self.fpar[:, 8:24], 2), ALU.add, (pmk, "fpar"), ("modfm",))
        self.ts("dve", self.scale1[:], self.modfm[:, 8:16, :], 1.0, None, ALU.add, None, ("modfm",), ("scale1",))
        self.tt("dve", self.scale1[:], self.scale1[:], bcast_last(self.fpar[:, 0:8], 2), ALU.mult,
                ("scale1", "fpar"), ("scale1",))

    def hT_stageA(self, src_rows):
        P = self.P
        i = self.xi % self.nx
        self.xi += 1
        xs, xb = self.xst[i], self.xbf[i]
        sm = self.sml
        P.dma(xs[:], src_rows, (), (("xst", i),), grp=f"xst{i}")
        c = 2 * i
        self.act(xb[:], xs[:], AF.Square, (("xst", i),), (("xbf", i), ("sml", c)), accum=sm[:, c:c + 1])
        self.rsqrt_small(sm[:, c + 1:c + 2], sm[:, c:c + 1], 1.0 / D, (("sml", c),), (("sml", c + 1),))
        self.ts("dve", xb[:], xs[:], sm[:, c + 1:c + 2], None, ALU.mult, None,
                (("xst", i), ("sml", c + 1)), (("xbf", i),))
        return i

    def hT_stageB(self, i, dst, dkey, tok0, j):
        xb = self.xbf[i]
        q = self.hti % 2
        self.hti += 1
        def bank(n):
            if n == 0:
                return self.pt, "pt"
            return self.psb[3 + n][:].bitcast(BF16).rearrange("p (a b) -> p a b", b=128), ("ps", 3 + n)
        bA, kA = bank(2 * q)
        bB, kB = bank(2 * q + 1)
        for kc in range(8):
            b_, k_ = (bA, kA) if kc % 2 == 0 else (bB, kB)
            self.tr(b_[:, kc // 2, :], xb[:, kc * 128:(kc + 1) * 128], (("xbf", i),), (k_,))
        for kc in range(8):
            if kc % 2 == 0:
                self.act(dst[:, kc, tok0:tok0 + 128], bA[:, kc // 2, :], AF.Identity,
                         (kA, "scale1", "modfm"), (("~", dkey),),
                         scale=self.scale1[:, kc, j:j + 1], bias=self.modfm[:, kc, j:j + 1])
            else:
                self.ts("dve", dst[:, kc, tok0:tok0 + 128], bB[:, kc // 2, :], self.scale1[:, kc, j:j + 1],
                        self.modfm[:, kc, j:j + 1], ALU.mult, ALU.add, (kB, "scale1", "modfm"), (("~", dkey),))

    def build_hT(self, io):
        tiles = []
        for t in range(NTO):
            tiles.append((io["x_own"][t * 128:(t + 1) * 128, :], self.hTo, ("hTo", t // 4), t * 128, 0))
        for t in range(NTO):
            tiles.append((io["x_oth"][t * 128:(t + 1) * 128, :], self.hTx, ("hTx", t // 4), t * 128, 0))
        for t in range(2):
            tiles.append((io["ctx"][t * 128:(t + 1) * 128, :], self.hTc, ("hTc", 0), t * 128, 1))
        pend = None
        for (src, dst, dkey, tok0, j) in tiles:
            i = self.hT_stageA(src)
            if pend is not None:
                self.hT_stageB(*pend)
            pend = (i, dst, dkey, tok0, j)
        self.hT_stageB(*pend)
        self.cp("pool", self.hTh[:], self.hTx[:, :, 0:1], (("hTx", 0),), ("hTh",))

    def hT_of(self, T):
        if T < 16:
            return self.hTo, ("hTo", T // 4), T * 128
        if T < 32:
            return self.hTx, ("hTx", (T - 16) // 4), (T - 16) * 128
        return self.hTc, ("hTc", 0), (T - 32) * 128

    def ab_stage(self, st, io):
        P = self.P
        sb = lambda name, shape, dt=F32: self.sb(st, name, shape, dt)
        wab = sb("wab", [128, 8, 32], BF16)
        ab = sb("ab", [128, 34, 32])
        rdtb = sb("rdtb", [128, 16])
        rneg = sb("rneg", [128, 16])
        P.dma(wab[:], io["w_ab"].rearrange("(kc p) c -> p kc c", p=128), (), ("wab",), grp="wab", eng="pool")
        P.dma(rdtb[:], io["r_dtb"], (), ("rdtb",), grp="rdtb")
        P.dma(rneg[:], io["r_alog"], (), ("rneg",), grp="rneg")
        for b0 in range(0, 34, 16):
            nb = min(16, 34 - b0)
            pa, pak = self.ps()
            for i in range(nb):
                T = b0 + i
                hT, hk, t0 = self.hT_of(T)
                for kc in range(8):
                    self.mm(pa[:, i * 32:(i + 1) * 32], hT[:, kc, t0:t0 + 128], wab[:, kc, :], kc == 0, kc == 7,
                            (hk, "wab"), (pak,))
            self.cp("act", ab[:, b0:b0 + nb, :], pa[:, 0:nb * 32].rearrange("p (t c) -> p t c", c=32), (pak,), ("ab",))
        self.act(rneg[:], rneg[:], AF.Exp, ("rneg",), ("rneg",))
        self.ts("dve", rneg[:], rneg[:], -1.0, None, ALU.mult, None, ("rneg",), ("rneg",))
        g = self.gall
        self.tt("dve", g[:], ab[:, :, 0:16], bcast_mid(rdtb[:], 34), ALU.add, ("ab", "rdtb"), ("gall",))
        self.act(g[:], g[:], AF.Exp, ("gall",), ("gall",))
        self.act(g[:], g[:], AF.Ln, ("gall",), ("gall",), bias=1.0)
        self.tt("dve", g[:], g[:], bcast_mid(rneg[:], 34), ALU.mult, ("gall", "rneg"), ("gall",))
        self.act(self.beta[:], ab[:, :, 16:32], AF.Sigmoid, ("ab",), ("beta",))
        cf = self.cf
        for d in range(2):
            rhs = g[:, :, d * 8:(d + 1) * 8]
            incl = cf[:, C_LE:C_LE + 128] if d == 0 else cf[:, C_GE:C_GE + 128]
            strict = cf[:, C_GT:C_GT + 128] if d == 0 else cf[:, C_LT:C_LT + 128]
            p1, k1 = self.ps()
            self.mm(p1[:, 0:272], incl, rhs, True, True, ("cf", "gall"), (k1,))
            self.act(self.gam[:, :, d * 8:(d + 1) * 8], p1[:, 0:272].rearrange("p (t c) -> p t c", c=8), AF.Exp,
                     (k1,), ("gam",))
            p2, k2 = self.ps()
            self.mm(p2[:, 0:272], strict, rhs, True, True, ("cf", "gall"), (k2,))
            self.act(self.eta[:, :, d * 8:(d + 1) * 8], p2[:, 0:272].rearrange("p (t c) -> p t c", c=8), AF.Exp,
                     (k2,), ("eta",))
            p3, k3 = self.ps()
            self.mm(p3[:, 0:272], incl, rhs, True, False, ("cf", "gall"), (k3,))
            self.mm(p3[:, 0:272], strict, rhs, False, True, ("cf", "gall"), (k3,))
            self.act(self.egl[:, :, d * 8:(d + 1) * 8], p3[:, 0:272].rearrange("p (t c) -> p t c", c=8), AF.Exp,
                     (k3,), ("egl",))
        self.tt("dve", self.bgam[:], self.beta[:], self.gam[:], ALU.mult, ("beta", "gam"), ("bgam",))

    def alloc_head_bufs(self, st, full):
        sb = lambda name, shape, dt=F32: self.sb(st, name, shape, dt)
        self.whs = [sb(f"wh{i}", [128, 8, 384], BF16) for i in range(2)]
        self.praw = sb("praw", [128, NOWN + 2])
        self.sqb = sb("sqb", [128, 1024], BF16)
        ntk = NOWN if full else NOWN + NCTX
        self.kTs = [sb(f"kT{i}", [128, ntk], BF16) for i in range(2)]
        if full:
            self.qTs = [sb(f"qT{i}", [128, NOWN], BF16) for i in range(2)]
        self.ktoks = [sb(f"ktok{i}", [128, ntk // 128, 128], BF16) for i in range(2)]
        self.vtoks = [sb(f"vtok{i}", [128, ntk // 128, 128], BF16) for i in range(2)]
        NS = 2
        self.gm = [sb(f"gm{t}", [128, 4, 128]) for t in range(NS)]
        self.Eb = self.gm
        self.Am = [sb(f"Am{t}", [128, 4, 128], BF16) for t in range(NS)]
        self.Vm = [sb(f"Vm{t}", [128, 4, 128], BF16) for t in range(NS)]
        self.Rm = [[sb(f"Rm{t}{i}", [128, 4, 128], BF16) for i in range(2)] for t in range(NS)]
        self.Rt = [[sb(f"Rt{t}{i}", [128, 4, 128], BF16) for i in range(2)] for t in range(NS)]
        self.Xb = [sb(f"Xb{t}", [128, 4, 256], BF16) for t in range(NS)]
        self.U = [[sb(f"U{d}{p}", [128, 4, 128]) for p in range(2)] for d in range(2)]
        self.WT = [[sb(f"WT{d}{p}", [128, 4, 128], BF16) for p in range(2)] for d in range(2)]
        self.KT = [[sb(f"KT{d}{p}", [128, 4, 128], BF16) for p in range(2)] for d in range(2)]
        if full:
            self.AT = [[sb(f"AT{d}{p}", [128, 4, 128], BF16) for p in range(2)] for d in range(2)]
        self.vnew = [sb(f"vnew{d}", [128, 128], BF16) for d in range(2)]
        self.Sb = [sb(f"Sb{d}", [128, 128], BF16) for d in range(2)]
        if full:
            self.oacc = sb("oacc", [128, NTO, 128])
            self.nsq = self.gm[0]

    def proj_seg(self, h, seg, io):
        P = self.P
        ntok = seg["ntok"]
        hT = seg["hT"]
        loff = seg["loff"]
        hp = h % 2
        wh = self.whs[hp]
        for kind in seg["kinds"]:
            ki = "qkv".index(kind)
            cb = ki * 8 + h
            wcol = lambda tap: self.fpar[:, 32 + tap * 24 + cb:32 + tap * 24 + cb + 1]
            wkeys = self.wk(f"wh{hp}", ki * 128, ki * 128 + 128)
            nblk = (ntok + 511) // 512
            for b in range(nblk):
                n = min(512, ntok - b * 512)
                bk = yield from self.acq(1)
                pp, pk = bk[0]
                for kc in range(8):
                    self.mm(pp[:, 0:n], wh[:, kc, ki * 128:(ki + 1) * 128], hT[:, kc, b * 512:b * 512 + n],
                            kc == 0, kc == 7, wkeys + (seg["hkey"](b),), (pk,))
                yield
                self.cp("act", self.praw[:, 1 + b * 512:1 + b * 512 + n], pp[:, 0:n], (pk,), ("praw",))
                self.rel(bk)
            for side, col in (("left", 0), ("right", ntok + 1)):
                src = seg[side]
                if src is None:
                    self.memset("pool", self.praw[:, col:col + 1], 0.0, ("praw",))
                else:
                    ht, hk, c = src
                    bk = yield from self.acq(1)
                    pp, pk = bk[0]
                    for kc in range(8):
                        self.mm(pp[:, 0:1], wh[:, kc, ki * 128:(ki + 1) * 128], ht[:, kc, c:c + 1],
                                kc == 0, kc == 7, wkeys + (hk,), (pk,))
                    yield
                    self.cp("act", self.praw[:, col:col + 1], pp[:, 0:1], (pk,), ("praw",))
                    self.rel(bk)
            for o in range(0, ntok, 1024):
                n = min(1024, ntok - o)
                acc = self.tmpA
                self.act(acc[:, 0:n], self.praw[:, 1 + o:1 + o + n], AF.Identity, ("praw", "fpar"), ("tmpA",), scale=wcol(1))
                yield
                self.stt(acc[:, 0:n], self.praw[:, o:o + n], wcol(0), acc[:, 0:n], ALU.mult, ALU.add,
                         ("praw", "fpar", "tmpA"), ("tmpA",))
                yield
                self.stt(acc[:, 0:n], self.praw[:, 2 + o:2 + o + n], wcol(2), acc[:, 0:n], ALU.mult, ALU.add,
                         ("praw", "fpar", "tmpA"), ("tmpA",))
                yield
                if kind == "v":
                    self.act(self.sqb[:, 0:n], acc[:, 0:n], AF.Silu, ("tmpA",), ("sqb",))
                    yield
                    src_bf = self.sqb
                    soff = 0
                else:
                    self.act(acc[:, 0:n], acc[:, 0:n], AF.Silu, ("tmpA",), ("tmpA",))
                    yield
                    self.tt("pool", self.sqb[:, 0:n], acc[:, 0:n], acc[:, 0:n], ALU.mult, ("tmpA",), ("sqb",))
                    yield
                    dstT = self.kTs[hp] if kind == "k" else self.qTs[hp]
                    dkey = ("kT", hp) if kind == "k" else ("qT", hp)
                    for c0 in range(0, n, 512):
                        m = min(512, n - c0)
                        bk = yield from self.acq(1)
                        pp, pk = bk[0]
                        self.mm(pp[:, 0:m], self.ones_bf[:], self.sqb[:, c0:c0 + m], True, True, ("ones", "sqb"), (pk,))
                        yield
                        rin = self.tmpB
                        self.act(rin[:, 0:m], pp[:, 0:m], AF.Ln, (pk, "cst"), ("tmpB",), bias=self.cst[:, 0:1])
                        self.rel(bk)
                        if kind == "q":
                            self.act(rin[:, 0:m], rin[:, 0:m], AF.Exp, ("tmpB", "cst"), ("tmpB",), scale=-0.5,
                                     bias=self.cst[:, 1:2])
                        else:
                            self.act(rin[:, 0:m], rin[:, 0:m], AF.Exp, ("tmpB",), ("tmpB",), scale=-0.5)
                        yield
                        self.tt("dve", dstT[:, loff + o + c0:loff + o + c0 + m], acc[:, c0:c0 + m], rin[:, 0:m], ALU.mult,
                                ("tmpA", "tmpB"), (dkey,))
                        yield
                    src_bf = dstT
                    soff = loff + o
                if kind in ("k", "v"):
                    dtok = self.ktoks[hp] if kind == "k" else self.vtoks[hp]
                    dk2 = ("ktok", hp) if kind == "k" else ("vtok", hp)
                    skey = "sqb" if kind == "v" else ("kT", hp)
                    nt = n // 128
                    lt0 = (loff + o) // 128
                    for t in range(nt):
                        self.tr(self.pt[:, t, :], src_bf[:, soff + t * 128:soff + (t + 1) * 128], (skey,), ("pt",))
                    yield
                    self.cp("act", dtok[:, lt0:lt0 + nt, :], self.pt[:, 0:nt, :], ("pt",), (dk2,))
                    yield

    def load_head_w(self, h, io, kinds):
        for kind in kinds:
            ki = "qkv".index(kind)
            self.load_w(self.whs[h % 2], f"wh{h % 2}", io["w_in"], ki * 1024 + h * 128, 128, ki * 128)

    def acq(self, n):
        while True:
            free = []
            for j in range(len(self.psb)):
                i = (self.psi + j) % len(self.psb)
                if i not in self.ps_live:
                    free.append(i)
            if len(free) >= n:
                take = free[:n]
                self.psi = take[-1] + 1
                for i in take:
                    self.ps_live.add(i)
                return [(self.psb[i], ("ps", i)) for i in take]
            yield

    def rel(self, banks):
        for (_, k) in banks:
            self.ps_live.discard(k[1])

    def gdn_pre(self, h, d, T0, nb, l0, full, par, ts):
        cf = self.cf
        col = d * 8 + h
        hp = h % 2
        G = self.gsfx
        N = nb * 128
        if d == 0:
            lh_nat, rm_nat, neg_nat = C_LE, C_GT, 0
            lh_tr, rm_tr, neg_tr = C_GT, C_LE, 1
        else:
            lh_nat, rm_nat, neg_nat = C_GE, C_LT, 1
            lh_tr, rm_tr, neg_tr = C_LT, C_GE, 0
        g_bc = bcast_last(self.gall[:, T0:T0 + nb, col], 128)
        beta_bc = bcast_last(self.beta[:, T0:T0 + nb, col], 128)
        bgam_bc = bcast_last(self.bgam[:, T0:T0 + nb, col], 128)
        eta_bc = bcast_last(self.eta[:, T0:T0 + nb, col], 128)
        kTi = lambda i: self.kTs[hp][:, (l0 + i) * 128:(l0 + i + 1) * 128]
        qTi = lambda i: self.qTs[hp][:, (l0 + i) * 128:(l0 + i + 1) * 128]
        idf = cf[:, C_ID:C_ID + 128]
        idb = self.id_bf
        gm, Eb_, Am, Vm, Xb = self.gm[ts], self.Eb[ts], self.Am[ts], self.Vm[ts], self.Xb[ts]
        Rm, Rt = self.Rm[ts], self.Rt[ts]
        kgm, kEb, kAm, kVm, kXb = ("gm", ts), ("gm", ts), ("Am", ts), ("Vm", ts), ("Xb", ts)
        v3 = lambda p: p[:, 0:N].rearrange("p (a b) -> p a b", b=128)
        self.tt("pool", gm[:, 0:nb, :], bcast_mid(cf[:, rm_nat:rm_nat + 128], nb), g_bc, ALU.mult, ("cf", "gall" + G), (kgm,))
        self.tt("pool", Xb[:, 0:nb, 0:128], self.vtoks[hp][:, l0:l0 + nb, :], beta_bc, ALU.mult, (("vtok", hp), "beta" + G), (kXb,))
        self.tt("pool", Xb[:, 0:nb, 128:256], self.ktoks[hp][:, l0:l0 + nb, :], bgam_bc, ALU.mult, (("ktok", hp), "bgam" + G), (kXb,))
        bk = yield from self.acq(2)
        pD, kD = bk[0]
        pG, kG = bk[1]
        self.mm(pD[:, 0:N], cf[:, lh_nat:lh_nat + 128], gm[:, 0:nb, :].rearrange("p a b -> p (a b)"), True, False,
                ("cf", kgm), (kD,))
        self.mm(pD[:, 0:N], idb[:], self.negb[:, neg_nat, 0:N], False, False, ("idbf", "negb"), (kD,))
        self.mm(pD[:, 0:N], idb[:], bcast_mid(self.negd[:], nb), False, True, ("idbf", "negd"), (kD,))
        for i in range(nb):
            self.mm(pG[:, i * 128:(i + 1) * 128], kTi(i), kTi(i), True, True, (("kT", hp),), (kG,))
        yield
        Eb = Eb_[:, 0:nb, :]
        self.act(Eb, v3(pD), AF.Exp, (kD,), (kEb,))
        yield
        self.tt("pool", Eb, Eb, beta_bc, ALU.mult, (kEb, "beta" + G), (kEb,))
        yield
        self.tt("dve", Am[:, 0:nb, :], v3(pG), Eb, ALU.mult, (kG, kEb), (kAm,))
        self.rel(bk)
        if full:
            self.spawned.append(self.gdn_at(h, d, T0, nb, l0, par, ts))
        yield
        for k in range(7):
            cur = (k + 1) % 2
            nxt = k % 2
            if k == 0:
                Rc = lambda i: idb[:]
                Rtc = lambda i: idb[:]
                rk, rtk = "idbf", "idbf"
            else:
                Rc = (lambda c: lambda i: Rm[c][:, i, :])(cur)
                Rtc = (lambda c: lambda i: Rt[c][:, i, :])(cur)
                rk, rtk = ("Rm", ts, cur), ("Rt", ts, cur)
                if k == 1:
                    Rc = lambda i: Vm[:, i, :]
                    rk = kVm
            bk = yield from self.acq(1)
            pZ, kZ = bk[0]
            for i in range(nb):
                self.mm(pZ[:, i * 128:(i + 1) * 128], Am[:, i, :], Rc(i), i == 0, False, (kAm, rk), (kZ,), skip=True)
            self.mm(pZ[:, 0:N], idb[:], bcast_mid(idb[:], nb), False, True, ("idbf",), (kZ,), skip=True)
            yield
            self.tt("dve", Vm[:, 0:nb, :], v3(pZ), bcast_mid(self.nm[:, d * 7 + k, :], nb), ALU.mult, (kZ, "nm"), (kVm,))
            self.rel(bk)
            bk = yield from self.acq(2 if k < 6 else 1)
            pR, kR = bk[0]
            if k > 0:
                for i in range(nb):
                    self.mm(pR[:, i * 128:(i + 1) * 128], Rtc(i), Vm[:, i, :], True, True, (rtk, kVm), (kR,))
            if k < 6:
                pR2, kR2 = bk[1]
                for i in range(nb):
                    self.mm(pR2[:, i * 128:(i + 1) * 128], Vm[:, i, :], Rtc(i), True, True, (kVm, rtk), (kR2,))
            yield
            if k > 0:
                self.cp("act", Rm[nxt][:, 0:nb, :], v3(pR), (kR,), (("Rm", ts, nxt),))
            if k < 6:
                self.cp("dve" if k % 2 == 0 else "act", Rt[nxt][:, 0:nb, :], v3(pR2), (kR2,), (("Rt", ts, nxt),))
            self.rel(bk)
            if k == 3:
                yield "MID"
                self.tt("pool", self.KT[d][par][:, 0:nb, :], self.ktoks[hp][:, l0:l0 + nb, :], eta_bc, ALU.mult,
                        (("ktok", hp), "eta" + G), (("KT", d, par),))
        Rf = Rm[0]
        rfk = ("Rm", ts, 0)
        bk = yield from self.acq(2)
        pU, kU = bk[0]
        for i in range(nb):
            self.mm(pU[:, i * 128:(i + 1) * 128], Rf[:, i, :], Xb[:, i, 0:128], True, True, (rfk, kXb), (kU,))
        pW, kW = bk[1]
        for i in range(nb):
            self.mm(pW[:, i * 128:(i + 1) * 128], Xb[:, i, 128:256], Rf[:, i, :], True, True, (rfk, kXb), (kW,))
        yield
        self.cp("act", self.U[d][par][:, 0:nb, :], v3(pU), (kU,), (("U", d, par),))
        self.cp("dve", self.WT[d][par][:, 0:nb, :], v3(pW), (kW,), (("WT", d, par),))
        self.rel(bk)
        yield

    def gdn_at(self, h, d, T0, nb, l0, par, ts):
        cf = self.cf
        col = d * 8 + h
        hp = h % 2
        G = self.gsfx
        N = nb * 128
        if d == 0:
            lh_tr, rm_tr, neg_tr = C_GT, C_LE, 1
        else:
            lh_tr, rm_tr, neg_tr = C_LT, C_GE, 0
        g_bc = bcast_last(self.gall[:, T0:T0 + nb, col], 128)
        idb = self.id_bf
        gm = self.gm[ts]
        kgm = ("gm", ts)
        Eb = gm[:, 0:nb, :]
        v3 = lambda p: p[:, 0:N].rearrange("p (a b) -> p a b", b=128)
        self.tt("pool", gm[:, 0:nb, :], bcast_mid(cf[:, rm_tr:rm_tr + 128], nb), g_bc, ALU.mult, ("cf", "gall" + G), (kgm,))
        bk = yield from self.acq(2)
        pD2, kD2 = bk[0]
        pQ, kQ = bk[1]
        self.mm(pD2[:, 0:N], cf[:, lh_tr:lh_tr + 128], gm[:, 0:nb, :].rearrange("p a b -> p (a b)"), True, False,
                ("cf", kgm), (kD2,))
        self.mm(pD2[:, 0:N], idb[:], self.negb[:, neg_tr, 0:N], False, True, ("idbf", "negb"), (kD2,))
        for i in range(nb):
            self.mm(pQ[:, i * 128:(i + 1) * 128], self.kTs[hp][:, (l0 + i) * 128:(l0 + i + 1) * 128],
                    self.qTs[hp][:, (l0 + i) * 128:(l0 + i + 1) * 128], True, True, (("kT", hp), ("qT", hp)), (kQ,))
        yield
        self.act(Eb, v3(pD2), AF.Exp, (kD2,), (kgm,))
        yield
        self.tt("dve", self.AT[d][par][:, 0:nb, :], v3(pQ), Eb, ALU.mult, (kQ, kgm), (("AT", d, par),))
        self.rel(bk)
        yield

    def gdn_step(self, h, d, T, i, par, l, full):
        col = d * 8 + h
        hp = h % 2
        G = self.gsfx
        S = self.Sst[:, col, :]
        sk = ("S", col)
        Sb = self.Sb[d]
        sbk = ("Sb", d)
        vn = self.vnew[d]
        vk = ("vnew", d)
        bk = yield from self.acq(1)
        pw, kw = bk[0]
        self.mm(pw[:, 0:128], self.WT[d][par][:, i, :], Sb[:], True, True, (("WT", d, par), sbk), (kw,))
        yield
        self.tt("dve", vn[:], self.U[d][par][:, i, :], pw[:, 0:128], ALU.subtract, (("U", d, par), kw), (vk,))
        self.rel(bk)
        bk = yield from self.acq(2 if full else 1)
        pS, kS = bk[0]
        self.mm(pS[:, 0:128], self.KT[d][par][:, i, :], vn[:], True, True, (("KT", d, par), vk), (kS,))
        if full:
            po, ko = bk[1]
            self.mm(po[:, 0:128], self.qTs[hp][:, l * 128:(l + 1) * 128], Sb[:], True, True, (("qT", hp), sbk), (ko,))
            self.mm(po[:, 128:256], self.AT[d][par][:, i, :], vn[:], True, True, (("AT", d, par), vk), (ko,))
        yield
        self.stt(S, S, self.egl[:, T, col:col + 1], pS[:, 0:128], ALU.mult, ALU.add, (sk, "egl" + G, kS), (sk,))
        yield
        self.cp("act", Sb[:], S, (sk,), (sbk,))
        if full:
            ok = ("oacc", l)
            self.stt(self.oacc[:, l, :], po[:, 0:128], self.gam[:, T, col:col + 1], self.oacc[:, l, :], ALU.mult, ALU.add,
                     (ko, "gam" + G, ok), (ok,))
            self.tt("dve", self.oacc[:, l, :], po[:, 128:256], self.oacc[:, l, :], ALU.add, (ko, ok), (ok,))
        self.rel(bk)
        yield

    def chain_batches(self, d, T_lo, n):
        out = []
        if d == 0:
            t = T_lo
            while t < T_lo + n:
                nb = min(4, T_lo + n - t)
                out.append((t, nb, list(range(nb))))
                t += nb
        else:
            t = T_lo + n
            while t > T_lo:
                nb = min(4, t - T_lo)
                out.append((t - nb, nb, list(range(nb - 1, -1, -1))))
                t -= nb
        return out

    def bg_step(self):
        if self.bg is not None:
            try:
                next(self.bg)
            except StopIteration:
                self.bg = None

    def bg_drain(self):
        while self.bg is not None:
            self.bg_step()

    def run_round(self, gens):
        gens = list(gens)
        while gens:
            alive = []
            for g_ in gens:
                try:
                    next(g_)
                    alive.append(g_)
                except StopIteration:
                    pass
            gens = alive + self.spawned
            self.spawned = []
            self.bg_step()

    def run_staggered(self, h, chain, full, extra=None):
        d, T_lo, n, lbase = chain
        bat = [(T0, nb, order, T0 - lbase) for (T0, nb, order) in self.chain_batches(d, T_lo, n)]
        nbt = len(bat)
        pre = {}

        def start(bi):
            T0, nb, order, l0 = bat[bi]
            pre[bi] = self.gdn_pre(h, d, T0, nb, l0, full, bi % 2, bi % 2)

        for r in range(-1, nbt):
            entries = []
            if r + 1 < nbt:
                if r + 1 not in pre:
                    start(r + 1)
                entries.append([pre[r + 1], False])
            if r + 2 < nbt:
                start(r + 2)
                entries.append([pre[r + 2], True])
            if r >= 0:
                T0, nb, order, l0 = bat[r]
                entries.append([self.scan_batch(h, d, T0, nb, order, l0, r % 2, full), False])
            if extra is not None:
                for mk in extra(r, nbt):
                    entries.append([mk, False])
            while entries:
                alive = []
                for e in entries:
                    try:
                        v = next(e[0])
                        if v == "MID" and e[1]:
                            continue
                        alive.append(e)
                    except StopIteration:
                        pass
                entries = alive
                self.bg_step()

    def scan_batch(self, h, d, T0, nb, order, l0, par, full):
        for i in order:
            yield from self.gdn_step(h, d, T0 + i, i, par, l0 + i, full)

    def run_chains(self, h, chains, full):
        sched = []
        for ci, (d, T_lo, n, lbase) in enumerate(chains):
            sched.append([(d, T0, nb, order, T0 - lbase, ci) for (T0, nb, order) in self.chain_batches(d, T_lo, n)])
        nround = max(len(s) for s in sched)
        for r in range(-1, nround):
            gens = []
            for s in sched:
                if r + 1 < len(s):
                    d, T0, nb, order, l0, ci = s[r + 1]
                    gens.append(self.gdn_pre(h, d, T0, nb, l0, full, (r + 1) % 2, ci))
            for s in sched:
                if 0 <= r < len(s):
                    d, T0, nb, order, l0, ci = s[r]
                    gens.append(self.scan_batch(h, d, T0, nb, order, l0, r % 2, full))
            self.run_round(gens)

    def set_state(self, h, d):
        col = d * 8 + h
        self.cp("act", self.Sb[d][:], self.Sst[:, col, :], (("S", col), "Sst"), (("Sb", d),))

    def proj_pass1(self, h, io):
        seg = dict(ntok=NCTX, hT=self.hTc, hkey=lambda b: ("hTc", 0), left=None, right=None, loff=NOWN, kinds="kv")
        yield from self.proj_seg(h, seg, io)
        seg = dict(ntok=NOWN, hT=self.hTx, hkey=lambda b: ("hTx", b), left=(self.hTo, ("hTo", 3), NOWN - 1), right=None,
                   loff=0, kinds="kv")
        yield from self.proj_seg(h, seg, io)
        if h + 1 < 8:
            self.load_head_w(h + 1, io, "kv")

    def head_pass1(self, h, io):
        if h == 0:
            self.load_head_w(0, io, "kv")
            self.bg = self.proj_pass1(0, io)
        self.bg_drain()
        self.bg = self.proj_pass1(h + 1, io) if h + 1 < 8 else None
        hp = h % 2
        if h == 0:
            self.dump("p1_kT", self.kTs[hp][:], (("kT", hp),))
            self.dump("p1_ktok", self.ktoks[hp][:], (("ktok", hp),))
            self.dump("p1_vtok", self.vtoks[hp][:], (("vtok", hp),))
        self.set_state(h, 0)
        self.set_state(h, 1)
        def extra(r, nbt):
            if r == nbt - 2:
                return [self.gdn_pre(h, 0, 32, 2, 16, False, 0, nbt % 2)]
            if r == nbt - 1:
                return [self.scan_batch(h, 0, 32, 2, [0, 1], 16, 0, False)]
            return []
        self.run_staggered(h, (1, 16, 18, 16), False, extra)

    def proj_pass2(self, h, io):
        seg = dict(ntok=NOWN, hT=self.hTo, hkey=lambda b: ("hTo", b), left=None, right=(self.hTh, "hTh", 0),
                   loff=0, kinds="qkv")
        yield from self.proj_seg(h, seg, io)
        if h + 1 < 8:
            self.load_head_w(h + 1, io, "qkv")

    def pass2_all(self, io):
        chains = [(0, 0, 16, 0), (1, 0, 16, 0)]
        sched = []
        for ci, (d, T_lo, n, lbase) in enumerate(chains):
            sched.append([(d, T0, nb, order, T0 - lbase, ci) for (T0, nb, order) in self.chain_batches(d, T_lo, n)])
        nround = len(sched[0])

        def pre_gens(h, bi):
            return [self.gdn_pre(h, d, T0, nb, l0, True, bi % 2, ci) for (d, T0, nb, order, l0, ci) in
                    (s_[bi] for s_ in sched)]

        def scan_gens(h, bi):
            return [self.scan_batch(h, d, T0, nb, order, l0, bi % 2, True) for (d, T0, nb, order, l0, ci) in
                    (s_[bi] for s_ in sched)]

        self.load_head_w(0, io, "qkv")
        self.bg = self.proj_pass2(0, io)
        self.bg_drain()
        self.bg = self.proj_pass2(1, io)
        self.run_round(pre_gens(0, 0))
        for h in range(8):
            hp = h % 2
            if h == 0:
                self.dump("p2_qT", self.qTs[hp][:], (("qT", hp),))
                self.dump("p2_kT", self.kTs[hp][:], (("kT", hp),))
                self.dump("p2_vtok", self.vtoks[hp][:], (("vtok", hp),))
            self.memset("pool", self.oacc[:], 0.0, tuple(("oacc", l) for l in range(NTO)))
            self.set_state(h, 0)
            self.set_state(h, 1)
            for r in range(nround):
                gens = []
                if r + 1 < nround:
                    gens += pre_gens(h, r + 1)
                elif h + 1 < 8:
                    self.bg_drain()
                    gens += pre_gens(h + 1, 0)
                gens += scan_gens(h, r)
                self.run_round(gens)
            self.bg = self.proj_pass2(h + 2, io) if h + 2 < 8 else None
            if h == 0:
                self.dump("p2_oacc", self.oacc[:], tuple(("oacc", l) for l in range(NTO)))
            self.head_norm(h)

    def head_norm(self, h):
        okeys = tuple(("oacc", l) for l in range(NTO))
        ssq = self.sml[:, 8:24]
        sq = self.nsq[:]
        for qt in range(4):
            o2 = self.oacc[:, qt * 4:(qt + 1) * 4, :]
            self.tt("dve", sq, o2, o2, ALU.mult, okeys, (("gm", 0),))
            self.P.op("dve", (lambda sq, qt: lambda e: e.tensor_reduce(
                out=self.sml[:, 8 + qt * 4:12 + qt * 4], in_=sq, axis=AX.X, op=ALU.add))(sq, qt),
                (("gm", 0),), (("sml", "ssq"),))
        self.rsqrt_small(ssq, ssq, 1.0 / 128, (("sml", "ssq"),), (("sml", "ssq"),))
        self.tt("dve", self.oacc[:], self.oacc[:], bcast_last(ssq, 128), ALU.mult, okeys + (("sml", "ssq"),), okeys)
        self.tt("pool", self.on_store[:, :, h * 128:(h + 1) * 128], self.oacc[:], bcast_mid(self.gon[:], NTO), ALU.mult,
                okeys + ("gon",), ("on_store",))

    def phase3a(self, io):
        nc, P = self.nc, self.P
        Bg = self.Bg
        hTo = self.hTo
        with ExitStack() as s:
            sb = lambda name, shape, dt=F32: self.sb(s, name, shape, dt)
            w_zb = sb("w_zb", [128, 8, 1024], BF16)
            w_gb = sb("w_gb", [128, 8, 1024], BF16)
            w_pb = sb("w_pb", [128, 8, 1024], BF16)
            ybb = sb("ybb", [128, 1024], BF16)
            ybT = sb("ybT", [128, 8, 512], BF16)
            self.load_w(w_zb, "w_zb", io["w_in"], OFF_ZB, 1024)
            self.load_w(w_gb, "w_gb", io["w_in"], OFF_G + 1024, 1024)
            self.load_w(w_pb, "w_pb", io["w_pb"], 0, 1024)
            ybb2 = sb("ybb2", [128, 1024], BF16)
            gbuf = sb("gbuf", [128, 8, 512])
            ybbs = [ybb, ybb2]
            souts = [self.tmpA, self.tmpB]
            def ZB(t):
                blk, tt_ = divmod(t, 4)
                so, yb = souts[t % 2], ybbs[t % 2]
                sok, ybk = ("so", t % 2), ("ybb", t % 2)
                for half in range(2):
                    pz, kz = self.ps()
                    for kc in range(8):
                        self.mm(pz[:], hTo[:, kc, t * 128:(t + 1) * 128], w_zb[:, kc, half * 512:(half + 1) * 512],
                                kc == 0, kc == 7, (("hTo", blk),) + self.wk("w_zb", half * 512, half * 512 + 512), (kz,))
                    self.act(so[:, half * 512:(half + 1) * 512], pz[:], AF.Silu, (kz,), (sok,))
                self.tt("pool", yb[:], self.on_store[:, t, :], so[:], ALU.mult, ("on_store", sok), (ybk,))

            def MIDP(tp):
                bp, tq = divmod(tp, 4)
                yb, ybk = ybbs[tp % 2], ("ybb", tp % 2)
                for kc in range(8):
                    self.tr(self.pt[:, kc, :], yb[:, kc * 128:(kc + 1) * 128], (ybk,), ("pt",))
                self.cp("dve", ybT[:, :, tq * 128:(tq + 1) * 128], self.pt[:], ("pt",), ("ybT",))
                if tq == 3:
                    for mb in range(8):
                        pB, kB = self.ps()
                        for kc in range(8):
                            self.mm(pB[:], w_pb[:, kc, mb * 128:(mb + 1) * 128], ybT[:, kc, :],
                                    kc == 0, kc == 7, ("ybT",) + self.wk("w_pb", mb * 128, mb * 128 + 128), (kB,))
                        self.tt("dve", Bg[:, mb, bp * 512:(bp + 1) * 512], pB[:], gbuf[:, mb, :], ALU.mult,
                                (kB, ("gbuf", mb)), (("Bg", bp),))

            def GATES(t):
                blk, tt_ = divmod(t, 4)
                for mb in (2 * tt_, 2 * tt_ + 1):
                    pg, kg = self.ps()
                    for kc in range(8):
                        self.mm(pg[:], w_gb[:, kc, mb * 128:(mb + 1) * 128], hTo[:, kc, blk * 512:(blk + 1) * 512],
                                kc == 0, kc == 7, (("hTo", blk),) + self.wk("w_gb", mb * 128, mb * 128 + 128), (kg,))
                    self.act(gbuf[:, mb, :], pg[:], AF.Sigmoid, (kg,), (("gbuf", mb),))

            for t in range(17):
                odd = (t % 4) % 2 == 1
                if t < 16 and odd:
                    GATES(t)
                if t < 16:
                    ZB(t)
                if t >= 1:
                    MIDP(t - 1)
                if t < 16 and not odd:
                    GATES(t)

    def phase3b(self, st, io):
        nc, P = self.nc, self.P
        Bg = self.Bg
        hTo = self.hTo
        yaT = self.sb(st, "yaT", [128, 8, NOWN], BF16)
        with ExitStack() as s:
            sb = lambda name, shape, dt=F32: self.sb(s, name, shape, dt)
            w_ua = sb("w_ua", [128, 8, 1024], BF16)
            w_va = sb("w_va", [128, 8, 1024], BF16)
            w_za = sb("w_za", [128, 8, 1024], BF16)
            wsp = sb("wsp", [128, 8, 128], BF16)
            rlg = sb("rlg", [128, 1024])
            rlb = sb("rlb", [128, 1024])
            rbs = sb("rbs", [128, 1024])
            uz = sb("uz", [128, 8, 512], BF16)
            gvs = [sb(f"gv{i}", [128, 1024]) for i in range(2)]
            vlns = [sb(f"vln{i}", [128, 1024], BF16) for i in range(2)]
            self.load_w(w_va, "w_va", io["w_in"], OFF_VA, 1024)
            self.load_w(w_ua, "w_ua", io["w_in"], OFF_UA, 1024)
            self.load_w(w_za, "w_za", io["w_in"], OFF_ZA, 1024)
            P.dma(gvs[1][:], io["w_spT"], (), (("gv", 1),), grp="wspst")
            self.cp("dve", wsp[:].rearrange("p a b -> p (a b)"), gvs[1][:], (("gv", 1),), ("wsp",))
            P.dma(rlg[:], io["r_lng"], (), ("rlg",), grp="rlg")
            P.dma(rlb[:], io["r_lnb"], (), ("rlb",), grp="rlb")
            P.dma(rbs[:], io["r_bsp"], (), ("rbs",), grp="rbs")
            sm = self.sml
            tu = self.tmpA[:, 0:512]
            tz = self.tmpB[:, 0:512]
            stmp = self.tmpA[:, 512:1024]

            def U(blk):
                hk = ("hTo", blk)
                for cbk in range(8):
                    pu, ku = self.ps()
                    for kc in range(8):
                        self.mm(pu[:], w_ua[:, kc, cbk * 128:(cbk + 1) * 128], hTo[:, kc, blk * 512:(blk + 1) * 512],
                                kc == 0, kc == 7, (hk,) + self.wk("w_ua", cbk * 128, cbk * 128 + 128), (ku,))
                    pz, kz = self.ps()
                    for kc in range(8):
                        self.mm(pz[:], w_za[:, kc, cbk * 128:(cbk + 1) * 128], hTo[:, kc, blk * 512:(blk + 1) * 512],
                                kc == 0, kc == 7, (hk,) + self.wk("w_za", cbk * 128, cbk * 128 + 128), (kz,))
                    if cbk % 2 == 0:
                        self.act(tu, pu[:], AF.Gelu_apprx_tanh, (ku,), ("tu",))
                        self.act(tz, pz[:], AF.Silu, (kz,), ("tz",))
                    else:
                        self.act(tz, pz[:], AF.Silu, (kz,), ("tz",))
                        self.act(tu, pu[:], AF.Gelu_apprx_tanh, (ku,), ("tu",))
                    self.tt("pool", uz[:, cbk, :], tu, tz, ALU.mult, ("tu", "tz"), ("uz",))

            def V(t):
                blk = t // 4
                hk = ("hTo", blk)
                q = t % 2
                gv, vln = gvs[q], vlns[q]
                gk, vk = ("gv", q), ("vln", q)
                c0 = 24 + 8 * q
                K = lambda j: ("sml", c0 + j)
                C = lambda j: sm[:, c0 + j:c0 + j + 1]
                for half in range(2):
                    pv, kv = self.ps()
                    for kc in range(8):
                        self.mm(pv[:], hTo[:, kc, t * 128:(t + 1) * 128], w_va[:, kc, half * 512:(half + 1) * 512],
                                kc == 0, kc == 7, (hk,) + self.wk("w_va", half * 512, half * 512 + 512), (kv,))
                    self.act(gv[:, half * 512:(half + 1) * 512], pv[:], AF.Gelu_apprx_tanh, (kv,), (gk, K(half)), accum=C(half))
                self.act(yaT[:, :, t * 128:(t + 1) * 128], gv[:].rearrange("p (a b) -> p a b", b=128), AF.Square,
                         (gk,), (("yaT", blk), K(2)), accum=C(2))
                self.tt("dve", C(3), C(0), C(1), ALU.add, (K(0), K(1)), (K(3),))
                self.ts("dve", C(3), C(3), 1.0 / 1024, None, ALU.mult, None, (K(3),), (K(3),))
                self.tt("dve", C(5), C(3), C(3), ALU.mult, (K(3),), (K(5),))
                self.ts("dve", C(4), C(2), 1.0 / 1024, None, ALU.mult, None, (K(2),), (K(4),))
                self.tt("dve", C(4), C(4), C(5), ALU.subtract, (K(4), K(5)), (K(4),))
                self.rsqrt_small(C(4), C(4), 1.0, (K(4),), (K(4),))
                self.ts("dve", gv[:], gv[:], C(3), C(4), ALU.subtract, ALU.mult, (gk, K(3), K(4)), (gk,))
                self.tt("pool", gv[:], gv[:], rlg[:], ALU.mult, (gk, "rlg"), (gk,))
                self.tt("pool", vln[:], gv[:], rlb[:], ALU.add, (gk, "rlb"), (vk,))

            def S(t):
                blk, tt_ = divmod(t, 4)
                q = t % 2
                vln, vk = vlns[q], ("vln", q)
                for gh in range(2):
                    pS, kS = self.ps()
                    for gi in range(4):
                        g_ = gh * 4 + gi
                        self.mm(pS[:, gi * 128:(gi + 1) * 128], vln[:, g_ * 128:(g_ + 1) * 128], wsp[:, g_, :], True, True,
                                (vk, "wsp"), (kS,))
                    self.tt("dve", stmp, pS[:], rbs[:, gh * 512:(gh + 1) * 512], ALU.add, (kS, "rbs"), ("stmp",))
                    self.tt("pool", yaT[:, gh * 4:(gh + 1) * 4, t * 128:(t + 1) * 128],
                            stmp.rearrange("p (a b) -> p a b", b=128),
                            uz[:, gh * 4:(gh + 1) * 4, tt_ * 128:(tt_ + 1) * 128], ALU.mult, ("stmp", "uz"), (("yaT", blk),))

            V(0)
            for blk in range(4):
                U(blk)
                for tt_ in range(4):
                    t = blk * 4 + tt_
                    if t + 1 < 16:
                        V(t + 1)
                    S(t)
        P.barrier()
        self.dump("yaT", yaT[:], tuple(("yaT", b) for b in range(4)))
        with ExitStack() as s:
            sb = lambda name, shape, dt=F32: self.sb(s, name, shape, dt)
            w_ga = sb("w_ga", [128, 8, 1024], BF16)
            w_pa = sb("w_pa", [128, 8, 1024], BF16)
            w_out = sb("w_out", [128, 8, 1024], BF16)
            mT = sb("mT", [128, 8, 512], BF16)
            ob = [sb(f"ob{i}", [128, 1024]) for i in range(2)]
            self.alloc_x(s)
            self.load_w(w_ga, "w_ga", io["w_in"], OFF_G, 1024)
            self.load_w(w_pa, "w_pa", io["w_pa"], 0, 1024)
            self.load_w(w_out, "w_out", io["w_out"], 0, 1024)
            sm = self.sml
            for blk in range(4):
                hk = ("hTo", blk)
                for mb in range(8):
                    pg, kg = self.ps()
                    for kc in range(8):
                        self.mm(pg[:], w_ga[:, kc, mb * 128:(mb + 1) * 128], hTo[:, kc, blk * 512:(blk + 1) * 512],
                                kc == 0, kc == 7, (hk,) + self.wk("w_ga", mb * 128, mb * 128 + 128), (kg,))
                    self.act(self.tmpB[:, 0:512], pg[:], AF.Sigmoid, (kg,), ("tmpB",))
                    pA, kA = self.ps()
                    for kc in range(8):
                        self.mm(pA[:], w_pa[:, kc, mb * 128:(mb + 1) * 128], yaT[:, kc, blk * 512:(blk + 1) * 512],
                                kc == 0, kc == 7, (("yaT", blk),) + self.wk("w_pa", mb * 128, mb * 128 + 128), (kA,))
                    self.tt("dve", self.tmpB[:, 512:1024], pA[:], self.tmpB[:, 0:512], ALU.mult, (kA, "tmpB"), ("tmpB2",))
                    self.tt("pool", mT[:, mb, :], self.tmpB[:, 512:1024], Bg[:, mb, blk * 512:(blk + 1) * 512], ALU.add,
                            ("tmpB2", ("Bg", blk)), ("mT",))
                for tt_ in range(4):
                    t = blk * 4 + tt_
                    i = self.xi % self.nx
                    self.xi += 1
                    xs = self.xst[i]
                    P.dma(xs[:], io["x_own"][t * 128:(t + 1) * 128, :], (), (("xst", i),), grp=f"xst{i}")
                    pos = []
                    for half in range(2):
                        po, ko = self.ps()
                        for kc in range(8):
                            self.mm(po[:], mT[:, kc, tt_ * 128:(tt_ + 1) * 128], w_out[:, kc, half * 512:(half + 1) * 512],
                                    kc == 0, kc == 7, ("mT",) + self.wk("w_out", half * 512, half * 512 + 512), (ko,))
                        self.act(self.tmpA[:, half * 512:(half + 1) * 512], po[:], AF.Square, (ko,),
                                 ("tmpA", ("sml", 32 + half)), accum=sm[:, 32 + half:33 + half])
                        pos.append((po, ko))
                    self.tt("dve", sm[:, 34:35], sm[:, 32:33], sm[:, 33:34], ALU.add, (("sml", 32), ("sml", 33)), (("sml", 34),))
                    self.rsqrt_small(sm[:, 34:35], sm[:, 34:35], 1.0 / 1024, (("sml", 34),), (("sml", 34),))
                    o_ = ob[t % 2]
                    okey = ("ob", t % 2)
                    for half in range(2):
                        po, ko = pos[half]
                        self.stt(o_[:, half * 512:(half + 1) * 512], po[:], sm[:, 34:35],
                                 self.gate_bc[:, half * 512:(half + 1) * 512], ALU.mult, ALU.mult,
                                 (ko, ("sml", 34), "gate_bc"), (okey,))
                    self.tt("pool", o_[:], o_[:], xs[:], ALU.add, (okey, ("xst", i)), (okey,))
                    P.dma(io["y"][t * 128:(t + 1) * 128, :], o_[:], (okey,), (), grp=f"yout{t % 2}")


IN_SPECS = [
    ("x_own", [NOWN, D]), ("x_oth", [NOWN, D]), ("ctx", [NCTX, D]), ("cvec", [128, 16]),
    ("w_mod", [D, 3 * D]), ("w_in", [D, 9248]), ("w_ab", [D, 32]),
    ("w_pa", [D, D]), ("w_pb", [D, D]), ("w_out", [D, D]), ("w_spT", [128, 1024]),
    ("fparams", [128, 104]), ("r_dtb", [128, 16]), ("r_alog", [128, 16]),
    ("r_gpost", [128, D]), ("r_bmodg", [128, D]), ("r_lng", [128, D]), ("r_lnb", [128, D]),
    ("r_bsp", [128, D]), ("r_gon", [128, 128]), ("consts", [128, NCONST]),
]


def _program(nc, plan, debug=None):
    io = {}
    for name, shape in IN_SPECS:
        io[name] = nc.dram_tensor(name, shape, F32, kind="ExternalInput").ap()
    io["y"] = nc.dram_tensor("y", [NOWN, D], F32, kind="ExternalOutput").ap()
    P = Prog(nc, plan)
    with ExitStack() as st:
        if plan is not None:
            P.setup_sems(st)
        Builder(nc, P, debug).build(io)
    return P


def build_nc(debug=None):
    nc0 = bass.Bass("TRN2", target_bir_lowering=False)
    P0 = _program(nc0, None, debug)
    plan = P0.analyze()
    nc = bass.Bass("TRN2", target_bir_lowering=False)
    P1 = _program(nc, plan, debug)
    assert P1.idx == len(plan["ops"])
    return nc, plan


def make_consts():
    c = np.zeros((128, NCONST), np.float32)
    i = np.arange(128)
    p, f = i[:, None], i[None, :]
    c[:, C_ID:C_ID + 128] = (p == f)
    c[:, C_LE:C_LE + 128] = (p <= f)
    c[:, C_GT:C_GT + 128] = (p > f)
    c[:, C_GE:C_GE + 128] = (p >= f)
    c[:, C_LT:C_LT + 128] = (p < f)
    c[:, C_NEGU:C_NEGU + 512] = np.tile(np.where(p < f, NEGV, 0.0), (1, 4))
    c[:, C_NEGL:C_NEGL + 512] = np.tile(np.where(p > f, NEGV, 0.0), (1, 4))
    for k in range(7):
        b = 1 << k
        blk = i // (2 * b)
        half = (i // b) % 2
        same = blk[:, None] == blk[None, :]
        mF = same & (half[:, None] == 0) & (half[None, :] == 1)
        mB = same & (half[:, None] == 1) & (half[None, :] == 0)
        c[:, C_NM + k * 128:C_NM + (k + 1) * 128] = -mF.astype(np.float32) + (p == f)
        c[:, C_NM + (7 + k) * 128:C_NM + (8 + k) * 128] = -mB.astype(np.float32) + (p == f)
    return c


def fm(v, n):
    return np.ascontiguousarray(np.asarray(v, np.float32).reshape(n, 128).T)


def rows(v):
    v = np.asarray(v, np.float32).reshape(1, -1)
    return np.ascontiguousarray(np.broadcast_to(v, (128, v.shape[1])))


def make_in_maps(x, c, ctx, c_ctx, w_mod, b_mod, g_pre, g_post, w_in, w_conv, a_log, dt_bias,
                 g_onorm, gm_ln_g, gm_ln_b, w_sp, b_sp, w_pa, w_pb, w_out):
    f32 = lambda a: np.ascontiguousarray(np.asarray(a, np.float32))
    w_mod0, b_mod0, w_in0 = f32(w_mod[0]), f32(b_mod[0]), f32(w_in[0])
    w_pa0, w_pb0, w_out0 = f32(w_pa[0]), f32(w_pb[0]), f32(w_out[0])
    consts = make_consts()
    maps = []
    for r in range(8):
        b, half = r // 2, r % 2
        rev = half == 1
        xs = np.asarray(x[b], np.float32)
        cx = np.asarray(ctx[b], np.float32)
        if rev:
            xs = xs[::-1]
            cx = cx[::-1]
        wc = np.asarray(w_conv[0], np.float32)
        al = np.asarray(a_log[0], np.float32)
        db = np.asarray(dt_bias[0], np.float32)
        wsp = np.asarray(w_sp[0], np.float32)
        bsp = np.asarray(b_sp[0], np.float32)
        wab = w_in0[:, OFF_A:OFF_ZB]
        if rev:
            wc = wc[::-1]
            al = al[::-1]
            db = db[::-1]
            wsp = wsp[:, ::-1, ::-1]
            bsp = bsp[:, ::-1]
            wab = wab[:, [8, 9, 10, 11, 12, 13, 14, 15, 0, 1, 2, 3, 4, 5, 6, 7,
                          24, 25, 26, 27, 28, 29, 30, 31, 16, 17, 18, 19, 20, 21, 22, 23]]
        cvec = np.zeros((128, 16), np.float32)
        cb_fm = fm(c[b], 8)
        cc_fm = fm(c_ctx, 8)
        cvec[:, 0::2] = cb_fm
        cvec[:, 1::2] = cc_fm
        fpar = np.zeros((128, 104), np.float32)
        fpar[:, 0:8] = fm(g_pre[0], 8)
        fpar[:, 8:32] = fm(b_mod0, 24)
        for tap in range(3):
            fpar[:, 32 + tap * 24:32 + (tap + 1) * 24] = fm(wc[tap], 24)
        m = {
            "x_own": f32(xs[:NOWN]), "x_oth": f32(xs[NOWN:]), "ctx": f32(cx), "cvec": cvec,
            "w_mod": w_mod0, "w_in": w_in0, "w_ab": f32(wab), "w_pa": w_pa0, "w_pb": w_pb0, "w_out": w_out0,
            "w_spT": f32(np.transpose(wsp, (2, 0, 1)).reshape(128, 1024)),
            "fparams": fpar, "r_dtb": rows(db.reshape(-1)), "r_alog": rows(al.reshape(-1)),
            "r_gpost": rows(g_post[0]), "r_bmodg": rows(b_mod0[2048:]), "r_lng": rows(gm_ln_g[0]),
            "r_lnb": rows(gm_ln_b[0]), "r_bsp": rows(bsp.reshape(-1)), "r_gon": rows(g_onorm[0]),
            "consts": consts,
        }
        maps.append(m)
    return maps


_CACHE = {}


def kernel(**inputs):
    if "nc" not in _CACHE:
        _CACHE["nc"] = build_nc()[0]
    nc = _CACHE["nc"]
    maps = make_in_maps(**inputs)
    res = run_bass_kernel_spmd(nc, maps, core_ids=list(range(8)))
    out = np.empty((4, 4096, D), np.float32)
    for r in range(8):
        b, half = r // 2, r % 2
        y = np.asarray(res.results[r]["y"], np.float32)
        if half == 0:
            out[b, :NOWN] = y
        else:
            out[b, ::-1][:NOWN] = y
    return out
```

```python
from contextlib import ExitStack
import numpy as np
import concourse.bass as bass
import concourse.mybir as mybir
from concourse.bass_utils import run_bass_kernel_spmd

F32 = mybir.dt.float32
BF16 = mybir.dt.bfloat16
AF = mybir.ActivationFunctionType
ALU = mybir.AluOpType
AX = mybir.AxisListType

D = 1024
NOWN = 2048
NTO = 16
NCTX = 256
EPS = 1e-6
OFF_A = 3072
OFF_ZB = 3104
OFF_UA = 4128
OFF_VA = 5152
OFF_ZA = 6176
OFF_G = 7200
EPOCH = 8000
NEGV = -30000.0

C_ID = 0
C_LE = 128
C_GT = 256
C_GE = 384
C_LT = 512
C_NEGU = 640
C_NEGL = 1152
C_NM = 1664
NCONST = C_NM + 14 * 128


class _Op:
    __slots__ = ("eng", "dma", "grp", "gseq", "deps", "sig", "signo", "waits")

    def __init__(self, eng, dma, grp):
        self.eng = eng
        self.dma = dma
        self.grp = grp
        self.gseq = 0
        self.deps = []
        self.sig = False
        self.signo = 0
        self.waits = []


class Prog:
    ENGS = ("pe", "dve", "act", "pool", "sp")

    def __init__(self, nc, plan=None):
        self.nc = nc
        self.plan = plan
        self.dry = plan is None
        self.ops = []
        self.last_w = {}
        self.readers = {}
        self.shared_w = {}
        self.grp_count = {}
        self.idx = 0
        self.last_on = {}
        self.last_dma = {}
        self.pending = {}
        self.serial_dep = None
        self.engs = {"pe": nc.tensor, "dve": nc.vector, "act": nc.scalar,
                     "pool": nc.gpsimd, "sp": nc.sync}

    def _record(self, eng, reads, writes, dma, grp):
        op = _Op(eng, dma, grp)
        oid = len(self.ops)
        deps = {}
        for k in reads:
            w = self.last_w.get(k)
            if w is not None:
                deps[w] = True
            for w in self.shared_w.get(k, ()):
                deps[w] = True
        for k in writes:
            if isinstance(k, tuple) and len(k) == 2 and k[0] == "~":
                kk = k[1]
                w = self.last_w.get(kk)
                if w is not None and w not in deps:
                    deps[w] = False
                for r in self.readers.get(kk, ()):
                    if r not in deps:
                        deps[r] = False
                continue
            w = self.last_w.get(k)
            if w is not None and w not in deps:
                deps[w] = False
            for w in self.shared_w.get(k, ()):
                if w not in deps:
                    deps[w] = False
            for r in self.readers.get(k, ()):
                if r not in deps:
                    deps[r] = False
        for k in reads:
            self.readers.setdefault(k, []).append(oid)
        for k in writes:
            if isinstance(k, tuple) and len(k) == 2 and k[0] == "~":
                self.shared_w.setdefault(k[1], []).append(oid)
                continue
            self.last_w[k] = oid
            self.readers[k] = []
            self.shared_w[k] = []
        if eng in self.pending:
            for did in self.pending.pop(eng):
                deps.setdefault(did, True)
        if self.serial_dep is not None:
            deps.setdefault(self.serial_dep, True)
            self.serial_dep = None
        op.deps = list(deps.items())
        if dma:
            self.grp_count[grp] = self.grp_count.get(grp, 0) + 1
            op.gseq = self.grp_count[grp]
            self.last_dma[grp] = oid
        else:
            self.last_on[eng] = oid
        self.ops.append(op)

    def barrier(self):
        if self.dry:
            snap = list(self.last_on.values()) + list(self.last_dma.values())
            self.pending = {e: list(snap) for e in self.ENGS}

    def analyze(self):
        ops = self.ops

        def need(op, d, is_raw):
            if d.eng != op.eng:
                return True
            if op.eng == "pe":
                return False
            return True

        for op in ops:
            best = {}
            for (did, is_raw) in op.deps:
                d = ops[did]
                if not d.dma and need(op, d, is_raw):
                    if best.get(d.eng, -1) < did:
                        best[d.eng] = did
            for did in best.values():
                ops[did].sig = True
        cnt = {e: 0 for e in self.ENGS}
        for op in ops:
            if op.sig and not op.dma:
                cnt[op.eng] += 1
                op.signo = cnt[op.eng]
        waited = {e: {} for e in self.ENGS}
        for op in ops:
            wl = waited[op.eng]
            ne, ng = {}, {}
            for (did, is_raw) in op.deps:
                d = ops[did]
                if d.dma:
                    ng[d.grp] = max(ng.get(d.grp, 0), d.gseq)
                elif need(op, d, is_raw) and d.sig:
                    ne[d.eng] = max(ne.get(d.eng, 0), d.signo)
            for g, s in ng.items():
                if wl.get(("g", g), 0) < s:
                    op.waits.append(("g", g, s))
                    wl[("g", g)] = s
            for e, s in ne.items():
                if wl.get(e, 0) < s:
                    op.waits.append(("e", e, s))
                    wl[e] = s
        return {"ops": ops, "cnt": cnt, "grp": dict(self.grp_count)}

    def setup_sems(self, stack):
        nc = self.nc
        self.sems = {}
        for e, c in self.plan["cnt"].items():
            n = max(1, (c + EPOCH - 1) // EPOCH)
            self.sems[e] = [stack.enter_context(nc.semaphore(f"s_{e}_{i}")) for i in range(n)]
        self.gsem = {}
        for i, g in enumerate(self.plan["grp"]):
            self.gsem[g] = stack.enter_context(nc.semaphore(f"g{i}"))

    def _emit(self, eng, fn):
        op = self.plan["ops"][self.idx]
        assert op.eng == eng, (self.idx, op.eng, eng)
        E = self.engs[eng]
        for (kind, key, s) in op.waits:
            if kind == "g":
                E.wait_ge(self.gsem[key], 16 * s)
            else:
                E.wait_ge(self.sems[key][(s - 1) // EPOCH], (s - 1) % EPOCH + 1)
        ins = fn(E)
        if op.dma:
            ins.then_inc(self.gsem[op.grp], 16)
        elif op.sig:
            ins.then_inc(self.sems[eng][(op.signo - 1) // EPOCH], 1)
        self.idx += 1

    def op(self, eng, fn, reads=(), writes=()):
        if self.dry:
            self._record(eng, reads, writes, False, None)
        else:
            self._emit(eng, fn)

    def dma(self, out, in_, reads=(), writes=(), grp=None, eng="sp", serial=False):
        if self.dry:
            if serial and grp in self.last_dma:
                self.serial_dep = self.last_dma[grp]
            self._record(eng, reads, writes, True, grp)
        else:
            self._emit(eng, lambda e: e.dma_start(out=out, in_=in_))

    def finish(self, groups):
        if not self.dry:
            E = self.engs["sp"]
            for g in groups:
                E.wait_ge(self.gsem[g], 16 * self.plan["grp"][g])


def bcast_mid(ap2d, n):
    a = ap2d.ap
    return bass.AP(ap2d.tensor, ap2d.offset, [list(a[0]), [0, n], list(a[1])])


def bcast_last(ap2d, n):
    return ap2d.unsqueeze(2).to_broadcast([ap2d.shape[0], ap2d.shape[1], n])


class Builder:
    def __init__(self, nc, P, debug=None):
        self.nc = nc
        self.P = P
        self.debug = debug
        self.dbg_groups = []
        self.bg = None
        self.hti = 0
        self.spawned = []
        self.gsfx = ""
        self.psi = 0
        self.ps_live = set()

    def mm(self, out, lhsT, rhs, start, stop, r, w, skip=False):
        if skip:
            self.P.op("pe", lambda e: e.matmul(out, lhsT=lhsT, rhs=rhs, start=start, stop=stop, skip_group_check=True), r, w)
        else:
            self.P.op("pe", lambda e: e.matmul(out, lhsT=lhsT, rhs=rhs, start=start, stop=stop), r, w)

    def tr(self, out, in_, r, w):
        idb = self.id_bf
        self.P.op("pe", lambda e: e.transpose(out, in_, idb[:]), tuple(r) + ("idbf",), w)

    def act(self, out, in_, func, r, w, scale=None, bias=None, accum=None):
        kw = {}
        if scale is not None:
            kw["scale"] = scale
        if bias is not None:
            kw["bias"] = bias
        if accum is not None:
            kw["accum_out"] = accum
        self.P.op("act", lambda e: e.activation(out=out, in_=in_, func=func, **kw), r, w)

    def tt(self, eng, out, in0, in1, op, r, w):
        self.P.op(eng, lambda e: e.tensor_tensor(out=out, in0=in0, in1=in1, op=op), r, w)

    def ts(self, eng, out, in0, s1, s2, op0, op1, r, w):
        if s2 is None:
            self.P.op(eng, lambda e: e.tensor_scalar(out=out, in0=in0, scalar1=s1, scalar2=None, op0=op0), r, w)
        else:
            self.P.op(eng, lambda e: e.tensor_scalar(out=out, in0=in0, scalar1=s1, scalar2=s2, op0=op0, op1=op1), r, w)

    def stt(self, out, in0, scalar, in1, op0, op1, r, w):
        self.P.op("dve", lambda e: e.scalar_tensor_tensor(out=out, in0=in0, scalar=scalar, in1=in1, op0=op0, op1=op1), r, w)

    def cp(self, eng, out, in_, r, w):
        if eng == "act":
            self.P.op("act", lambda e: e.copy(out=out, in_=in_), r, w)
        else:
            self.P.op(eng, lambda e: e.tensor_copy(out=out, in_=in_), r, w)

    def memset(self, eng, ap, val, w):
        self.P.op(eng, lambda e: e.memset(ap, val), (), w)

    def ps(self):
        for j in range(len(self.psb)):
            i = (self.psi + j) % len(self.psb)
            if i not in self.ps_live:
                self.psi = i + 1
                return self.psb[i], ("ps", i)
        raise RuntimeError("no free PSUM bank")

    def dump(self, name, ap, keys):
        if not self.debug:
            return
        t = self.nc.dram_tensor("dbg_" + name, list(ap.shape), ap.dtype, kind="ExternalOutput").ap()
        self.P.dma(t, ap, tuple(keys), (), grp="dbg_" + name)
        self.dbg_groups.append("dbg_" + name)

    def sb(self, st, name, shape, dt=F32, side="left"):
        self.uid = getattr(self, "uid", 0) + 1
        return st.enter_context(self.nc.sbuf_tensor(f"{name}_{self.uid}", shape, dt, side=side))

    def rsqrt_small(self, out, in_, scale, r, w):
        k = out.shape[1]
        self.ts("dve", out, in_, float(scale), EPS, ALU.mult, ALU.add, r, w)
        self.tt("pool", out, out, self.negh[:, 0:k], ALU.pow, tuple(w) + ("negh",), w)

    def load_w(self, dst, name, dram, col0, ncols, dcol0=0):
        CH = 256
        c = 0
        while c < ncols:
            n = min(CH, ncols - c)
            src = dram[:, col0 + c:col0 + c + n].rearrange("(kc p) c -> p kc c", p=128)
            keys = tuple({(name, (dcol0 + c + j) // 128) for j in range(0, n, 128)} | {(name, (dcol0 + c + n - 1) // 128)})
            grp = f"wq{self.wq % 8}"
            self.wq += 1
            self.P.dma(dst[:, :, dcol0 + c:dcol0 + c + n], src, (), keys, grp=grp, eng="pool", serial=True)
            c += n

    @staticmethod
    def wk(name, c0, c1):
        return tuple((name, j) for j in range(c0 // 128, (c1 - 1) // 128 + 1))

    def build(self, io):
        nc, P = self.nc, self.P
        with ExitStack() as g:
            sb = lambda name, shape, dt=F32: self.sb(g, name, shape, dt, side="right")
            self.pt = g.enter_context(nc.psum_tensor("pt", [128, 8, 128], BF16))
            self.psb = [g.enter_context(nc.psum_tensor(f"psb{i}", [128, 512], F32)) for i in range(7)]
            self.id_bf = sb("id_bf", [128, 128], BF16)
            self.ones_bf = sb("ones_bf", [128, 128], BF16)
            self.negd = sb("negd", [128, 128], BF16)
            self.cst = sb("cst", [128, 4])
            self.negh = sb("negh", [128, 16])
            self.fpar = sb("fpar", [128, 104])
            self.gate_bc = sb("gate_bc", [128, 1024])
            self.hTo = sb("hTo", [128, 8, NOWN], BF16)
            self.hTh = sb("hTh", [128, 8, 1], BF16)
            self.wq = 0
            self.sml = sb("sml", [128, 64])
            self.tmpA = sb("tmpA", [128, 1024])
            self.tmpB = sb("tmpB", [128, 1024])
            self.xi = 0

            self.memset("pool", self.cst[:, 0:1], EPS, ("cst",))
            self.memset("pool", self.cst[:, 1:2], float(np.log(128.0 ** -0.5)), ("cst",))
            self.memset("pool", self.ones_bf[:], 1.0, ("ones",))
            self.memset("pool", self.negh[:], -0.5, ("negh",))
            P.dma(self.fpar[:], io["fparams"], (), ("fpar",), grp="fpar")

            son = ExitStack()
            with son:
                with ExitStack() as s1:
                    self.phase_gdn(s1, son, io)
                P.barrier()
                with ExitStack() as s3:
                    self.Bg = self.sb(s3, "Bg", [128, 8, NOWN], BF16)
                    self.phase3a(io)
                    self.dump("Bg", self.Bg[:], tuple(("Bg", b) for b in range(4)))
                    P.barrier()
                    son.close()
                    self.phase3b(s3, io)
        P.finish(["yout0", "yout1"] + self.dbg_groups)

    def alloc_x(self, st, n=2):
        self.nx = n
        self.xst = [self.sb(st, f"xst{i}", [128, 1024]) for i in range(n)]
        self.xbf = [self.sb(st, f"xbf{i}", [128, 1024], BF16) for i in range(n)]

    def phase_gdn(self, st, son, io):
        nc, P = self.nc, self.P
        sb = lambda name, shape, dt=F32: self.sb(st, name, shape, dt)
        self.cf = sb("cf", [128, C_NEGU])
        self.negb = sb("negb", [128, 2, 512], BF16)
        self.nm = sb("nm", [128, 14, 128], BF16)
        GN = ("gall", "beta", "gam", "bgam", "eta", "egl")
        own = {n: sb(n + "_o", [128, NTO, 16]) for n in GN}
        self.Sst = sb("Sst", [128, 16, 128])
        self.modfm = sb("modfm", [128, 16, 2])
        self.scale1 = sb("scale1", [128, 8, 2])
        P.dma(self.cf[:], io["consts"][:, 0:C_NEGU], (), ("cf",), grp="cf")
        self.cp("dve", self.id_bf[:], self.cf[:, C_ID:C_ID + 128], ("cf",), ("idbf",))
        self.ts("dve", self.negd[:], self.cf[:, C_ID:C_ID + 128], NEGV, None, ALU.mult, None, ("cf",), ("negd",))
        with ExitStack() as s0:
            nmst = self.sb(s0, "nmst", [128, NCONST - C_NEGU])
            P.dma(nmst[:], io["consts"][:, C_NEGU:NCONST], (), ("nmst",), grp="nmst")
            self.cp("dve", self.negb[:].rearrange("p a b -> p (a b)"), nmst[:, 0:1024], ("nmst",), ("negb",))
            self.cp("dve", self.nm[:].rearrange("p a b -> p (a b)"), nmst[:, 1024:], ("nmst",), ("nm",))
            self.phase0(s0, io)
        P.barrier()
        self.dump("gate_bc", self.gate_bc[:], ("gate_bc",))
        self.dump("scale1", self.scale1[:], ("scale1",))
        self.dump("modfm", self.modfm[:], ("modfm",))
        with ExitStack() as s1:
            self.hTx = self.sb(s1, "hTx", [128, 8, NOWN], BF16)
            self.hTc = self.sb(s1, "hTc", [128, 8, NCTX], BF16)
            for n in GN:
                setattr(self, n, self.sb(s1, n, [128, 34, 16]))
            self.gsfx = ""
            with ExitStack() as sx:
                self.alloc_x(sx, 4)
                self.build_hT(io)
            P.barrier()
            with ExitStack() as sab:
                self.ab_stage(sab, io)
            P.barrier()
            for n in GN:
                self.cp("pool", own[n][:], getattr(self, n)[:, 0:NTO, :], (n,), (n + "_o",))
            self.dump("hTo", self.hTo[:], tuple(("hTo", b) for b in range(4)))
            self.dump("hTx", self.hTx[:], tuple(("hTx", b) for b in range(4)))
            self.dump("hTc", self.hTc[:], (("hTc", 0),))
            for nm_ in ("gall", "beta", "gam", "eta", "egl", "bgam"):
                self.dump(nm_, getattr(self, nm_)[:], (nm_,))
            self.memset("pool", self.Sst[:], 0.0, ("Sst",) + tuple(("S", c) for c in range(16)))
            with ExitStack() as s2:
                self.alloc_head_bufs(s2, False)
                for h in range(8):
                    self.head_pass1(h, io)
                self.dump("Sst", self.Sst[:], tuple(("S", c) for c in range(16)))
        P.barrier()
        self.on_store = self.sb(son, "on_store", [128, NTO, 1024], BF16, side="right")
        for n in GN:
            setattr(self, n, own[n])
        self.gsfx = "_o"
        with ExitStack() as s2:
            self.alloc_head_bufs(s2, True)
            self.gon = self.sb(s2, "gon", [128, 128])
            P.dma(self.gon[:], io["r_gon"], (), ("gon",), grp="gon")
            self.pass2_all(io)
            self.dump("on_store", self.on_store[:], ("on_store",))
        P.barrier()

    def phase0(self, st, io):
        P = self.P
        sb = lambda name, shape, dt=F32: self.sb(st, name, shape, dt)
        cv = sb("cv", [128, 16])
        sc = sb("sc", [128, 16])
        screp = sb("screp", [128, 8, 128])
        wm = [sb(f"wm{i}", [128, 8, 512]) for i in range(6)]
        rg = sb("rg", [128, 1024])
        rb = sb("rb", [128, 1024])
        rows = sb("rows", [2, 2048])
        P.dma(cv[:], io["cvec"], (), ("cv",), grp="cv")
        P.dma(rg[:], io["r_gpost"], (), ("rg",), grp="rg")
        P.dma(rb[:], io["r_bmodg"], (), ("rb",), grp="rb")
        self.act(sc[:], cv[:], AF.Silu, ("cv",), ("sc",))
        sc3 = sc[:].rearrange("p (k j) -> p k j", j=2)
        self.cp("dve", screp[:], sc3[:, :, 0:1].to_broadcast([128, 8, 128]), ("sc",), ("screp",))
        pm, pmk = self.ps()
        for piece in range(6):
            w = wm[piece]
            wkey = ("wm", piece)
            src = io["w_mod"][:, piece * 512:(piece + 1) * 512].rearrange("(kc p) c -> p kc c", p=128)
            P.dma(w[:], src, (), (wkey,), grp=f"wm{piece}")
            if piece < 4:
                pr, prk = self.ps()
                for kc in range(8):
                    self.mm(pr[0:2, :], sc[:, kc * 2:kc * 2 + 2], w[:, kc, :], kc == 0, kc == 7, (wkey, "sc"), (prk,))
                self.cp("act", rows[0:2, piece * 512:(piece + 1) * 512], pr[0:2, :], (prk,), (("rows", piece),))
                for i in range(4):
                    blk = piece * 4 + i
                    self.mm(pm[:, blk * 2:blk * 2 + 2], rows[0:2, blk * 128:(blk + 1) * 128],
                            self.cf[0:2, C_ID:C_ID + 2], True, True, (("rows", piece), "cf"), (pmk,))
            else:
                pg, pgk = self.ps()
                for kc in range(8):
                    self.mm(pg[:], screp[:, kc, :], w[:, kc, :], kc == 0, kc == 7, (wkey, "screp"), (pgk,))
                c0 = (piece - 4) * 512
                self.tt("dve", self.tmpA[:, 0:512], pg[:], rb[:, c0:c0 + 512], ALU.add, (pgk, "rb"), ("tmpA",))
                self.tt("dve", self.gate_bc[:, c0:c0 + 512], self.tmpA[:, 0:512], rg[:, c0:c0 + 512], ALU.mult,
                        ("tmpA", "rg"), ("gate_bc",))
        self.tt("dve", self.modfm[:], pm[:, 0:32].rearrange("p (b j) -> p b j", j=2),
                bcast_last(self.fpar[:, 8:24], 2), ALU.add, (pmk, "fpar"), ("modfm",))
        self.ts("dve", self.scale1[:], self.modfm[:, 8:16, :], 1.0, None, ALU.add, None, ("modfm",), ("scale1",))
        self.tt("dve", self.scale1[:], self.scale1[:], bcast_last(self.fpar[:, 0:8], 2), ALU.mult,
                ("scale1", "fpar"), ("scale1",))

    def hT_stageA(self, src_rows):
        P = self.P
        i = self.xi % self.nx
        self.xi += 1
        xs, xb = self.xst[i], self.xbf[i]
        sm = self.sml
        P.dma(xs[:], src_rows, (), (("xst", i),), grp=f"xst{i}")
        c = 2 * i
        self.act(xb[:], xs[:], AF.Square, (("xst", i),), (("xbf", i), ("sml", c)), accum=sm[:, c:c + 1])
        self.rsqrt_small(sm[:, c + 1:c + 2], sm[:, c:c + 1], 1.0 / D, (("sml", c),), (("sml", c + 1),))
        self.ts("dve", xb[:], xs[:], sm[:, c + 1:c + 2], None, ALU.mult, None,
                (("xst", i), ("sml", c + 1)), (("xbf", i),))
        return i

    def hT_stageB(self, i, dst, dkey, tok0, j):
        xb = self.xbf[i]
        q = self.hti % 2
        self.hti += 1
        def bank(n):
            if n == 0:
                return self.pt, "pt"
            return self.psb[3 + n][:].bitcast(BF16).rearrange("p (a b) -> p a b", b=128), ("ps", 3 + n)
        bA, kA = bank(2 * q)
        bB, kB = bank(2 * q + 1)
        for kc in range(8):
            b_, k_ = (bA, kA) if kc % 2 == 0 else (bB, kB)
            self.tr(b_[:, kc // 2, :], xb[:, kc * 128:(kc + 1) * 128], (("xbf", i),), (k_,))
        for kc in range(8):
            if kc % 2 == 0:
                self.act(dst[:, kc, tok0:tok0 + 128], bA[:, kc // 2, :], AF.Identity,
                         (kA, "scale1", "modfm"), (("~", dkey),),
                         scale=self.scale1[:, kc, j:j + 1], bias=self.modfm[:, kc, j:j + 1])
            else:
                self.ts("dve", dst[:, kc, tok0:tok0 + 128], bB[:, kc // 2, :], self.scale1[:, kc, j:j + 1],
                        self.modfm[:, kc, j:j + 1], ALU.mult, ALU.add, (kB, "scale1", "modfm"), (("~", dkey),))

    def build_hT(self, io):
        tiles = []
        for t in range(NTO):
            tiles.append((io["x_own"][t * 128:(t + 1) * 128, :], self.hTo, ("hTo", t // 4), t * 128, 0))
        for t in range(NTO):
            tiles.append((io["x_oth"][t * 128:(t + 1) * 128, :], self.hTx, ("hTx", t // 4), t * 128, 0))
        for t in range(2):
            tiles.append((io["ctx"][t * 128:(t + 1) * 128, :], self.hTc, ("hTc", 0), t * 128, 1))
        pend = None
        for (src, dst, dkey, tok0, j) in tiles:
            i = self.hT_stageA(src)
            if pend is not None:
                self.hT_stageB(*pend)
            pend = (i, dst, dkey, tok0, j)
        self.hT_stageB(*pend)
        self.cp("pool", self.hTh[:], self.hTx[:, :, 0:1], (("hTx", 0),), ("hTh",))

    def hT_of(self, T):
        if T < 16:
            return self.hTo, ("hTo", T // 4), T * 128
        if T < 32:
            return self.hTx, ("hTx", (T - 16) // 4), (T - 16) * 128
        return self.hTc, ("hTc", 0), (T - 32) * 128

    def ab_stage(self, st, io):
        P = self.P
        sb = lambda name, shape, dt=F32: self.sb(st, name, shape, dt)
        wab = sb("wab", [128, 8, 32], BF16)
        ab = sb("ab", [128, 34, 32])
        rdtb = sb("rdtb", [128, 16])
        rneg = sb("rneg", [128, 16])
        P.dma(wab[:], io["w_ab"].rearrange("(kc p) c -> p kc c", p=128), (), ("wab",), grp="wab", eng="pool")
        P.dma(rdtb[:], io["r_dtb"], (), ("rdtb",), grp="rdtb")
        P.dma(rneg[:], io["r_alog"], (), ("rneg",), grp="rneg")
        for b0 in range(0, 34, 16):
            nb = min(16, 34 - b0)
            pa, pak = self.ps()
            for i in range(nb):
                T = b0 + i
                hT, hk, t0 = self.hT_of(T)
                for kc in range(8):
                    self.mm(pa[:, i * 32:(i + 1) * 32], hT[:, kc, t0:t0 + 128], wab[:, kc, :], kc == 0, kc == 7,
                            (hk, "wab"), (pak,))
            self.cp("act", ab[:, b0:b0 + nb, :], pa[:, 0:nb * 32].rearrange("p (t c) -> p t c", c=32), (pak,), ("ab",))
        self.act(rneg[:], rneg[:], AF.Exp, ("rneg",), ("rneg",))
        self.ts("dve", rneg[:], rneg[:], -1.0, None, ALU.mult, None, ("rneg",), ("rneg",))
        g = self.gall
        self.tt("dve", g[:], ab[:, :, 0:16], bcast_mid(rdtb[:], 34), ALU.add, ("ab", "rdtb"), ("gall",))
        self.act(g[:], g[:], AF.Exp, ("gall",), ("gall",))
        self.act(g[:], g[:], AF.Ln, ("gall",), ("gall",), bias=1.0)
        self.tt("dve", g[:], g[:], bcast_mid(rneg[:], 34), ALU.mult, ("gall", "rneg"), ("gall",))
        self.act(self.beta[:], ab[:, :, 16:32], AF.Sigmoid, ("ab",), ("beta",))
        cf = self.cf
        for d in range(2):
            rhs = g[:, :, d * 8:(d + 1) * 8]
            incl = cf[:, C_LE:C_LE + 128] if d == 0 else cf[:, C_GE:C_GE + 128]
            strict = cf[:, C_GT:C_GT + 128] if d == 0 else cf[:, C_LT:C_LT + 128]
            p1, k1 = self.ps()
            self.mm(p1[:, 0:272], incl, rhs, True, True, ("cf", "gall"), (k1,))
            self.act(self.gam[:, :, d * 8:(d + 1) * 8], p1[:, 0:272].rearrange("p (t c) -> p t c", c=8), AF.Exp,
                     (k1,), ("gam",))
            p2, k2 = self.ps()
            self.mm(p2[:, 0:272], strict, rhs, True, True, ("cf", "gall"), (k2,))
            self.act(self.eta[:, :, d * 8:(d + 1) * 8], p2[:, 0:272].rearrange("p (t c) -> p t c", c=8), AF.Exp,
                     (k2,), ("eta",))
            p3, k3 = self.ps()
            self.mm(p3[:, 0:272], incl, rhs, True, False, ("cf", "gall"), (k3,))
            self.mm(p3[:, 0:272], strict, rhs, False, True, ("cf", "gall"), (k3,))
            self.act(self.egl[:, :, d * 8:(d + 1) * 8], p3[:, 0:272].rearrange("p (t c) -> p t c", c=8), AF.Exp,
                     (k3,), ("egl",))
        self.tt("dve", self.bgam[:], self.beta[:], self.gam[:], ALU.mult, ("beta", "gam"), ("bgam",))

    def alloc_head_bufs(self, st, full):
        sb = lambda name, shape, dt=F32: self.sb(st, name, shape, dt)
        self.whs = [sb(f"wh{i}", [128, 8, 384], BF16) for i in range(2)]
        self.praw = sb("praw", [128, NOWN + 2])
        self.sqb = sb("sqb", [128, 1024], BF16)
        ntk = NOWN if full else NOWN + NCTX
        self.kTs = [sb(f"kT{i}", [128, ntk], BF16) for i in range(2)]
        if full:
            self.qTs = [sb(f"qT{i}", [128, NOWN], BF16) for i in range(2)]
        self.ktoks = [sb(f"ktok{i}", [128, ntk // 128, 128], BF16) for i in range(2)]
        self.vtoks = [sb(f"vtok{i}", [128, ntk // 128, 128], BF16) for i in range(2)]
        NS = 2
        self.gm = [sb(f"gm{t}", [128, 4, 128]) for t in range(NS)]
        self.Eb = self.gm
        self.Am = [sb(f"Am{t}", [128, 4, 128], BF16) for t in range(NS)]
        self.Vm = [sb(f"Vm{t}", [128, 4, 128], BF16) for t in range(NS)]
        self.Rm = [[sb(f"Rm{t}{i}", [128, 4, 128], BF16) for i in range(2)] for t in range(NS)]
        self.Rt = [[sb(f"Rt{t}{i}", [128, 4, 128], BF16) for i in range(2)] for t in range(NS)]
        self.Xb = [sb(f"Xb{t}", [128, 4, 256], BF16) for t in range(NS)]
        self.U = [[sb(f"U{d}{p}", [128, 4, 128]) for p in range(2)] for d in range(2)]
        self.WT = [[sb(f"WT{d}{p}", [128, 4, 128], BF16) for p in range(2)] for d in range(2)]
        self.KT = [[sb(f"KT{d}{p}", [128, 4, 128], BF16) for p in range(2)] for d in range(2)]
        if full:
            self.AT = [[sb(f"AT{d}{p}", [128, 4, 128], BF16) for p in range(2)] for d in range(2)]
        self.vnew = [sb(f"vnew{d}", [128, 128], BF16) for d in range(2)]
        self.Sb = [sb(f"Sb{d}", [128, 128], BF16) for d in range(2)]
        if full:
            self.oacc = sb("oacc", [128, NTO, 128])
            self.nsq = self.gm[0]

    def proj_seg(self, h, seg, io):
        P = self.P
        ntok = seg["ntok"]
        hT = seg["hT"]
        loff = seg["loff"]
        hp = h % 2
        wh = self.whs[hp]
        for kind in seg["kinds"]:
            ki = "qkv".index(kind)
            cb = ki * 8 + h
            wcol = lambda tap: self.fpar[:, 32 + tap * 24 + cb:32 + tap * 24 + cb + 1]
            wkeys = self.wk(f"wh{hp}", ki * 128, ki * 128 + 128)
            nblk = (ntok + 511) // 512
            for b in range(nblk):
                n = min(512, ntok - b * 512)
                bk = yield from self.acq(1)
                pp, pk = bk[0]
                for kc in range(8):
                    self.mm(pp[:, 0:n], wh[:, kc, ki * 128:(ki + 1) * 128], hT[:, kc, b * 512:b * 512 + n],
                            kc == 0, kc == 7, wkeys + (seg["hkey"](b),), (pk,))
                yield
                self.cp("act", self.praw[:, 1 + b * 512:1 + b * 512 + n], pp[:, 0:n], (pk,), ("praw",))
                self.rel(bk)
            for side, col in (("left", 0), ("right", ntok + 1)):
                src = seg[side]
                if src is None:
                    self.memset("pool", self.praw[:, col:col + 1], 0.0, ("praw",))
                else:
                    ht, hk, c = src
                    bk = yield from self.acq(1)
                    pp, pk = bk[0]
                    for kc in range(8):
                        self.mm(pp[:, 0:1], wh[:, kc, ki * 128:(ki + 1) * 128], ht[:, kc, c:c + 1],
                                kc == 0, kc == 7, wkeys + (hk,), (pk,))
                    yield
                    self.cp("act", self.praw[:, col:col + 1], pp[:, 0:1], (pk,), ("praw",))
                    self.rel(bk)
            for o in range(0, ntok, 1024):
                n = min(1024, ntok - o)
                acc = self.tmpA
                self.act(acc[:, 0:n], self.praw[:, 1 + o:1 + o + n], AF.Identity, ("praw", "fpar"), ("tmpA",), scale=wcol(1))
                yield
                self.stt(acc[:, 0:n], self.praw[:, o:o + n], wcol(0), acc[:, 0:n], ALU.mult, ALU.add,
                         ("praw", "fpar", "tmpA"), ("tmpA",))
                yield
                self.stt(acc[:, 0:n], self.praw[:, 2 + o:2 + o + n], wcol(2), acc[:, 0:n], ALU.mult, ALU.add,
                         ("praw", "fpar", "tmpA"), ("tmpA",))
                yield
                if kind == "v":
                    self.act(self.sqb[:, 0:n], acc[:, 0:n], AF.Silu, ("tmpA",), ("sqb",))
                    yield
                    src_bf = self.sqb
                    soff = 0
                else:
                    self.act(acc[:, 0:n], acc[:, 0:n], AF.Silu, ("tmpA",), ("tmpA",))
                    yield
                    self.tt("pool", self.sqb[:, 0:n], acc[:, 0:n], acc[:, 0:n], ALU.mult, ("tmpA",), ("sqb",))
                    yield
                    dstT = self.kTs[hp] if kind == "k" else self.qTs[hp]
                    dkey = ("kT", hp) if kind == "k" else ("qT", hp)
                    for c0 in range(0, n, 512):
                        m = min(512, n - c0)
                        bk = yield from self.acq(1)
                        pp, pk = bk[0]
                        self.mm(pp[:, 0:m], self.ones_bf[:], self.sqb[:, c0:c0 + m], True, True, ("ones", "sqb"), (pk,))
                        yield
                        rin = self.tmpB
                        self.act(rin[:, 0:m], pp[:, 0:m], AF.Ln, (pk, "cst"), ("tmpB",), bias=self.cst[:, 0:1])
                        self.rel(bk)
                        if kind == "q":
                            self.act(rin[:, 0:m], rin[:, 0:m], AF.Exp, ("tmpB", "cst"), ("tmpB",), scale=-0.5,
                                     bias=self.cst[:, 1:2])
                        else:
                            self.act(rin[:, 0:m], rin[:, 0:m], AF.Exp, ("tmpB",), ("tmpB",), scale=-0.5)
                        yield
                        self.tt("dve", dstT[:, loff + o + c0:loff + o + c0 + m], acc[:, c0:c0 + m], rin[:, 0:m], ALU.mult,
                                ("tmpA", "tmpB"), (dkey,))
                        yield
                    src_bf = dstT
                    soff = loff + o
                if kind in ("k", "v"):
                    dtok = self.ktoks[hp] if kind == "k" else self.vtoks[hp]
                    dk2 = ("ktok", hp) if kind == "k" else ("vtok", hp)
                    skey = "sqb" if kind == "v" else ("kT", hp)
                    nt = n // 128
                    lt0 = (loff + o) // 128
                    for t in range(nt):
                        self.tr(self.pt[:, t, :], src_bf[:, soff + t * 128:soff + (t + 1) * 128], (skey,), ("pt",))
                    yield
                    self.cp("act", dtok[:, lt0:lt0 + nt, :], self.pt[:, 0:nt, :], ("pt",), (dk2,))
                    yield

    def load_head_w(self, h, io, kinds):
        for kind in kinds:
            ki = "qkv".index(kind)
            self.load_w(self.whs[h % 2], f"wh{h % 2}", io["w_in"], ki * 1024 + h * 128, 128, ki * 128)

    def acq(self, n):
        while True:
            free = []
            for j in range(len(self.psb)):
                i = (self.psi + j) % len(self.psb)
                if i not in self.ps_live:
                    free.append(i)
            if len(free) >= n:
                take = free[:n]
                self.psi = take[-1] + 1
                for i in take:
                    self.ps_live.add(i)
                return [(self.psb[i], ("ps", i)) for i in take]
            yield

    def rel(self, banks):
        for (_, k) in banks:
            self.ps_live.discard(k[1])

    def gdn_pre(self, h, d, T0, nb, l0, full, par, ts):
        cf = self.cf
        col = d * 8 + h
        hp = h % 2
        G = self.gsfx
        N = nb * 128
        if d == 0:
            lh_nat, rm_nat, neg_nat = C_LE, C_GT, 0
            lh_tr, rm_tr, neg_tr = C_GT, C_LE, 1
        else:
            lh_nat, rm_nat, neg_nat = C_GE, C_LT, 1
            lh_tr, rm_tr, neg_tr = C_LT, C_GE, 0
        g_bc = bcast_last(self.gall[:, T0:T0 + nb, col], 128)
        beta_bc = bcast_last(self.beta[:, T0:T0 + nb, col], 128)
        bgam_bc = bcast_last(self.bgam[:, T0:T0 + nb, col], 128)
        eta_bc = bcast_last(self.eta[:, T0:T0 + nb, col], 128)
        kTi = lambda i: self.kTs[hp][:, (l0 + i) * 128:(l0 + i + 1) * 128]
        qTi = lambda i: self.qTs[hp][:, (l0 + i) * 128:(l0 + i + 1) * 128]
        idf = cf[:, C_ID:C_ID + 128]
        idb = self.id_bf
        gm, Eb_, Am, Vm, Xb = self.gm[ts], self.Eb[ts], self.Am[ts], self.Vm[ts], self.Xb[ts]
        Rm, Rt = self.Rm[ts], self.Rt[ts]
        kgm, kEb, kAm, kVm, kXb = ("gm", ts), ("gm", ts), ("Am", ts), ("Vm", ts), ("Xb", ts)
        v3 = lambda p: p[:, 0:N].rearrange("p (a b) -> p a b", b=128)
        self.tt("pool", gm[:, 0:nb, :], bcast_mid(cf[:, rm_nat:rm_nat + 128], nb), g_bc, ALU.mult, ("cf", "gall" + G), (kgm,))
        self.tt("pool", Xb[:, 0:nb, 0:128], self.vtoks[hp][:, l0:l0 + nb, :], beta_bc, ALU.mult, (("vtok", hp), "beta" + G), (kXb,))
        self.tt("pool", Xb[:, 0:nb, 128:256], self.ktoks[hp][:, l0:l0 + nb, :], bgam_bc, ALU.mult, (("ktok", hp), "bgam" + G), (kXb,))
        bk = yield from self.acq(2)
        pD, kD = bk[0]
        pG, kG = bk[1]
        self.mm(pD[:, 0:N], cf[:, lh_nat:lh_nat + 128], gm[:, 0:nb, :].rearrange("p a b -> p (a b)"), True, False,
                ("cf", kgm), (kD,))
        self.mm(pD[:, 0:N], idb[:], self.negb[:, neg_nat, 0:N], False, False, ("idbf", "negb"), (kD,))
        self.mm(pD[:, 0:N], idb[:], bcast_mid(self.negd[:], nb), False, True, ("idbf", "negd"), (kD,))
        for i in range(nb):
            self.mm(pG[:, i * 128:(i + 1) * 128], kTi(i), kTi(i), True, True, (("kT", hp),), (kG,))
        yield
        Eb = Eb_[:, 0:nb, :]
        self.act(Eb, v3(pD), AF.Exp, (kD,), (kEb,))
        yield
        self.tt("pool", Eb, Eb, beta_bc, ALU.mult, (kEb, "beta" + G), (kEb,))
        yield
        self.tt("dve", Am[:, 0:nb, :], v3(pG), Eb, ALU.mult, (kG, kEb), (kAm,))
        self.rel(bk)
        if full:
            self.spawned.append(self.gdn_at(h, d, T0, nb, l0, par, ts))
        yield
        for k in range(7):
            cur = (k + 1) % 2
            nxt = k % 2
            if k == 0:
                Rc = lambda i: idb[:]
                Rtc = lambda i: idb[:]
                rk, rtk = "idbf", "idbf"
            else:
                Rc = (lambda c: lambda i: Rm[c][:, i, :])(cur)
                Rtc = (lambda c: lambda i: Rt[c][:, i, :])(cur)
                rk, rtk = ("Rm", ts, cur), ("Rt", ts, cur)
                if k == 1:
                    Rc = lambda i: Vm[:, i, :]
                    rk = kVm
            bk = yield from self.acq(1)
            pZ, kZ = bk[0]
            for i in range(nb):
                self.mm(pZ[:, i * 128:(i + 1) * 128], Am[:, i, :], Rc(i), i == 0, False, (kAm, rk), (kZ,), skip=True)
            self.mm(pZ[:, 0:N], idb[:], bcast_mid(idb[:], nb), False, True, ("idbf",), (kZ,), skip=True)
            yield
            self.tt("dve", Vm[:, 0:nb, :], v3(pZ), bcast_mid(self.nm[:, d * 7 + k, :], nb), ALU.mult, (kZ, "nm"), (kVm,))
            self.rel(bk)
            bk = yield from self.acq(2 if k < 6 else 1)
            pR, kR = bk[0]
            if k > 0:
                for i in range(nb):
                    self.mm(pR[:, i * 128:(i + 1) * 128], Rtc(i), Vm[:, i, :], True, True, (rtk, kVm), (kR,))
            if k < 6:
                pR2, kR2 = bk[1]
                for i in range(nb):
                    self.mm(pR2[:, i * 128:(i + 1) * 128], Vm[:, i, :], Rtc(i), True, True, (kVm, rtk), (kR2,))
            yield
            if k > 0:
                self.cp("act", Rm[nxt][:, 0:nb, :], v3(pR), (kR,), (("Rm", ts, nxt),))
            if k < 6:
                self.cp("dve" if k % 2 == 0 else "act", Rt[nxt][:, 0:nb, :], v3(pR2), (kR2,), (("Rt", ts, nxt),))
            self.rel(bk)
            if k == 3:
                yield "MID"
                self.tt("pool", self.KT[d][par][:, 0:nb, :], self.ktoks[hp][:, l0:l0 + nb, :], eta_bc, ALU.mult,
                        (("ktok", hp), "eta" + G), (("KT", d, par),))
        Rf = Rm[0]
        rfk = ("Rm", ts, 0)
        bk = yield from self.acq(2)
        pU, kU = bk[0]
        for i in range(nb):
            self.mm(pU[:, i * 128:(i + 1) * 128], Rf[:, i, :], Xb[:, i, 0:128], True, True, (rfk, kXb), (kU,))
        pW, kW = bk[1]
        for i in range(nb):
            self.mm(pW[:, i * 128:(i + 1) * 128], Xb[:, i, 128:256], Rf[:, i, :], True, True, (rfk, kXb), (kW,))
        yield
        self.cp("act", self.U[d][par][:, 0:nb, :], v3(pU), (kU,), (("U", d, par),))
        self.cp("dve", self.WT[d][par][:, 0:nb, :], v3(pW), (kW,), (("WT", d, par),))
        self.rel(bk)
        yield

    def gdn_at(self, h, d, T0, nb, l0, par, ts):
        cf = self.cf
        col = d * 8 + h
        hp = h % 2
        G = self.gsfx
        N = nb * 128
        if d == 0:
            lh_tr, rm_tr, neg_tr = C_GT, C_LE, 1
        else:
            lh_tr, rm_tr, neg_tr = C_LT, C_GE, 0
        g_bc = bcast_last(self.gall[:, T0:T0 + nb, col], 128)
        idb = self.id_bf
        gm = self.gm[ts]
        kgm = ("gm", ts)
        Eb = gm[:, 0:nb, :]
        v3 = lambda p: p[:, 0:N].rearrange("p (a b) -> p a b", b=128)
        self.tt("pool", gm[:, 0:nb, :], bcast_mid(cf[:, rm_tr:rm_tr + 128], nb), g_bc, ALU.mult, ("cf", "gall" + G), (kgm,))
        bk = yield from self.acq(2)
        pD2, kD2 = bk[0]
        pQ, kQ = bk[1]
        self.mm(pD2[:, 0:N], cf[:, lh_tr:lh_tr + 128], gm[:, 0:nb, :].rearrange("p a b -> p (a b)"), True, False,
                ("cf", kgm), (kD2,))
        self.mm(pD2[:, 0:N], idb[:], self.negb[:, neg_tr, 0:N], False, True, ("idbf", "negb"), (kD2,))
        for i in range(nb):
            self.mm(pQ[:, i * 128:(i + 1) * 128], self.kTs[hp][:, (l0 + i) * 128:(l0 + i + 1) * 128],
                    self.qTs[hp][:, (l0 + i) * 128:(l0 + i + 1) * 128], True, True, (("kT", hp), ("qT", hp)), (kQ,))
        yield
        self.act(Eb, v3(pD2), AF.Exp, (kD2,), (kgm,))
        yield
        self.tt("dve", self.AT[d][par][:, 0:nb, :], v3(pQ), Eb, ALU.mult, (kQ, kgm), (("AT", d, par),))
        self.rel(bk)
        yield

    def gdn_step(self, h, d, T, i, par, l, full):
        col = d * 8 + h
        hp = h % 2
        G = self.gsfx
        S = self.Sst[:, col, :]
        sk = ("S", col)
        Sb = self.Sb[d]
        sbk = ("Sb", d)
        vn = self.vnew[d]
        vk = ("vnew", d)
        bk = yield from self.acq(1)
        pw, kw = bk[0]
        self.mm(pw[:, 0:128], self.WT[d][par][:, i, :], Sb[:], True, True, (("WT", d, par), sbk), (kw,))
        yield
        self.tt("dve", vn[:], self.U[d][par][:, i, :], pw[:, 0:128], ALU.subtract, (("U", d, par), kw), (vk,))
        self.rel(bk)
        bk = yield from self.acq(2 if full else 1)
        pS, kS = bk[0]
        self.mm(pS[:, 0:128], self.KT[d][par][:, i, :], vn[:], True, True, (("KT", d, par), vk), (kS,))
        if full:
            po, ko = bk[1]
            self.mm(po[:, 0:128], self.qTs[hp][:, l * 128:(l + 1) * 128], Sb[:], True, True, (("qT", hp), sbk), (ko,))
            self.mm(po[:, 128:256], self.AT[d][par][:, i, :], vn[:], True, True, (("AT", d, par), vk), (ko,))
        yield
        self.stt(S, S, self.egl[:, T, col:col + 1], pS[:, 0:128], ALU.mult, ALU.add, (sk, "egl" + G, kS), (sk,))
        yield
        self.cp("act", Sb[:], S, (sk,), (sbk,))
        if full:
            ok = ("oacc", l)
            self.stt(self.oacc[:, l, :], po[:, 0:128], self.gam[:, T, col:col + 1], self.oacc[:, l, :], ALU.mult, ALU.add,
                     (ko, "gam" + G, ok), (ok,))
            self.tt("dve", self.oacc[:, l, :], po[:, 128:256], self.oacc[:, l, :], ALU.add, (ko, ok), (ok,))
        self.rel(bk)
        yield

    def chain_batches(self, d, T_lo, n):
        out = []
        if d == 0:
            t = T_lo
            while t < T_lo + n:
                nb = min(4, T_lo + n - t)
                out.append((t, nb, list(range(nb))))
                t += nb
        else:
            t = T_lo + n
            while t > T_lo:
                nb = min(4, t - T_lo)
                out.append((t - nb, nb, list(range(nb - 1, -1, -1))))
                t -= nb
        return out

    def bg_step(self):
        if self.bg is not None:
            try:
                next(self.bg)
            except StopIteration:
                self.bg = None

    def bg_drain(self):
        while self.bg is not None:
            self.bg_step()

    def run_round(self, gens):
        gens = list(gens)
        while gens:
            alive = []
            for g_ in gens:
                try:
                    next(g_)
                    alive.append(g_)
                except StopIteration:
                    pass
            gens = alive + self.spawned
            self.spawned = []
            self.bg_step()

    def run_staggered(self, h, chain, full, extra=None):
        d, T_lo, n, lbase = chain
        bat = [(T0, nb, order, T0 - lbase) for (T0, nb, order) in self.chain_batches(d, T_lo, n)]
        nbt = len(bat)
        pre = {}

        def start(bi):
            T0, nb, order, l0 = bat[bi]
            pre[bi] = self.gdn_pre(h, d, T0, nb, l0, full, bi % 2, bi % 2)

        for r in range(-1, nbt):
            entries = []
            if r + 1 < nbt:
                if r + 1 not in pre:
                    start(r + 1)
                entries.append([pre[r + 1], False])
            if r + 2 < nbt:
                start(r + 2)
                entries.append([pre[r + 2], True])
            if r >= 0:
                T0, nb, order, l0 = bat[r]
                entries.append([self.scan_batch(h, d, T0, nb, order, l0, r % 2, full), False])
            if extra is not None:
                for mk in extra(r, nbt):
                    entries.append([mk, False])
            while entries:
                alive = []
                for e in entries:
                    try:
                        v = next(e[0])
                        if v == "MID" and e[1]:
                            continue
                        alive.append(e)
                    except StopIteration:
                        pass
                entries = alive
                self.bg_step()

    def scan_batch(self, h, d, T0, nb, order, l0, par, full):
        for i in order:
            yield from self.gdn_step(h, d, T0 + i, i, par, l0 + i, full)

    def run_chains(self, h, chains, full):
        sched = []
        for ci, (d, T_lo, n, lbase) in enumerate(chains):
            sched.append([(d, T0, nb, order, T0 - lbase, ci) for (T0, nb, order) in self.chain_batches(d, T_lo, n)])
        nround = max(len(s) for s in sched)
        for r in range(-1, nround):
            gens = []
            for s in sched:
                if r + 1 < len(s):
                    d, T0, nb, order, l0, ci = s[r + 1]
                    gens.append(self.gdn_pre(h, d, T0, nb, l0, full, (r + 1) % 2, ci))
            for s in sched:
                if 0 <= r < len(s):
                    d, T0, nb, order, l0, ci = s[r]
                    gens.append(self.scan_batch(h, d, T0, nb, order, l0, r % 2, full))
            self.run_round(gens)

    def set_state(self, h, d):
        col = d * 8 + h
        self.cp("act", self.Sb[d][:], self.Sst[:, col, :], (("S", col), "Sst"), (("Sb", d),))

    def proj_pass1(self, h, io):
        seg = dict(ntok=NCTX, hT=self.hTc, hkey=lambda b: ("hTc", 0), left=None, right=None, loff=NOWN, kinds="kv")
        yield from self.proj_seg(h, seg, io)
        seg = dict(ntok=NOWN, hT=self.hTx, hkey=lambda b: ("hTx", b), left=(self.hTo, ("hTo", 3), NOWN - 1), right=None,
                   loff=0, kinds="kv")
        yield from self.proj_seg(h, seg, io)
        if h + 1 < 8:
            self.load_head_w(h + 1, io, "kv")

    def head_pass1(self, h, io):
        if h == 0:
            self.load_head_w(0, io, "kv")
            self.bg = self.proj_pass1(0, io)
        self.bg_drain()
        self.bg = self.proj_pass1(h + 1, io) if h + 1 < 8 else None
        hp = h % 2
        if h == 0:
            self.dump("p1_kT", self.kTs[hp][:], (("kT", hp),))
            self.dump("p1_ktok", self.ktoks[hp][:], (("ktok", hp),))
            self.dump("p1_vtok", self.vtoks[hp][:], (("vtok", hp),))
        self.set_state(h, 0)
        self.set_state(h, 1)
        def extra(r, nbt):
            if r == nbt - 2:
                return [self.gdn_pre(h, 0, 32, 2, 16, False, 0, nbt % 2)]
            if r == nbt - 1:
                return [self.scan_batch(h, 0, 32, 2, [0, 1], 16, 0, False)]
            return []
        self.run_staggered(h, (1, 16, 18, 16), False, extra)

    def proj_pass2(self, h, io):
        seg = dict(ntok=NOWN, hT=self.hTo, hkey=lambda b: ("hTo", b), left=None, right=(self.hTh, "hTh", 0),
                   loff=0, kinds="qkv")
        yield from self.proj_seg(h, seg, io)
        if h + 1 < 8:
            self.load_head_w(h + 1, io, "qkv")

    def pass2_all(self, io):
        chains = [(0, 0, 16, 0), (1, 0, 16, 0)]
        sched = []
        for ci, (d, T_lo, n, lbase) in enumerate(chains):
            sched.append([(d, T0, nb, order, T0 - lbase, ci) for (T0, nb, order) in self.chain_batches(d, T_lo, n)])
        nround = len(sched[0])

        def pre_gens(h, bi):
            return [self.gdn_pre(h, d, T0, nb, l0, True, bi % 2, ci) for (d, T0, nb, order, l0, ci) in
                    (s_[bi] for s_ in sched)]

        def scan_gens(h, bi):
            return [self.scan_batch(h, d, T0, nb, order, l0, bi % 2, True) for (d, T0, nb, order, l0, ci) in
                    (s_[bi] for s_ in sched)]

        self.load_head_w(0, io, "qkv")
        self.bg = self.proj_pass2(0, io)
        self.bg_drain()
        self.bg = self.proj_pass2(1, io)
        self.run_round(pre_gens(0, 0))
        for h in range(8):
            hp = h % 2
            if h == 0:
                self.dump("p2_qT", self.qTs[hp][:], (("qT", hp),))
                self.dump("p2_kT", self.kTs[hp][:], (("kT", hp),))
                self.dump("p2_vtok", self.vtoks[hp][:], (("vtok", hp),))
            self.memset("pool", self.oacc[:], 0.0, tuple(("oacc", l) for l in range(NTO)))
            self.set_state(h, 0)
            self.set_state(h, 1)
            for r in range(nround):
                gens = []
                if r + 1 < nround:
                    gens += pre_gens(h, r + 1)
                elif h + 1 < 8:
                    self.bg_drain()
                    gens += pre_gens(h + 1, 0)
                gens += scan_gens(h, r)
                self.run_round(gens)
            self.bg = self.proj_pass2(h + 2, io) if h + 2 < 8 else None
            if h == 0:
                self.dump("p2_oacc", self.oacc[:], tuple(("oacc", l) for l in range(NTO)))
            self.head_norm(h)

    def head_norm(self, h):
        okeys = tuple(("oacc", l) for l in range(NTO))
        ssq = self.sml[:, 8:24]
        sq = self.nsq[:]
        for qt in range(4):
            o2 = self.oacc[:, qt * 4:(qt + 1) * 4, :]
            self.tt("dve", sq, o2, o2, ALU.mult, okeys, (("gm", 0),))
            self.P.op("dve", (lambda sq, qt: lambda e: e.tensor_reduce(
                out=self.sml[:, 8 + qt * 4:12 + qt * 4], in_=sq, axis=AX.X, op=ALU.add))(sq, qt),
                (("gm", 0),), (("sml", "ssq"),))
        self.rsqrt_small(ssq, ssq, 1.0 / 128, (("sml", "ssq"),), (("sml", "ssq"),))
        self.tt("dve", self.oacc[:], self.oacc[:], bcast_last(ssq, 128), ALU.mult, okeys + (("sml", "ssq"),), okeys)
        self.tt("pool", self.on_store[:, :, h * 128:(h + 1) * 128], self.oacc[:], bcast_mid(self.gon[:], NTO), ALU.mult,
                okeys + ("gon",), ("on_store",))

    def phase3a(self, io):
        nc, P = self.nc, self.P
        Bg = self.Bg
        hTo = self.hTo
        with ExitStack() as s:
            sb = lambda name, shape, dt=F32: self.sb(s, name, shape, dt)
            w_zb = sb("w_zb", [128, 8, 1024], BF16)
            w_gb = sb("w_gb", [128, 8, 1024], BF16)
            w_pb = sb("w_pb", [128, 8, 1024], BF16)
            ybb = sb("ybb", [128, 1024], BF16)
            ybT = sb("ybT", [128, 8, 512], BF16)
            self.load_w(w_zb, "w_zb", io["w_in"], OFF_ZB, 1024)
            self.load_w(w_gb, "w_gb", io["w_in"], OFF_G + 1024, 1024)
            self.load_w(w_pb, "w_pb", io["w_pb"], 0, 1024)
            ybb2 = sb("ybb2", [128, 1024], BF16)
            gbuf = sb("gbuf", [128, 8, 512])
            ybbs = [ybb, ybb2]
            souts = [self.tmpA, self.tmpB]
            def ZB(t):
                blk, tt_ = divmod(t, 4)
                so, yb = souts[t % 2], ybbs[t % 2]
                sok, ybk = ("so", t % 2), ("ybb", t % 2)
                for half in range(2):
                    pz, kz = self.ps()
                    for kc in range(8):
                        self.mm(pz[:], hTo[:, kc, t * 128:(t + 1) * 128], w_zb[:, kc, half * 512:(half + 1) * 512],
                                kc == 0, kc == 7, (("hTo", blk),) + self.wk("w_zb", half * 512, half * 512 + 512), (kz,))
                    self.act(so[:, half * 512:(half + 1) * 512], pz[:], AF.Silu, (kz,), (sok,))
                self.tt("pool", yb[:], self.on_store[:, t, :], so[:], ALU.mult, ("on_store", sok), (ybk,))

            def MIDP(tp):
                bp, tq = divmod(tp, 4)
                yb, ybk = ybbs[tp % 2], ("ybb", tp % 2)
                for kc in range(8):
                    self.tr(self.pt[:, kc, :], yb[:, kc * 128:(kc + 1) * 128], (ybk,), ("pt",))
                self.cp("dve", ybT[:, :, tq * 128:(tq + 1) * 128], self.pt[:], ("pt",), ("ybT",))
                if tq == 3:
                    for mb in range(8):
                        pB, kB = self.ps()
                        for kc in range(8):
                            self.mm(pB[:], w_pb[:, kc, mb * 128:(mb + 1) * 128], ybT[:, kc, :],
                                    kc == 0, kc == 7, ("ybT",) + self.wk("w_pb", mb * 128, mb * 128 + 128), (kB,))
                        self.tt("dve", Bg[:, mb, bp * 512:(bp + 1) * 512], pB[:], gbuf[:, mb, :], ALU.mult,
                                (kB, ("gbuf", mb)), (("Bg", bp),))

            def GATES(t):
                blk, tt_ = divmod(t, 4)
                for mb in (2 * tt_, 2 * tt_ + 1):
                    pg, kg = self.ps()
                    for kc in range(8):
                        self.mm(pg[:], w_gb[:, kc, mb * 128:(mb + 1) * 128], hTo[:, kc, blk * 512:(blk + 1) * 512],
                                kc == 0, kc == 7, (("hTo", blk),) + self.wk("w_gb", mb * 128, mb * 128 + 128), (kg,))
                    self.act(gbuf[:, mb, :], pg[:], AF.Sigmoid, (kg,), (("gbuf", mb),))

            for t in range(17):
                odd = (t % 4) % 2 == 1
                if t < 16 and odd:
                    GATES(t)
                if t < 16:
                    ZB(t)
                if t >= 1:
                    MIDP(t - 1)
                if t < 16 and not odd:
                    GATES(t)

    def phase3b(self, st, io):
        nc, P = self.nc, self.P
        Bg = self.Bg
        hTo = self.hTo
        yaT = self.sb(st, "yaT", [128, 8, NOWN], BF16)
        with ExitStack() as s:
            sb = lambda name, shape, dt=F32: self.sb(s, name, shape, dt)
            w_ua = sb("w_ua", [128, 8, 1024], BF16)
            w_va = sb("w_va", [128, 8, 1024], BF16)
            w_za = sb("w_za", [128, 8, 1024], BF16)
            wsp = sb("wsp", [128, 8, 128], BF16)
            rlg = sb("rlg", [128, 1024])
            rlb = sb("rlb", [128, 1024])
            rbs = sb("rbs", [128, 1024])
            uz = sb("uz", [128, 8, 512], BF16)
            gvs = [sb(f"gv{i}", [128, 1024]) for i in range(3)]
            vlns = [sb(f"vln{i}", [128, 1024], BF16) for i in range(3)]
            self.load_w(w_va, "w_va", io["w_in"], OFF_VA, 1024)
            self.load_w(w_ua, "w_ua", io["w_in"], OFF_UA, 1024)
            self.load_w(w_za, "w_za", io["w_in"], OFF_ZA, 1024)
            P.dma(gvs[1][:], io["w_spT"], (), (("gv", 1),), grp="wspst")
            self.cp("dve", wsp[:].rearrange("p a b -> p (a b)"), gvs[1][:], (("gv", 1),), ("wsp",))
            P.dma(rlg[:], io["r_lng"], (), ("rlg",), grp="rlg")
            P.dma(rlb[:], io["r_lnb"], (), ("rlb",), grp="rlb")
            P.dma(rbs[:], io["r_bsp"], (), ("rbs",), grp="rbs")
            sm = self.sml
            tu = self.tmpA[:, 0:512]
            tz = self.tmpB[:, 0:512]
            stmp = self.tmpA[:, 512:1024]

            def U(blk):
                hk = ("hTo", blk)
                for cbk in range(8):
                    pu, ku = self.ps()
                    for kc in range(8):
                        self.mm(pu[:], w_ua[:, kc, cbk * 128:(cbk + 1) * 128], hTo[:, kc, blk * 512:(blk + 1) * 512],
                                kc == 0, kc == 7, (hk,) + self.wk("w_ua", cbk * 128, cbk * 128 + 128), (ku,))
                    pz, kz = self.ps()
                    for kc in range(8):
                        self.mm(pz[:], w_za[:, kc, cbk * 128:(cbk + 1) * 128], hTo[:, kc, blk * 512:(blk + 1) * 512],
                                kc == 0, kc == 7, (hk,) + self.wk("w_za", cbk * 128, cbk * 128 + 128), (kz,))
                    if cbk % 2 == 0:
                        self.act(tu, pu[:], AF.Gelu_apprx_tanh, (ku,), ("tu",))
                        self.act(tz, pz[:], AF.Silu, (kz,), ("tz",))
                    else:
                        self.act(tz, pz[:], AF.Silu, (kz,), ("tz",))
                        self.act(tu, pu[:], AF.Gelu_apprx_tanh, (ku,), ("tu",))
                    self.tt("pool", uz[:, cbk, :], tu, tz, ALU.mult, ("tu", "tz"), ("uz",))

            def V(t):
                blk = t // 4
                hk = ("hTo", blk)
                q = t % 3
                gv, vln = gvs[q], vlns[q]
                gk, vk = ("gv", q), ("vln", q)
                c0 = 24 + 8 * q
                K = lambda j: ("sml", c0 + j)
                C = lambda j: sm[:, c0 + j:c0 + j + 1]
                for half in range(2):
                    pv, kv = self.ps()
                    for kc in range(8):
                        self.mm(pv[:], hTo[:, kc, t * 128:(t + 1) * 128], w_va[:, kc, half * 512:(half + 1) * 512],
                                kc == 0, kc == 7, (hk,) + self.wk("w_va", half * 512, half * 512 + 512), (kv,))
                    self.act(gv[:, half * 512:(half + 1) * 512], pv[:], AF.Gelu_apprx_tanh, (kv,), (gk, K(half)), accum=C(half))
                self.act(yaT[:, :, t * 128:(t + 1) * 128], gv[:].rearrange("p (a b) -> p a b", b=128), AF.Square,
                         (gk,), (("yaT", blk), K(2)), accum=C(2))
                self.tt("dve", C(3), C(0), C(1), ALU.add, (K(0), K(1)), (K(3),))
                self.ts("dve", C(3), C(3), 1.0 / 1024, None, ALU.mult, None, (K(3),), (K(3),))
                self.tt("dve", C(5), C(3), C(3), ALU.mult, (K(3),), (K(5),))
                self.ts("dve", C(4), C(2), 1.0 / 1024, None, ALU.mult, None, (K(2),), (K(4),))
                self.tt("dve", C(4), C(4), C(5), ALU.subtract, (K(4), K(5)), (K(4),))
                self.rsqrt_small(C(4), C(4), 1.0, (K(4),), (K(4),))
                self.ts("dve", gv[:], gv[:], C(3), C(4), ALU.subtract, ALU.mult, (gk, K(3), K(4)), (gk,))
                self.tt("pool", gv[:], gv[:], rlg[:], ALU.mult, (gk, "rlg"), (gk,))
                self.tt("pool", vln[:], gv[:], rlb[:], ALU.add, (gk, "rlb"), (vk,))

            def S(t):
                blk, tt_ = divmod(t, 4)
                q = t % 3
                vln, vk = vlns[q], ("vln", q)
                for gh in range(2):
                    pS, kS = self.ps()
                    for gi in range(4):
                        g_ = gh * 4 + gi
                        self.mm(pS[:, gi * 128:(gi + 1) * 128], vln[:, g_ * 128:(g_ + 1) * 128], wsp[:, g_, :], True, True,
                                (vk, "wsp"), (kS,))
                    self.tt("dve", stmp, pS[:], rbs[:, gh * 512:(gh + 1) * 512], ALU.add, (kS, "rbs"), ("stmp",))
                    self.tt("pool", yaT[:, gh * 4:(gh + 1) * 4, t * 128:(t + 1) * 128],
                            stmp.rearrange("p (a b) -> p a b", b=128),
                            uz[:, gh * 4:(gh + 1) * 4, tt_ * 128:(tt_ + 1) * 128], ALU.mult, ("stmp", "uz"), (("yaT", blk),))

            V(0)
            V(1)
            for blk in range(4):
                U(blk)
                for tt_ in range(4):
                    t = blk * 4 + tt_
                    if t + 2 < 16:
                        V(t + 2)
                    S(t)
        P.barrier()
        self.dump("yaT", yaT[:], tuple(("yaT", b) for b in range(4)))
        with ExitStack() as s:
            sb = lambda name, shape, dt=F32: self.sb(s, name, shape, dt)
            w_ga = sb("w_ga", [128, 8, 1024], BF16)
            w_pa = sb("w_pa", [128, 8, 1024], BF16)
            w_out = sb("w_out", [128, 8, 1024], BF16)
            mT = sb("mT", [128, 8, 512], BF16)
            ob = [sb(f"ob{i}", [128, 1024]) for i in range(2)]
            self.alloc_x(s)
            self.load_w(w_ga, "w_ga", io["w_in"], OFF_G, 1024)
            self.load_w(w_pa, "w_pa", io["w_pa"], 0, 1024)
            self.load_w(w_out, "w_out", io["w_out"], 0, 1024)
            sm = self.sml
            for blk in range(4):
                hk = ("hTo", blk)
                for mb in range(8):
                    pg, kg = self.ps()
                    for kc in range(8):
                        self.mm(pg[:], w_ga[:, kc, mb * 128:(mb + 1) * 128], hTo[:, kc, blk * 512:(blk + 1) * 512],
                                kc == 0, kc == 7, (hk,) + self.wk("w_ga", mb * 128, mb * 128 + 128), (kg,))
                    self.act(self.tmpB[:, 0:512], pg[:], AF.Sigmoid, (kg,), ("tmpB",))
                    pA, kA = self.ps()
                    for kc in range(8):
                        self.mm(pA[:], w_pa[:, kc, mb * 128:(mb + 1) * 128], yaT[:, kc, blk * 512:(blk + 1) * 512],
                                kc == 0, kc == 7, (("yaT", blk),) + self.wk("w_pa", mb * 128, mb * 128 + 128), (kA,))
                    self.tt("dve", self.tmpB[:, 512:1024], pA[:], self.tmpB[:, 0:512], ALU.mult, (kA, "tmpB"), ("tmpB2",))
                    self.tt("pool", mT[:, mb, :], self.tmpB[:, 512:1024], Bg[:, mb, blk * 512:(blk + 1) * 512], ALU.add,
                            ("tmpB2", ("Bg", blk)), ("mT",))
                for tt_ in range(4):
                    t = blk * 4 + tt_
                    i = self.xi % self.nx
                    self.xi += 1
                    xs = self.xst[i]
                    P.dma(xs[:], io["x_own"][t * 128:(t + 1) * 128, :], (), (("xst", i),), grp=f"xst{i}")
                    pos = []
                    for half in range(2):
                        po, ko = self.ps()
                        for kc in range(8):
                            self.mm(po[:], mT[:, kc, tt_ * 128:(tt_ + 1) * 128], w_out[:, kc, half * 512:(half + 1) * 512],
                                    kc == 0, kc == 7, ("mT",) + self.wk("w_out", half * 512, half * 512 + 512), (ko,))
                        self.act(self.tmpA[:, half * 512:(half + 1) * 512], po[:], AF.Square, (ko,),
                                 ("tmpA", ("sml", 32 + half)), accum=sm[:, 32 + half:33 + half])
                        pos.append((po, ko))
                    self.tt("dve", sm[:, 34:35], sm[:, 32:33], sm[:, 33:34], ALU.add, (("sml", 32), ("sml", 33)), (("sml", 34),))
                    self.rsqrt_small(sm[:, 34:35], sm[:, 34:35], 1.0 / 1024, (("sml", 34),), (("sml", 34),))
                    o_ = ob[t % 2]
                    okey = ("ob", t % 2)
                    for half in range(2):
                        po, ko = pos[half]
                        self.stt(o_[:, half * 512:(half + 1) * 512], po[:], sm[:, 34:35],
                                 self.gate_bc[:, half * 512:(half + 1) * 512], ALU.mult, ALU.mult,
                                 (ko, ("sml", 34), "gate_bc"), (okey,))
                    self.tt("pool", o_[:], o_[:], xs[:], ALU.add, (okey, ("xst", i)), (okey,))
                    P.dma(io["y"][t * 128:(t + 1) * 128, :], o_[:], (okey,), (), grp=f"yout{t % 2}")


IN_SPECS = [
    ("x_own", [NOWN, D]), ("x_oth", [NOWN, D]), ("ctx", [NCTX, D]), ("cvec", [128, 16]),
    ("w_mod", [D, 3 * D]), ("w_in", [D, 9248]), ("w_ab", [D, 32]),
    ("w_pa", [D, D]), ("w_pb", [D, D]), ("w_out", [D, D]), ("w_spT", [128, 1024]),
    ("fparams", [128, 104]), ("r_dtb", [128, 16]), ("r_alog", [128, 16]),
    ("r_gpost", [128, D]), ("r_bmodg", [128, D]), ("r_lng", [128, D]), ("r_lnb", [128, D]),
    ("r_bsp", [128, D]), ("r_gon", [128, 128]), ("consts", [128, NCONST]),
]


def _program(nc, plan, debug=None):
    io = {}
    for name, shape in IN_SPECS:
        io[name] = nc.dram_tensor(name, shape, F32, kind="ExternalInput").ap()
    io["y"] = nc.dram_tensor("y", [NOWN, D], F32, kind="ExternalOutput").ap()
    P = Prog(nc, plan)
    with ExitStack() as st:
        if plan is not None:
            P.setup_sems(st)
        Builder(nc, P, debug).build(io)
    return P


def build_nc(debug=None):
    nc0 = bass.Bass("TRN2", target_bir_lowering=False)
    P0 = _program(nc0, None, debug)
    plan = P0.analyze()
    nc = bass.Bass("TRN2", target_bir_lowering=False)
    P1 = _program(nc, plan, debug)
    assert P1.idx == len(plan["ops"])
    return nc, plan


def make_consts():
    c = np.zeros((128, NCONST), np.float32)
    i = np.arange(128)
    p, f = i[:, None], i[None, :]
    c[:, C_ID:C_ID + 128] = (p == f)
    c[:, C_LE:C_LE + 128] = (p <= f)
    c[:, C_GT:C_GT + 128] = (p > f)
    c[:, C_GE:C_GE + 128] = (p >= f)
    c[:, C_LT:C_LT + 128] = (p < f)
    c[:, C_NEGU:C_NEGU + 512] = np.tile(np.where(p < f, NEGV, 0.0), (1, 4))
    c[:, C_NEGL:C_NEGL + 512] = np.tile(np.where(p > f, NEGV, 0.0), (1, 4))
    for k in range(7):
        b = 1 << k
        blk = i // (2 * b)
        half = (i // b) % 2
        same = blk[:, None] == blk[None, :]
        mF = same & (half[:, None] == 0) & (half[None, :] == 1)
        mB = same & (half[:, None] == 1) & (half[None, :] == 0)
        c[:, C_NM + k * 128:C_NM + (k + 1) * 128] = -mF.astype(np.float32) + (p == f)
        c[:, C_NM + (7 + k) * 128:C_NM + (8 + k) * 128] = -mB.astype(np.float32) + (p == f)
    return c


def fm(v, n):
    return np.ascontiguousarray(np.asarray(v, np.float32).reshape(n, 128).T)


def rows(v):
    v = np.asarray(v, np.float32).reshape(1, -1)
    return np.ascontiguousarray(np.broadcast_to(v, (128, v.shape[1])))


def make_in_maps(x, c, ctx, c_ctx, w_mod, b_mod, g_pre, g_post, w_in, w_conv, a_log, dt_bias,
                 g_onorm, gm_ln_g, gm_ln_b, w_sp, b_sp, w_pa, w_pb, w_out):
    f32 = lambda a: np.ascontiguousarray(np.asarray(a, np.float32))
    w_mod0, b_mod0, w_in0 = f32(w_mod[0]), f32(b_mod[0]), f32(w_in[0])
    w_pa0, w_pb0, w_out0 = f32(w_pa[0]), f32(w_pb[0]), f32(w_out[0])
    consts = make_consts()
    maps = []
    for r in range(8):
        b, half = r // 2, r % 2
        rev = half == 1
        xs = np.asarray(x[b], np.float32)
        cx = np.asarray(ctx[b], np.float32)
        if rev:
            xs = xs[::-1]
            cx = cx[::-1]
        wc = np.asarray(w_conv[0], np.float32)
        al = np.asarray(a_log[0], np.float32)
        db = np.asarray(dt_bias[0], np.float32)
        wsp = np.asarray(w_sp[0], np.float32)
        bsp = np.asarray(b_sp[0], np.float32)
        wab = w_in0[:, OFF_A:OFF_ZB]
        if rev:
            wc = wc[::-1]
            al = al[::-1]
            db = db[::-1]
            wsp = wsp[:, ::-1, ::-1]
            bsp = bsp[:, ::-1]
            wab = wab[:, [8, 9, 10, 11, 12, 13, 14, 15, 0, 1, 2, 3, 4, 5, 6, 7,
                          24, 25, 26, 27, 28, 29, 30, 31, 16, 17, 18, 19, 20, 21, 22, 23]]
        cvec = np.zeros((128, 16), np.float32)
        cb_fm = fm(c[b], 8)
        cc_fm = fm(c_ctx, 8)
        cvec[:, 0::2] = cb_fm
        cvec[:, 1::2] = cc_fm
        fpar = np.zeros((128, 104), np.float32)
        fpar[:, 0:8] = fm(g_pre[0], 8)
        fpar[:, 8:32] = fm(b_mod0, 24)
        for tap in range(3):
            fpar[:, 32 + tap * 24:32 + (tap + 1) * 24] = fm(wc[tap], 24)
        m = {
            "x_own": f32(xs[:NOWN]), "x_oth": f32(xs[NOWN:]), "ctx": f32(cx), "cvec": cvec,
            "w_mod": w_mod0, "w_in": w_in0, "w_ab": f32(wab), "w_pa": w_pa0, "w_pb": w_pb0, "w_out": w_out0,
            "w_spT": f32(np.transpose(wsp, (2, 0, 1)).reshape(128, 1024)),
            "fparams": fpar, "r_dtb": rows(db.reshape(-1)), "r_alog": rows(al.reshape(-1)),
            "r_gpost": rows(g_post[0]), "r_bmodg": rows(b_mod0[2048:]), "r_lng": rows(gm_ln_g[0]),
            "r_lnb": rows(gm_ln_b[0]), "r_bsp": rows(bsp.reshape(-1)), "r_gon": rows(g_onorm[0]),
            "consts": consts,
        }
        maps.append(m)
    return maps


_CACHE = {}


def kernel(**inputs):
    if "nc" not in _CACHE:
        _CACHE["nc"] = build_nc()[0]
    nc = _CACHE["nc"]
    maps = make_in_maps(**inputs)
    res = run_bass_kernel_spmd(nc, maps, core_ids=list(range(8)))
    out = np.empty((4, 4096, D), np.float32)
    for r in range(8):
        b, half = r // 2, r % 2
        y = np.asarray(res.results[r]["y"], np.float32)
        if half == 0:
            out[b, :NOWN] = y
        else:
            out[b, ::-1][:NOWN] = y
    return out
```

```python
from contextlib import ExitStack
import numpy as np
import concourse.bass as bass
import concourse.mybir as mybir
from concourse.bass_utils import run_bass_kernel_spmd

F32 = mybir.dt.float32
BF16 = mybir.dt.bfloat16
AF = mybir.ActivationFunctionType
ALU = mybir.AluOpType
AX = mybir.AxisListType

D = 1024
NOWN = 2048
NTO = 16
NCTX = 256
EPS = 1e-6
OFF_A = 3072
OFF_ZB = 3104
OFF_UA = 4128
OFF_VA = 5152
OFF_ZA = 6176
OFF_G = 7200
EPOCH = 8000
NEGV = -30000.0

C_ID = 0
C_LE = 128
C_GT = 256
C_GE = 384
C_LT = 512
C_NEGU = 640
C_NEGL = 1152
C_NM = 1664
NCONST = C_NM + 14 * 128


class _Op:
    __slots__ = ("eng", "dma", "grp", "gseq", "deps", "sig", "signo", "waits")

    def __init__(self, eng, dma, grp):
        self.eng = eng
        self.dma = dma
        self.grp = grp
        self.gseq = 0
        self.deps = []
        self.sig = False
        self.signo = 0
        self.waits = []


class Prog:
    ENGS = ("pe", "dve", "act", "pool", "sp")

    def __init__(self, nc, plan=None):
        self.nc = nc
        self.plan = plan
        self.dry = plan is None
        self.ops = []
        self.last_w = {}
        self.readers = {}
        self.shared_w = {}
        self.grp_count = {}
        self.idx = 0
        self.last_on = {}
        self.last_dma = {}
        self.pending = {}
        self.serial_dep = None
        self.engs = {"pe": nc.tensor, "dve": nc.vector, "act": nc.scalar,
                     "pool": nc.gpsimd, "sp": nc.sync}

    def _record(self, eng, reads, writes, dma, grp):
        op = _Op(eng, dma, grp)
        oid = len(self.ops)
        deps = {}
        for k in reads:
            w = self.last_w.get(k)
            if w is not None:
                deps[w] = True
            for w in self.shared_w.get(k, ()):
                deps[w] = True
        for k in writes:
            if isinstance(k, tuple) and len(k) == 2 and k[0] == "~":
                kk = k[1]
                w = self.last_w.get(kk)
                if w is not None and w not in deps:
                    deps[w] = False
                for r in self.readers.get(kk, ()):
                    if r not in deps:
                        deps[r] = False
                continue
            w = self.last_w.get(k)
            if w is not None and w not in deps:
                deps[w] = False
            for w in self.shared_w.get(k, ()):
                if w not in deps:
                    deps[w] = False
            for r in self.readers.get(k, ()):
                if r not in deps:
                    deps[r] = False
        for k in reads:
            self.readers.setdefault(k, []).append(oid)
        for k in writes:
            if isinstance(k, tuple) and len(k) == 2 and k[0] == "~":
                self.shared_w.setdefault(k[1], []).append(oid)
                continue
            self.last_w[k] = oid
            self.readers[k] = []
            self.shared_w[k] = []
        if eng in self.pending:
            for did in self.pending.pop(eng):
                deps.setdefault(did, True)
        if self.serial_dep is not None:
            deps.setdefault(self.serial_dep, True)
            self.serial_dep = None
        op.deps = list(deps.items())
        if dma:
            self.grp_count[grp] = self.grp_count.get(grp, 0) + 1
            op.gseq = self.grp_count[grp]
            self.last_dma[grp] = oid
        else:
            self.last_on[eng] = oid
        self.ops.append(op)

    def barrier(self):
        if self.dry:
            snap = list(self.last_on.values()) + list(self.last_dma.values())
            self.pending = {e: list(snap) for e in self.ENGS}

    def analyze(self):
        ops = self.ops

        def need(op, d, is_raw):
            if d.eng != op.eng:
                return True
            if op.eng == "pe":
                return False
            return True

        for op in ops:
            best = {}
            for (did, is_raw) in op.deps:
                d = ops[did]
                if not d.dma and need(op, d, is_raw):
                    if best.get(d.eng, -1) < did:
                        best[d.eng] = did
            for did in best.values():
                ops[did].sig = True
        cnt = {e: 0 for e in self.ENGS}
        for op in ops:
            if op.sig and not op.dma:
                cnt[op.eng] += 1
                op.signo = cnt[op.eng]
        waited = {e: {} for e in self.ENGS}
        for op in ops:
            wl = waited[op.eng]
            ne, ng = {}, {}
            for (did, is_raw) in op.deps:
                d = ops[did]
                if d.dma:
                    ng[d.grp] = max(ng.get(d.grp, 0), d.gseq)
                elif need(op, d, is_raw) and d.sig:
                    ne[d.eng] = max(ne.get(d.eng, 0), d.signo)
            for g, s in ng.items():
                if wl.get(("g", g), 0) < s:
                    op.waits.append(("g", g, s))
                    wl[("g", g)] = s
            for e, s in ne.items():
                if wl.get(e, 0) < s:
                    op.waits.append(("e", e, s))
                    wl[e] = s
        return {"ops": ops, "cnt": cnt, "grp": dict(self.grp_count)}

    def setup_sems(self, stack):
        nc = self.nc
        self.sems = {}
        for e, c in self.plan["cnt"].items():
            n = max(1, (c + EPOCH - 1) // EPOCH)
            self.sems[e] = [stack.enter_context(nc.semaphore(f"s_{e}_{i}")) for i in range(n)]
        self.gsem = {}
        for i, g in enumerate(self.plan["grp"]):
            self.gsem[g] = stack.enter_context(nc.semaphore(f"g{i}"))

    def _emit(self, eng, fn):
        op = self.plan["ops"][self.idx]
        assert op.eng == eng, (self.idx, op.eng, eng)
        E = self.engs[eng]
        for (kind, key, s) in op.waits:
            if kind == "g":
                E.wait_ge(self.gsem[key], 16 * s)
            else:
                E.wait_ge(self.sems[key][(s - 1) // EPOCH], (s - 1) % EPOCH + 1)
        ins = fn(E)
        if op.dma:
            ins.then_inc(self.gsem[op.grp], 16)
        elif op.sig:
            ins.then_inc(self.sems[eng][(op.signo - 1) // EPOCH], 1)
        self.idx += 1

    def op(self, eng, fn, reads=(), writes=()):
        if self.dry:
            self._record(eng, reads, writes, False, None)
        else:
            self._emit(eng, fn)

    def dma(self, out, in_, reads=(), writes=(), grp=None, eng="sp", serial=False):
        if self.dry:
            if serial and grp in self.last_dma:
                self.serial_dep = self.last_dma[grp]
            self._record(eng, reads, writes, True, grp)
        else:
            self._emit(eng, lambda e: e.dma_start(out=out, in_=in_))

    def finish(self, groups):
        if not self.dry:
            E = self.engs["sp"]
            for g in groups:
                E.wait_ge(self.gsem[g], 16 * self.plan["grp"][g])


def bcast_mid(ap2d, n):
    a = ap2d.ap
    return bass.AP(ap2d.tensor, ap2d.offset, [list(a[0]), [0, n], list(a[1])])


def bcast_last(ap2d, n):
    return ap2d.unsqueeze(2).to_broadcast([ap2d.shape[0], ap2d.shape[1], n])


class Builder:
    def __init__(self, nc, P, debug=None):
        self.nc = nc
        self.P = P
        self.debug = debug
        self.dbg_groups = []
        self.bg = None
        self.hti = 0
        self.spawned = []
        self.gsfx = ""
        self.psi = 0
        self.ps_live = set()

    def mm(self, out, lhsT, rhs, start, stop, r, w, skip=False):
        if skip:
            self.P.op("pe", lambda e: e.matmul(out, lhsT=lhsT, rhs=rhs, start=start, stop=stop, skip_group_check=True), r, w)
        else:
            self.P.op("pe", lambda e: e.matmul(out, lhsT=lhsT, rhs=rhs, start=start, stop=stop), r, w)

    def tr(self, out, in_, r, w):
        idb = self.id_bf
        self.P.op("pe", lambda e: e.transpose(out, in_, idb[:]), tuple(r) + ("idbf",), w)

    def act(self, out, in_, func, r, w, scale=None, bias=None, accum=None):
        kw = {}
        if scale is not None:
            kw["scale"] = scale
        if bias is not None:
            kw["bias"] = bias
        if accum is not None:
            kw["accum_out"] = accum
        self.P.op("act", lambda e: e.activation(out=out, in_=in_, func=func, **kw), r, w)

    def tt(self, eng, out, in0, in1, op, r, w):
        self.P.op(eng, lambda e: e.tensor_tensor(out=out, in0=in0, in1=in1, op=op), r, w)

    def ts(self, eng, out, in0, s1, s2, op0, op1, r, w):
        if s2 is None:
            self.P.op(eng, lambda e: e.tensor_scalar(out=out, in0=in0, scalar1=s1, scalar2=None, op0=op0), r, w)
        else:
            self.P.op(eng, lambda e: e.tensor_scalar(out=out, in0=in0, scalar1=s1, scalar2=s2, op0=op0, op1=op1), r, w)

    def stt(self, out, in0, scalar, in1, op0, op1, r, w):
        self.P.op("dve", lambda e: e.scalar_tensor_tensor(out=out, in0=in0, scalar=scalar, in1=in1, op0=op0, op1=op1), r, w)

    def cp(self, eng, out, in_, r, w):
        if eng == "act":
            self.P.op("act", lambda e: e.copy(out=out, in_=in_), r, w)
        else:
            self.P.op(eng, lambda e: e.tensor_copy(out=out, in_=in_), r, w)

    def memset(self, eng, ap, val, w):
        self.P.op(eng, lambda e: e.memset(ap, val), (), w)

    def ps(self):
        for j in range(len(self.psb)):
            i = (self.psi + j) % len(self.psb)
            if i not in self.ps_live:
                self.psi = i + 1
                return self.psb[i], ("ps", i)
        raise RuntimeError("no free PSUM bank")

    def dump(self, name, ap, keys):
        if not self.debug:
            return
        t = self.nc.dram_tensor("dbg_" + name, list(ap.shape), ap.dtype, kind="ExternalOutput").ap()
        self.P.dma(t, ap, tuple(keys), (), grp="dbg_" + name)
        self.dbg_groups.append("dbg_" + name)

    def sb(self, st, name, shape, dt=F32, side="left"):
        self.uid = getattr(self, "uid", 0) + 1
        return st.enter_context(self.nc.sbuf_tensor(f"{name}_{self.uid}", shape, dt, side=side))

    def rsqrt_small(self, out, in_, scale, r, w):
        k = out.shape[1]
        self.ts("dve", out, in_, float(scale), EPS, ALU.mult, ALU.add, r, w)
        self.tt("pool", out, out, self.negh[:, 0:k], ALU.pow, tuple(w) + ("negh",), w)

    def load_w(self, dst, name, dram, col0, ncols, dcol0=0):
        CH = 256
        c = 0
        while c < ncols:
            n = min(CH, ncols - c)
            src = dram[:, col0 + c:col0 + c + n].rearrange("(kc p) c -> p kc c", p=128)
            keys = tuple({(name, (dcol0 + c + j) // 128) for j in range(0, n, 128)} | {(name, (dcol0 + c + n - 1) // 128)})
            grp = f"wq{self.wq % 8}"
            self.wq += 1
            self.P.dma(dst[:, :, dcol0 + c:dcol0 + c + n], src, (), keys, grp=grp, eng="pool", serial=True)
            c += n

    @staticmethod
    def wk(name, c0, c1):
        return tuple((name, j) for j in range(c0 // 128, (c1 - 1) // 128 + 1))

    def build(self, io):
        nc, P = self.nc, self.P
        with ExitStack() as g:
            sb = lambda name, shape, dt=F32: self.sb(g, name, shape, dt, side="right")
            self.pt = g.enter_context(nc.psum_tensor("pt", [128, 8, 128], BF16))
            self.psb = [g.enter_context(nc.psum_tensor(f"psb{i}", [128, 512], F32)) for i in range(7)]
            self.id_bf = sb("id_bf", [128, 128], BF16)
            self.ones_bf = sb("ones_bf", [128, 128], BF16)
            self.negd = sb("negd", [128, 128], BF16)
            self.cst = sb("cst", [128, 4])
            self.negh = sb("negh", [128, 16])
            self.fpar = sb("fpar", [128, 104])
            self.gate_bc = sb("gate_bc", [128, 1024])
            self.hTo = sb("hTo", [128, 8, NOWN], BF16)
            self.hTh = sb("hTh", [128, 8, 1], BF16)
            self.wq = 0
            self.sml = sb("sml", [128, 64])
            self.tmpA = sb("tmpA", [128, 1024])
            self.tmpB = sb("tmpB", [128, 1024])
            self.xi = 0

            self.memset("pool", self.cst[:, 0:1], EPS, ("cst",))
            self.memset("pool", self.cst[:, 1:2], float(np.log(128.0 ** -0.5)), ("cst",))
            self.memset("pool", self.ones_bf[:], 1.0, ("ones",))
            self.memset("pool", self.negh[:], -0.5, ("negh",))
            P.dma(self.fpar[:], io["fparams"], (), ("fpar",), grp="fpar")

            son = ExitStack()
            with son:
                with ExitStack() as s1:
                    self.phase_gdn(s1, son, io)
                P.barrier()
                with ExitStack() as s3:
                    self.Bg = self.sb(s3, "Bg", [128, 8, NOWN], BF16)
                    self.phase3a(io)
                    self.dump("Bg", self.Bg[:], tuple(("Bg", b) for b in range(4)))
                    P.barrier()
                    son.close()
                    self.phase3b(s3, io)
        P.finish(["yout0", "yout1"] + self.dbg_groups)

    def alloc_x(self, st, n=2):
        self.nx = n
        self.xst = [self.sb(st, f"xst{i}", [128, 1024]) for i in range(n)]
        self.xbf = [self.sb(st, f"xbf{i}", [128, 1024], BF16) for i in range(n)]

    def phase_gdn(self, st, son, io):
        nc, P = self.nc, self.P
        sb = lambda name, shape, dt=F32: self.sb(st, name, shape, dt)
        self.cf = sb("cf", [128, C_NEGU])
        self.negb = sb("negb", [128, 2, 512], BF16)
        self.nm = sb("nm", [128, 14, 128], BF16)
        GN = ("gall", "beta", "gam", "bgam", "eta", "egl")
        own = {n: sb(n + "_o", [128, NTO, 16]) for n in GN}
        self.Sst = sb("Sst", [128, 16, 128])
        self.modfm = sb("modfm", [128, 16, 2])
        self.scale1 = sb("scale1", [128, 8, 2])
        P.dma(self.cf[:], io["consts"][:, 0:C_NEGU], (), ("cf",), grp="cf")
        self.cp("dve", self.id_bf[:], self.cf[:, C_ID:C_ID + 128], ("cf",), ("idbf",))
        self.ts("dve", self.negd[:], self.cf[:, C_ID:C_ID + 128], NEGV, None, ALU.mult, None, ("cf",), ("negd",))
        with ExitStack() as s0:
            nmst = self.sb(s0, "nmst", [128, NCONST - C_NEGU])
            P.dma(nmst[:], io["consts"][:, C_NEGU:NCONST], (), ("nmst",), grp="nmst")
            self.cp("dve", self.negb[:].rearrange("p a b -> p (a b)"), nmst[:, 0:1024], ("nmst",), ("negb",))
            self.cp("dve", self.nm[:].rearrange("p a b -> p (a b)"), nmst[:, 1024:], ("nmst",), ("nm",))
            self.phase0(s0, io)
        P.barrier()
        self.dump("gate_bc", self.gate_bc[:], ("gate_bc",))
        self.dump("scale1", self.scale1[:], ("scale1",))
        self.dump("modfm", self.modfm[:], ("modfm",))
        with ExitStack() as s1:
            self.hTx = self.sb(s1, "hTx", [128, 8, NOWN], BF16)
            self.hTc = self.sb(s1, "hTc", [128, 8, NCTX], BF16)
            for n in GN:
                setattr(self, n, self.sb(s1, n, [128, 34, 16]))
            self.gsfx = ""
            with ExitStack() as sx:
                self.alloc_x(sx, 4)
                self.build_hT(io)
            P.barrier()
            with ExitStack() as sab:
                self.ab_stage(sab, io)
            P.barrier()
            for n in GN:
                self.cp("pool", own[n][:], getattr(self, n)[:, 0:NTO, :], (n,), (n + "_o",))
            self.dump("hTo", self.hTo[:], tuple(("hTo", b) for b in range(4)))
            self.dump("hTx", self.hTx[:], tuple(("hTx", b) for b in range(4)))
            self.dump("hTc", self.hTc[:], (("hTc", 0),))
            for nm_ in ("gall", "beta", "gam", "eta", "egl", "bgam"):
                self.dump(nm_, getattr(self, nm_)[:], (nm_,))
            self.memset("pool", self.Sst[:], 0.0, ("Sst",) + tuple(("S", c) for c in range(16)))
            with ExitStack() as s2:
                self.alloc_head_bufs(s2, False)
                for h in range(8):
                    self.head_pass1(h, io)
                self.dump("Sst", self.Sst[:], tuple(("S", c) for c in range(16)))
        P.barrier()
        self.on_store = self.sb(son, "on_store", [128, NTO, 1024], BF16, side="right")
        for n in GN:
            setattr(self, n, own[n])
        self.gsfx = "_o"
        with ExitStack() as s2:
            self.alloc_head_bufs(s2, True)
            self.gon = self.sb(s2, "gon", [128, 128])
            P.dma(self.gon[:], io["r_gon"], (), ("gon",), grp="gon")
            self.pass2_all(io)
            self.dump("on_store", self.on_store[:], ("on_store",))
        P.barrier()

    def phase0(self, st, io):
        P = self.P
        sb = lambda name, shape, dt=F32: self.sb(st, name, shape, dt)
        cv = sb("cv", [128, 16])
        sc = sb("sc", [128, 16])
        screp = sb("screp", [128, 8, 128])
        wm = [sb(f"wm{i}", [128, 8, 512]) for i in range(6)]
        rg = sb("rg", [128, 1024])
        rb = sb("rb", [128, 1024])
        rows = sb("rows", [2, 2048])
        P.dma(cv[:], io["cvec"], (), ("cv",), grp="cv")
        P.dma(rg[:], io["r_gpost"], (), ("rg",), grp="rg")
        P.dma(rb[:], io["r_bmodg"], (), ("rb",), grp="rb")
        self.act(sc[:], cv[:], AF.Silu, ("cv",), ("sc",))
        sc3 = sc[:].rearrange("p (k j) -> p k j", j=2)
        self.cp("dve", screp[:], sc3[:, :, 0:1].to_broadcast([128, 8, 128]), ("sc",), ("screp",))
        pm, pmk = self.ps()
        for piece in range(6):
            w = wm[piece]
            wkey = ("wm", piece)
            src = io["w_mod"][:, piece * 512:(piece + 1) * 512].rearrange("(kc p) c -> p kc c", p=128)
            P.dma(w[:], src, (), (wkey,), grp=f"wm{piece}")
            if piece < 4:
                pr, prk = self.ps()
                for kc in range(8):
                    self.mm(pr[0:2, :], sc[:, kc * 2:kc * 2 + 2], w[:, kc, :], kc == 0, kc == 7, (wkey, "sc"), (prk,))
                self.cp("act", rows[0:2, piece * 512:(piece + 1) * 512], pr[0:2, :], (prk,), (("rows", piece),))
                for i in range(4):
                    blk = piece * 4 + i
                    self.mm(pm[:, blk * 2:blk * 2 + 2], rows[0:2, blk * 128:(blk + 1) * 128],
                            self.cf[0:2, C_ID:C_ID + 2], True, True, (("rows", piece), "cf"), (pmk,))
            else:
                pg, pgk = self.ps()
                for kc in range(8):
                    self.mm(pg[:], screp[:, kc, :], w[:, kc, :], kc == 0, kc == 7, (wkey, "screp"), (pgk,))
                c0 = (piece - 4) * 512
                self.tt("dve", self.tmpA[:, 0:512], pg[:], rb[:, c0:c0 + 512], ALU.add, (pgk, "rb"), ("tmpA",))
                self.tt("dve", self.gate_bc[:, c0:c0 + 512], self.tmpA[:, 0:512], rg[:, c0:c0 + 512], ALU.mult,
                        ("tmpA", "rg"), ("gate_bc",))
        self.tt("dve", self.modfm[:], pm[:, 0:32].rearrange("p (b j) -> p b j", j=2),
                bcast_last(self.fpar[:, 8:24], 2), ALU.add, (pmk, "fpar"), ("modfm",))
        self.ts("dve", self.scale1[:], self.modfm[:, 8:16, :], 1.0, None, ALU.add, None, ("modfm",), ("scale1",))
        self.tt("dve", self.scale1[:], self.scale1[:], bcast_last(self.fpar[:, 0:8], 2), ALU.mult,
                ("scale1", "fpar"), ("scale1",))

    def hT_stageA(self, src_rows):
        P = self.P
        i = self.xi % self.nx
        self.xi += 1
        xs, xb = self.xst[i], self.xbf[i]
        sm = self.sml
        P.dma(xs[:], src_rows, (), (("xst", i),), grp=f"xst{i}")
        c = 2 * i
        self.act(xb[:], xs[:], AF.Square, (("xst", i),), (("xbf", i), ("sml", c)), accum=sm[:, c:c + 1])
        self.rsqrt_small(sm[:, c + 1:c + 2], sm[:, c:c + 1], 1.0 / D, (("sml", c),), (("sml", c + 1),))
        self.ts("dve", xb[:], xs[:], sm[:, c + 1:c + 2], None, ALU.mult, None,
                (("xst", i), ("sml", c + 1)), (("xbf", i),))
        return i

    def hT_stageB(self, i, dst, dkey, tok0, j):
        xb = self.xbf[i]
        q = self.hti % 2
        self.hti += 1
        def bank(n):
            if n == 0:
                return self.pt, "pt"
            return self.psb[3 + n][:].bitcast(BF16).rearrange("p (a b) -> p a b", b=128), ("ps", 3 + n)
        bA, kA = bank(2 * q)
        bB, kB = bank(2 * q + 1)
        for kc in range(8):
            b_, k_ = (bA, kA) if kc < 3 else (bB, kB)
            self.tr(b_[:, (kc if kc < 3 else kc - 3), :], xb[:, kc * 128:(kc + 1) * 128], (("xbf", i),), (k_,))
        for kc in range(8):
            if kc < 3:
                self.act(dst[:, kc, tok0:tok0 + 128], bA[:, kc, :], AF.Identity,
                         (kA, "scale1", "modfm"), (("~", dkey),),
                         scale=self.scale1[:, kc, j:j + 1], bias=self.modfm[:, kc, j:j + 1])
            else:
                self.ts("dve", dst[:, kc, tok0:tok0 + 128], bB[:, kc - 3, :], self.scale1[:, kc, j:j + 1],
                        self.modfm[:, kc, j:j + 1], ALU.mult, ALU.add, (kB, "scale1", "modfm"), (("~", dkey),))

    def build_hT(self, io):
        tiles = []
        for t in range(NTO):
            tiles.append((io["x_own"][t * 128:(t + 1) * 128, :], self.hTo, ("hTo", t // 4), t * 128, 0))
        for t in range(NTO):
            tiles.append((io["x_oth"][t * 128:(t + 1) * 128, :], self.hTx, ("hTx", t // 4), t * 128, 0))
        for t in range(2):
            tiles.append((io["ctx"][t * 128:(t + 1) * 128, :], self.hTc, ("hTc", 0), t * 128, 1))
        pend = None
        for (src, dst, dkey, tok0, j) in tiles:
            i = self.hT_stageA(src)
            if pend is not None:
                self.hT_stageB(*pend)
            pend = (i, dst, dkey, tok0, j)
        self.hT_stageB(*pend)
        self.cp("pool", self.hTh[:], self.hTx[:, :, 0:1], (("hTx", 0),), ("hTh",))

    def hT_of(self, T):
        if T < 16:
            return self.hTo, ("hTo", T // 4), T * 128
        if T < 32:
            return self.hTx, ("hTx", (T - 16) // 4), (T - 16) * 128
        return self.hTc, ("hTc", 0), (T - 32) * 128

    def ab_stage(self, st, io):
        P = self.P
        sb = lambda name, shape, dt=F32: self.sb(st, name, shape, dt)
        wab = sb("wab", [128, 8, 32], BF16)
        ab = sb("ab", [128, 34, 32])
        rdtb = sb("rdtb", [128, 16])
        rneg = sb("rneg", [128, 16])
        P.dma(wab[:], io["w_ab"].rearrange("(kc p) c -> p kc c", p=128), (), ("wab",), grp="wab", eng="pool")
        P.dma(rdtb[:], io["r_dtb"], (), ("rdtb",), grp="rdtb")
        P.dma(rneg[:], io["r_alog"], (), ("rneg",), grp="rneg")
        for b0 in range(0, 34, 16):
            nb = min(16, 34 - b0)
            pa, pak = self.ps()
            for i in range(nb):
                T = b0 + i
                hT, hk, t0 = self.hT_of(T)
                for kc in range(8):
                    self.mm(pa[:, i * 32:(i + 1) * 32], hT[:, kc, t0:t0 + 128], wab[:, kc, :], kc == 0, kc == 7,
                            (hk, "wab"), (pak,))
            self.cp("act", ab[:, b0:b0 + nb, :], pa[:, 0:nb * 32].rearrange("p (t c) -> p t c", c=32), (pak,), ("ab",))
        self.act(rneg[:], rneg[:], AF.Exp, ("rneg",), ("rneg",))
        self.ts("dve", rneg[:], rneg[:], -1.0, None, ALU.mult, None, ("rneg",), ("rneg",))
        g = self.gall
        self.tt("dve", g[:], ab[:, :, 0:16], bcast_mid(rdtb[:], 34), ALU.add, ("ab", "rdtb"), ("gall",))
        self.act(g[:], g[:], AF.Exp, ("gall",), ("gall",))
        self.act(g[:], g[:], AF.Ln, ("gall",), ("gall",), bias=1.0)
        self.tt("dve", g[:], g[:], bcast_mid(rneg[:], 34), ALU.mult, ("gall", "rneg"), ("gall",))
        self.act(self.beta[:], ab[:, :, 16:32], AF.Sigmoid, ("ab",), ("beta",))
        cf = self.cf
        for d in range(2):
            rhs = g[:, :, d * 8:(d + 1) * 8]
            incl = cf[:, C_LE:C_LE + 128] if d == 0 else cf[:, C_GE:C_GE + 128]
            strict = cf[:, C_GT:C_GT + 128] if d == 0 else cf[:, C_LT:C_LT + 128]
            p1, k1 = self.ps()
            self.mm(p1[:, 0:272], incl, rhs, True, True, ("cf", "gall"), (k1,))
            self.act(self.gam[:, :, d * 8:(d + 1) * 8], p1[:, 0:272].rearrange("p (t c) -> p t c", c=8), AF.Exp,
                     (k1,), ("gam",))
            p2, k2 = self.ps()
            self.mm(p2[:, 0:272], strict, rhs, True, True, ("cf", "gall"), (k2,))
            self.act(self.eta[:, :, d * 8:(d + 1) * 8], p2[:, 0:272].rearrange("p (t c) -> p t c", c=8), AF.Exp,
                     (k2,), ("eta",))
            p3, k3 = self.ps()
            self.mm(p3[:, 0:272], incl, rhs, True, False, ("cf", "gall"), (k3,))
            self.mm(p3[:, 0:272], strict, rhs, False, True, ("cf", "gall"), (k3,))
            self.act(self.egl[:, :, d * 8:(d + 1) * 8], p3[:, 0:272].rearrange("p (t c) -> p t c", c=8), AF.Exp,
                     (k3,), ("egl",))
        self.tt("dve", self.bgam[:], self.beta[:], self.gam[:], ALU.mult, ("beta", "gam"), ("bgam",))

    def alloc_head_bufs(self, st, full):
        sb = lambda name, shape, dt=F32: self.sb(st, name, shape, dt)
        self.whs = [sb(f"wh{i}", [128, 8, 384], BF16) for i in range(2)]
        self.praw = sb("praw", [128, NOWN + 2])
        self.sqb = sb("sqb", [128, 1024], BF16)
        ntk = NOWN if full else NOWN + NCTX
        self.kTs = [sb(f"kT{i}", [128, ntk], BF16) for i in range(2)]
        if full:
            self.qTs = [sb(f"qT{i}", [128, NOWN], BF16) for i in range(2)]
        self.ktoks = [sb(f"ktok{i}", [128, ntk // 128, 128], BF16) for i in range(2)]
        self.vtoks = [sb(f"vtok{i}", [128, ntk // 128, 128], BF16) for i in range(2)]
        NS = 2
        self.gm = [sb(f"gm{t}", [128, 4, 128]) for t in range(NS)]
        self.Eb = self.gm
        self.Am = [sb(f"Am{t}", [128, 4, 128], BF16) for t in range(NS)]
        self.Vm = [sb(f"Vm{t}", [128, 4, 128], BF16) for t in range(NS)]
        self.Rm = [[sb(f"Rm{t}{i}", [128, 4, 128], BF16) for i in range(2)] for t in range(NS)]
        self.Rt = [[sb(f"Rt{t}{i}", [128, 4, 128], BF16) for i in range(2)] for t in range(NS)]
        self.Xb = [sb(f"Xb{t}", [128, 4, 256], BF16) for t in range(NS)]
        self.U = [[sb(f"U{d}{p}", [128, 4, 128]) for p in range(2)] for d in range(2)]
        self.WT = [[sb(f"WT{d}{p}", [128, 4, 128], BF16) for p in range(2)] for d in range(2)]
        self.KT = [[sb(f"KT{d}{p}", [128, 4, 128], BF16) for p in range(2)] for d in range(2)]
        if full:
            self.AT = [[sb(f"AT{d}{p}", [128, 4, 128], BF16) for p in range(2)] for d in range(2)]
        self.vnew = [sb(f"vnew{d}", [128, 128], BF16) for d in range(2)]
        self.Sb = [sb(f"Sb{d}", [128, 128], BF16) for d in range(2)]
        if full:
            self.oacc = sb("oacc", [128, NTO, 128])
            self.nsq = self.gm[0]

    def proj_seg(self, h, seg, io):
        P = self.P
        ntok = seg["ntok"]
        hT = seg["hT"]
        loff = seg["loff"]
        hp = h % 2
        wh = self.whs[hp]
        for kind in seg["kinds"]:
            ki = "qkv".index(kind)
            cb = ki * 8 + h
            wcol = lambda tap: self.fpar[:, 32 + tap * 24 + cb:32 + tap * 24 + cb + 1]
            wkeys = self.wk(f"wh{hp}", ki * 128, ki * 128 + 128)
            nblk = (ntok + 511) // 512
            for b in range(nblk):
                n = min(512, ntok - b * 512)
                bk = yield from self.acq(1)
                pp, pk = bk[0]
                for kc in range(8):
                    self.mm(pp[:, 0:n], wh[:, kc, ki * 128:(ki + 1) * 128], hT[:, kc, b * 512:b * 512 + n],
                            kc == 0, kc == 7, wkeys + (seg["hkey"](b),), (pk,))
                yield
                self.cp("act", self.praw[:, 1 + b * 512:1 + b * 512 + n], pp[:, 0:n], (pk,), ("praw",))
                self.rel(bk)
            for side, col in (("left", 0), ("right", ntok + 1)):
                src = seg[side]
                if src is None:
                    self.memset("pool", self.praw[:, col:col + 1], 0.0, ("praw",))
                else:
                    ht, hk, c = src
                    bk = yield from self.acq(1)
                    pp, pk = bk[0]
                    for kc in range(8):
                        self.mm(pp[:, 0:1], wh[:, kc, ki * 128:(ki + 1) * 128], ht[:, kc, c:c + 1],
                                kc == 0, kc == 7, wkeys + (hk,), (pk,))
                    yield
                    self.cp("act", self.praw[:, col:col + 1], pp[:, 0:1], (pk,), ("praw",))
                    self.rel(bk)
            for o in range(0, ntok, 1024):
                n = min(1024, ntok - o)
                acc = self.tmpA
                self.act(acc[:, 0:n], self.praw[:, 1 + o:1 + o + n], AF.Identity, ("praw", "fpar"), ("tmpA",), scale=wcol(1))
                yield
                self.stt(acc[:, 0:n], self.praw[:, o:o + n], wcol(0), acc[:, 0:n], ALU.mult, ALU.add,
                         ("praw", "fpar", "tmpA"), ("tmpA",))
                yield
                self.stt(acc[:, 0:n], self.praw[:, 2 + o:2 + o + n], wcol(2), acc[:, 0:n], ALU.mult, ALU.add,
                         ("praw", "fpar", "tmpA"), ("tmpA",))
                yield
                if kind == "v":
                    self.act(self.sqb[:, 0:n], acc[:, 0:n], AF.Silu, ("tmpA",), ("sqb",))
                    yield
                    src_bf = self.sqb
                    soff = 0
                else:
                    self.act(acc[:, 0:n], acc[:, 0:n], AF.Silu, ("tmpA",), ("tmpA",))
                    yield
                    self.tt("pool", self.sqb[:, 0:n], acc[:, 0:n], acc[:, 0:n], ALU.mult, ("tmpA",), ("sqb",))
                    yield
                    dstT = self.kTs[hp] if kind == "k" else self.qTs[hp]
                    dkey = ("kT", hp) if kind == "k" else ("qT", hp)
                    for c0 in range(0, n, 512):
                        m = min(512, n - c0)
                        bk = yield from self.acq(1)
                        pp, pk = bk[0]
                        self.mm(pp[:, 0:m], self.ones_bf[:], self.sqb[:, c0:c0 + m], True, True, ("ones", "sqb"), (pk,))
                        yield
                        rin = self.tmpB
                        self.act(rin[:, 0:m], pp[:, 0:m], AF.Ln, (pk, "cst"), ("tmpB",), bias=self.cst[:, 0:1])
                        self.rel(bk)
                        if kind == "q":
                            self.act(rin[:, 0:m], rin[:, 0:m], AF.Exp, ("tmpB", "cst"), ("tmpB",), scale=-0.5,
                                     bias=self.cst[:, 1:2])
                        else:
                            self.act(rin[:, 0:m], rin[:, 0:m], AF.Exp, ("tmpB",), ("tmpB",), scale=-0.5)
                        yield
                        self.tt("dve", dstT[:, loff + o + c0:loff + o + c0 + m], acc[:, c0:c0 + m], rin[:, 0:m], ALU.mult,
                                ("tmpA", "tmpB"), (dkey,))
                        yield
                    src_bf = dstT
                    soff = loff + o
                if kind in ("k", "v"):
                    dtok = self.ktoks[hp] if kind == "k" else self.vtoks[hp]
                    dk2 = ("ktok", hp) if kind == "k" else ("vtok", hp)
                    skey = "sqb" if kind == "v" else ("kT", hp)
                    nt = n // 128
                    lt0 = (loff + o) // 128
                    for t in range(nt):
                        self.tr(self.pt[:, t, :], src_bf[:, soff + t * 128:soff + (t + 1) * 128], (skey,), ("pt",))
                    yield
                    self.cp("act", dtok[:, lt0:lt0 + nt, :], self.pt[:, 0:nt, :], ("pt",), (dk2,))
                    yield

    def load_head_w(self, h, io, kinds):
        for kind in kinds:
            ki = "qkv".index(kind)
            self.load_w(self.whs[h % 2], f"wh{h % 2}", io["w_in"], ki * 1024 + h * 128, 128, ki * 128)

    def acq(self, n):
        while True:
            free = []
            for j in range(len(self.psb)):
                i = (self.psi + j) % len(self.psb)
                if i not in self.ps_live:
                    free.append(i)
            if len(free) >= n:
                take = free[:n]
                self.psi = take[-1] + 1
                for i in take:
                    self.ps_live.add(i)
                return [(self.psb[i], ("ps", i)) for i in take]
            yield

    def rel(self, banks):
        for (_, k) in banks:
            self.ps_live.discard(k[1])

    def gdn_pre(self, h, d, T0, nb, l0, full, par, ts):
        cf = self.cf
        col = d * 8 + h
        hp = h % 2
        G = self.gsfx
        N = nb * 128
        if d == 0:
            lh_nat, rm_nat, neg_nat = C_LE, C_GT, 0
            lh_tr, rm_tr, neg_tr = C_GT, C_LE, 1
        else:
            lh_nat, rm_nat, neg_nat = C_GE, C_LT, 1
            lh_tr, rm_tr, neg_tr = C_LT, C_GE, 0
        g_bc = bcast_last(self.gall[:, T0:T0 + nb, col], 128)
        beta_bc = bcast_last(self.beta[:, T0:T0 + nb, col], 128)
        bgam_bc = bcast_last(self.bgam[:, T0:T0 + nb, col], 128)
        eta_bc = bcast_last(self.eta[:, T0:T0 + nb, col], 128)
        kTi = lambda i: self.kTs[hp][:, (l0 + i) * 128:(l0 + i + 1) * 128]
        qTi = lambda i: self.qTs[hp][:, (l0 + i) * 128:(l0 + i + 1) * 128]
        idf = cf[:, C_ID:C_ID + 128]
        idb = self.id_bf
        gm, Eb_, Am, Vm, Xb = self.gm[ts], self.Eb[ts], self.Am[ts], self.Vm[ts], self.Xb[ts]
        Rm, Rt = self.Rm[ts], self.Rt[ts]
        kgm, kEb, kAm, kVm, kXb = ("gm", ts), ("gm", ts), ("Am", ts), ("Vm", ts), ("Xb", ts)
        v3 = lambda p: p[:, 0:N].rearrange("p (a b) -> p a b", b=128)
        self.tt("pool", gm[:, 0:nb, :], bcast_mid(cf[:, rm_nat:rm_nat + 128], nb), g_bc, ALU.mult, ("cf", "gall" + G), (kgm,))
        self.tt("pool", Xb[:, 0:nb, 0:128], self.vtoks[hp][:, l0:l0 + nb, :], beta_bc, ALU.mult, (("vtok", hp), "beta" + G), (kXb,))
        self.tt("pool", Xb[:, 0:nb, 128:256], self.ktoks[hp][:, l0:l0 + nb, :], bgam_bc, ALU.mult, (("ktok", hp), "bgam" + G), (kXb,))
        bk = yield from self.acq(2)
        pD, kD = bk[0]
        pG, kG = bk[1]
        self.mm(pD[:, 0:N], cf[:, lh_nat:lh_nat + 128], gm[:, 0:nb, :].rearrange("p a b -> p (a b)"), True, False,
                ("cf", kgm), (kD,))
        self.mm(pD[:, 0:N], idb[:], self.negb[:, neg_nat, 0:N], False, False, ("idbf", "negb"), (kD,))
        self.mm(pD[:, 0:N], idb[:], bcast_mid(self.negd[:], nb), False, True, ("idbf", "negd"), (kD,))
        for i in range(nb):
            self.mm(pG[:, i * 128:(i + 1) * 128], kTi(i), kTi(i), True, True, (("kT", hp),), (kG,))
        yield
        Eb = Eb_[:, 0:nb, :]
        self.act(Eb, v3(pD), AF.Exp, (kD,), (kEb,))
        yield
        self.tt("pool", Eb, Eb, beta_bc, ALU.mult, (kEb, "beta" + G), (kEb,))
        yield
        self.tt("dve", Am[:, 0:nb, :], v3(pG), Eb, ALU.mult, (kG, kEb), (kAm,))
        self.rel(bk)
        if full:
            self.spawned.append(self.gdn_at(h, d, T0, nb, l0, par, ts))
        yield
        for k in range(7):
            cur = (k + 1) % 2
            nxt = k % 2
            if k == 0:
                Rc = lambda i: idb[:]
                Rtc = lambda i: idb[:]
                rk, rtk = "idbf", "idbf"
            else:
                Rc = (lambda c: lambda i: Rm[c][:, i, :])(cur)
                Rtc = (lambda c: lambda i: Rt[c][:, i, :])(cur)
                rk, rtk = ("Rm", ts, cur), ("Rt", ts, cur)
                if k == 1:
                    Rc = lambda i: Vm[:, i, :]
                    rk = kVm
            bk = yield from self.acq(1)
            pZ, kZ = bk[0]
            for i in range(nb):
                self.mm(pZ[:, i * 128:(i + 1) * 128], Am[:, i, :], Rc(i), i == 0, False, (kAm, rk), (kZ,), skip=True)
            self.mm(pZ[:, 0:N], idb[:], bcast_mid(idb[:], nb), False, True, ("idbf",), (kZ,), skip=True)
            yield
            self.tt("dve", Vm[:, 0:nb, :], v3(pZ), bcast_mid(self.nm[:, d * 7 + k, :], nb), ALU.mult, (kZ, "nm"), (kVm,))
            self.rel(bk)
            bk = yield from self.acq(2 if k < 6 else 1)
            pR, kR = bk[0]
            if k > 0:
                for i in range(nb):
                    self.mm(pR[:, i * 128:(i + 1) * 128], Rtc(i), Vm[:, i, :], True, True, (rtk, kVm), (kR,))
            if k < 6:
                pR2, kR2 = bk[1]
                for i in range(nb):
                    self.mm(pR2[:, i * 128:(i + 1) * 128], Vm[:, i, :], Rtc(i), True, True, (kVm, rtk), (kR2,))
            yield
            if k > 0:
                self.cp("act", Rm[nxt][:, 0:nb, :], v3(pR), (kR,), (("Rm", ts, nxt),))
            if k < 6:
                self.cp("dve" if k % 2 == 0 else "act", Rt[nxt][:, 0:nb, :], v3(pR2), (kR2,), (("Rt", ts, nxt),))
            self.rel(bk)
            if k == 3:
                yield "MID"
                self.tt("pool", self.KT[d][par][:, 0:nb, :], self.ktoks[hp][:, l0:l0 + nb, :], eta_bc, ALU.mult,
                        (("ktok", hp), "eta" + G), (("KT", d, par),))
        Rf = Rm[0]
        rfk = ("Rm", ts, 0)
        bk = yield from self.acq(2)
        pU, kU = bk[0]
        for i in range(nb):
            self.mm(pU[:, i * 128:(i + 1) * 128], Rf[:, i, :], Xb[:, i, 0:128], True, True, (rfk, kXb), (kU,))
        pW, kW = bk[1]
        for i in range(nb):
            self.mm(pW[:, i * 128:(i + 1) * 128], Xb[:, i, 128:256], Rf[:, i, :], True, True, (rfk, kXb), (kW,))
        yield
        self.cp("act", self.U[d][par][:, 0:nb, :], v3(pU), (kU,), (("U", d, par),))
        self.cp("dve", self.WT[d][par][:, 0:nb, :], v3(pW), (kW,), (("WT", d, par),))
        self.rel(bk)
        yield

    def gdn_at(self, h, d, T0, nb, l0, par, ts):
        cf = self.cf
        col = d * 8 + h
        hp = h % 2
        G = self.gsfx
        N = nb * 128
        if d == 0:
            lh_tr, rm_tr, neg_tr = C_GT, C_LE, 1
        else:
            lh_tr, rm_tr, neg_tr = C_LT, C_GE, 0
        g_bc = bcast_last(self.gall[:, T0:T0 + nb, col], 128)
        idb = self.id_bf
        gm = self.gm[ts]
        kgm = ("gm", ts)
        Eb = gm[:, 0:nb, :]
        v3 = lambda p: p[:, 0:N].rearrange("p (a b) -> p a b", b=128)
        self.tt("pool", gm[:, 0:nb, :], bcast_mid(cf[:, rm_tr:rm_tr + 128], nb), g_bc, ALU.mult, ("cf", "gall" + G), (kgm,))
        bk = yield from self.acq(2)
        pD2, kD2 = bk[0]
        pQ, kQ = bk[1]
        self.mm(pD2[:, 0:N], cf[:, lh_tr:lh_tr + 128], gm[:, 0:nb, :].rearrange("p a b -> p (a b)"), True, False,
                ("cf", kgm), (kD2,))
        self.mm(pD2[:, 0:N], idb[:], self.negb[:, neg_tr, 0:N], False, True, ("idbf", "negb"), (kD2,))
        for i in range(nb):
            self.mm(pQ[:, i * 128:(i + 1) * 128], self.kTs[hp][:, (l0 + i) * 128:(l0 + i + 1) * 128],
                    self.qTs[hp][:, (l0 + i) * 128:(l0 + i + 1) * 128], True, True, (("kT", hp), ("qT", hp)), (kQ,))
        yield
        self.act(Eb, v3(pD2), AF.Exp, (kD2,), (kgm,))
        yield
        self.tt("dve", self.AT[d][par][:, 0:nb, :], v3(pQ), Eb, ALU.mult, (kQ, kgm), (("AT", d, par),))
        self.rel(bk)
        yield

    def gdn_step(self, h, d, T, i, par, l, full):
        col = d * 8 + h
        hp = h % 2
        G = self.gsfx
        S = self.Sst[:, col, :]
        sk = ("S", col)
        Sb = self.Sb[d]
        sbk = ("Sb", d)
        vn = self.vnew[d]
        vk = ("vnew", d)
        bk = yield from self.acq(1)
        pw, kw = bk[0]
        self.mm(pw[:, 0:128], self.WT[d][par][:, i, :], Sb[:], True, True, (("WT", d, par), sbk), (kw,))
        yield
        self.tt("dve", vn[:], self.U[d][par][:, i, :], pw[:, 0:128], ALU.subtract, (("U", d, par), kw), (vk,))
        self.rel(bk)
        bk = yield from self.acq(2 if full else 1)
        pS, kS = bk[0]
        self.mm(pS[:, 0:128], self.KT[d][par][:, i, :], vn[:], True, True, (("KT", d, par), vk), (kS,))
        if full:
            po, ko = bk[1]
            self.mm(po[:, 0:128], self.qTs[hp][:, l * 128:(l + 1) * 128], Sb[:], True, True, (("qT", hp), sbk), (ko,))
            self.mm(po[:, 128:256], self.AT[d][par][:, i, :], vn[:], True, True, (("AT", d, par), vk), (ko,))
        yield
        self.stt(S, S, self.egl[:, T, col:col + 1], pS[:, 0:128], ALU.mult, ALU.add, (sk, "egl" + G, kS), (sk,))
        yield
        self.cp("act", Sb[:], S, (sk,), (sbk,))
        if full:
            ok = ("oacc", l)
            self.stt(self.oacc[:, l, :], po[:, 0:128], self.gam[:, T, col:col + 1], self.oacc[:, l, :], ALU.mult, ALU.add,
                     (ko, "gam" + G, ok), (ok,))
            self.tt("dve", self.oacc[:, l, :], po[:, 128:256], self.oacc[:, l, :], ALU.add, (ko, ok), (ok,))
        self.rel(bk)
        yield

    def chain_batches(self, d, T_lo, n):
        out = []
        if d == 0:
            t = T_lo
            while t < T_lo + n:
                nb = min(4, T_lo + n - t)
                out.append((t, nb, list(range(nb))))
                t += nb
        else:
            t = T_lo + n
            while t > T_lo:
                nb = min(4, t - T_lo)
                out.append((t - nb, nb, list(range(nb - 1, -1, -1))))
                t -= nb
        return out

    def bg_step(self):
        if self.bg is not None:
            try:
                next(self.bg)
            except StopIteration:
                self.bg = None

    def bg_drain(self):
        while self.bg is not None:
            self.bg_step()

    def run_round(self, gens):
        gens = list(gens)
        while gens:
            alive = []
            for g_ in gens:
                try:
                    next(g_)
                    alive.append(g_)
                except StopIteration:
                    pass
            gens = alive + self.spawned
            self.spawned = []
            self.bg_step()

    def run_staggered(self, h, chain, full, extra=None):
        d, T_lo, n, lbase = chain
        bat = [(T0, nb, order, T0 - lbase) for (T0, nb, order) in self.chain_batches(d, T_lo, n)]
        nbt = len(bat)
        pre = {}

        def start(bi):
            T0, nb, order, l0 = bat[bi]
            pre[bi] = self.gdn_pre(h, d, T0, nb, l0, full, bi % 2, bi % 2)

        for r in range(-1, nbt):
            entries = []
            if r + 1 < nbt:
                if r + 1 not in pre:
                    start(r + 1)
                entries.append([pre[r + 1], False])
            if r + 2 < nbt:
                start(r + 2)
                entries.append([pre[r + 2], True])
            if r >= 0:
                T0, nb, order, l0 = bat[r]
                entries.append([self.scan_batch(h, d, T0, nb, order, l0, r % 2, full), False])
            if extra is not None:
                for mk in extra(r, nbt):
                    entries.append([mk, False])
            while entries:
                alive = []
                for e in entries:
                    try:
                        v = next(e[0])
                        if v == "MID" and e[1]:
                            continue
                        alive.append(e)
                    except StopIteration:
                        pass
                entries = alive
                self.bg_step()

    def scan_batch(self, h, d, T0, nb, order, l0, par, full):
        for i in order:
            yield from self.gdn_step(h, d, T0 + i, i, par, l0 + i, full)

    def run_chains(self, h, chains, full):
        sched = []
        for ci, (d, T_lo, n, lbase) in enumerate(chains):
            sched.append([(d, T0, nb, order, T0 - lbase, ci) for (T0, nb, order) in self.chain_batches(d, T_lo, n)])
        nround = max(len(s) for s in sched)
        for r in range(-1, nround):
            gens = []
            for s in sched:
                if r + 1 < len(s):
                    d, T0, nb, order, l0, ci = s[r + 1]
                    gens.append(self.gdn_pre(h, d, T0, nb, l0, full, (r + 1) % 2, ci))
            for s in sched:
                if 0 <= r < len(s):
                    d, T0, nb, order, l0, ci = s[r]
                    gens.append(self.scan_batch(h, d, T0, nb, order, l0, r % 2, full))
            self.run_round(gens)

    def set_state(self, h, d):
        col = d * 8 + h
        self.cp("act", self.Sb[d][:], self.Sst[:, col, :], (("S", col), "Sst"), (("Sb", d),))

    def proj_pass1(self, h, io):
        seg = dict(ntok=NCTX, hT=self.hTc, hkey=lambda b: ("hTc", 0), left=None, right=None, loff=NOWN, kinds="kv")
        yield from self.proj_seg(h, seg, io)
        seg = dict(ntok=NOWN, hT=self.hTx, hkey=lambda b: ("hTx", b), left=(self.hTo, ("hTo", 3), NOWN - 1), right=None,
                   loff=0, kinds="kv")
        yield from self.proj_seg(h, seg, io)
        if h + 1 < 8:
            self.load_head_w(h + 1, io, "kv")

    def head_pass1(self, h, io):
        if h == 0:
            self.load_head_w(0, io, "kv")
            self.bg = self.proj_pass1(0, io)
        self.bg_drain()
        self.bg = self.proj_pass1(h + 1, io) if h + 1 < 8 else None
        hp = h % 2
        if h == 0:
            self.dump("p1_kT", self.kTs[hp][:], (("kT", hp),))
            self.dump("p1_ktok", self.ktoks[hp][:], (("ktok", hp),))
            self.dump("p1_vtok", self.vtoks[hp][:], (("vtok", hp),))
        self.set_state(h, 0)
        self.set_state(h, 1)
        def extra(r, nbt):
            if r == nbt - 2:
                return [self.gdn_pre(h, 0, 32, 2, 16, False, 0, nbt % 2)]
            if r == nbt - 1:
                return [self.scan_batch(h, 0, 32, 2, [0, 1], 16, 0, False)]
            return []
        self.run_staggered(h, (1, 16, 18, 16), False, extra)

    def proj_pass2(self, h, io):
        seg = dict(ntok=NOWN, hT=self.hTo, hkey=lambda b: ("hTo", b), left=None, right=(self.hTh, "hTh", 0),
                   loff=0, kinds="qkv")
        yield from self.proj_seg(h, seg, io)
        if h + 1 < 8:
            self.load_head_w(h + 1, io, "qkv")

    def pass2_all(self, io):
        chains = [(0, 0, 16, 0), (1, 0, 16, 0)]
        sched = []
        for ci, (d, T_lo, n, lbase) in enumerate(chains):
            sched.append([(d, T0, nb, order, T0 - lbase, ci) for (T0, nb, order) in self.chain_batches(d, T_lo, n)])
        nround = len(sched[0])

        def pre_gens(h, bi):
            return [self.gdn_pre(h, d, T0, nb, l0, True, bi % 2, ci) for (d, T0, nb, order, l0, ci) in
                    (s_[bi] for s_ in sched)]

        def scan_gens(h, bi):
            return [self.scan_batch(h, d, T0, nb, order, l0, bi % 2, True) for (d, T0, nb, order, l0, ci) in
                    (s_[bi] for s_ in sched)]

        self.load_head_w(0, io, "qkv")
        self.bg = self.proj_pass2(0, io)
        self.bg_drain()
        self.bg = self.proj_pass2(1, io)
        self.run_round(pre_gens(0, 0))
        for h in range(8):
            hp = h % 2
            if h == 0:
                self.dump("p2_qT", self.qTs[hp][:], (("qT", hp),))
                self.dump("p2_kT", self.kTs[hp][:], (("kT", hp),))
                self.dump("p2_vtok", self.vtoks[hp][:], (("vtok", hp),))
            self.memset("pool", self.oacc[:], 0.0, tuple(("oacc", l) for l in range(NTO)))
            self.set_state(h, 0)
            self.set_state(h, 1)
            for r in range(nround):
                gens = []
                if r + 1 < nround:
                    gens += pre_gens(h, r + 1)
                elif h + 1 < 8:
                    self.bg_drain()
                    gens += pre_gens(h + 1, 0)
                gens += scan_gens(h, r)
                self.run_round(gens)
            self.bg = self.proj_pass2(h + 2, io) if h + 2 < 8 else None
            if h == 0:
                self.dump("p2_oacc", self.oacc[:], tuple(("oacc", l) for l in range(NTO)))
            self.head_norm(h)

    def head_norm(self, h):
        okeys = tuple(("oacc", l) for l in range(NTO))
        ssq = self.sml[:, 8:24]
        sq = self.nsq[:]
        for qt in range(4):
            o2 = self.oacc[:, qt * 4:(qt + 1) * 4, :]
            self.tt("dve", sq, o2, o2, ALU.mult, okeys, (("gm", 0),))
            self.P.op("dve", (lambda sq, qt: lambda e: e.tensor_reduce(
                out=self.sml[:, 8 + qt * 4:12 + qt * 4], in_=sq, axis=AX.X, op=ALU.add))(sq, qt),
                (("gm", 0),), (("sml", "ssq"),))
        self.rsqrt_small(ssq, ssq, 1.0 / 128, (("sml", "ssq"),), (("sml", "ssq"),))
        self.tt("dve", self.oacc[:], self.oacc[:], bcast_last(ssq, 128), ALU.mult, okeys + (("sml", "ssq"),), okeys)
        self.tt("pool", self.on_store[:, :, h * 128:(h + 1) * 128], self.oacc[:], bcast_mid(self.gon[:], NTO), ALU.mult,
                okeys + ("gon",), ("on_store",))

    def phase3a(self, io):
        nc, P = self.nc, self.P
        Bg = self.Bg
        hTo = self.hTo
        with ExitStack() as s:
            sb = lambda name, shape, dt=F32: self.sb(s, name, shape, dt)
            w_zb = sb("w_zb", [128, 8, 1024], BF16)
            w_gb = sb("w_gb", [128, 8, 1024], BF16)
            w_pb = sb("w_pb", [128, 8, 1024], BF16)
            ybb = sb("ybb", [128, 1024], BF16)
            ybT = sb("ybT", [128, 8, 512], BF16)
            self.load_w(w_zb, "w_zb", io["w_in"], OFF_ZB, 1024)
            self.load_w(w_gb, "w_gb", io["w_in"], OFF_G + 1024, 1024)
            self.load_w(w_pb, "w_pb", io["w_pb"], 0, 1024)
            ybb2 = sb("ybb2", [128, 1024], BF16)
            gbuf = sb("gbuf", [128, 8, 512])
            ybbs = [ybb, ybb2]
            souts = [self.tmpA, self.tmpB]
            def ZB(t):
                blk, tt_ = divmod(t, 4)
                so, yb = souts[t % 2], ybbs[t % 2]
                sok, ybk = ("so", t % 2), ("ybb", t % 2)
                for half in range(2):
                    pz, kz = self.ps()
                    for kc in range(8):
                        self.mm(pz[:], hTo[:, kc, t * 128:(t + 1) * 128], w_zb[:, kc, half * 512:(half + 1) * 512],
                                kc == 0, kc == 7, (("hTo", blk),) + self.wk("w_zb", half * 512, half * 512 + 512), (kz,))
                    self.act(so[:, half * 512:(half + 1) * 512], pz[:], AF.Silu, (kz,), (sok,))
                self.tt("pool", yb[:], self.on_store[:, t, :], so[:], ALU.mult, ("on_store", sok), (ybk,))

            def MIDP(tp):
                bp, tq = divmod(tp, 4)
                yb, ybk = ybbs[tp % 2], ("ybb", tp % 2)
                for kc in range(8):
                    self.tr(self.pt[:, kc, :], yb[:, kc * 128:(kc + 1) * 128], (ybk,), ("pt",))
                self.cp("dve", ybT[:, :, tq * 128:(tq + 1) * 128], self.pt[:], ("pt",), ("ybT",))
                if tq == 3:
                    for mb in range(8):
                        pB, kB = self.ps()
                        for kc in range(8):
                            self.mm(pB[:], w_pb[:, kc, mb * 128:(mb + 1) * 128], ybT[:, kc, :],
                                    kc == 0, kc == 7, ("ybT",) + self.wk("w_pb", mb * 128, mb * 128 + 128), (kB,))
                        self.tt("dve", Bg[:, mb, bp * 512:(bp + 1) * 512], pB[:], gbuf[:, mb, :], ALU.mult,
                                (kB, ("gbuf", mb)), (("Bg", bp),))

            def GATES(t):
                blk, tt_ = divmod(t, 4)
                for mb in (2 * tt_, 2 * tt_ + 1):
                    pg, kg = self.ps()
                    for kc in range(8):
                        self.mm(pg[:], w_gb[:, kc, mb * 128:(mb + 1) * 128], hTo[:, kc, blk * 512:(blk + 1) * 512],
                                kc == 0, kc == 7, (("hTo", blk),) + self.wk("w_gb", mb * 128, mb * 128 + 128), (kg,))
                    self.act(gbuf[:, mb, :], pg[:], AF.Sigmoid, (kg,), (("gbuf", mb),))

            for t in range(17):
                odd = (t % 4) % 2 == 1
                if t < 16 and odd:
                    GATES(t)
                if t < 16:
                    ZB(t)
                if t >= 1:
                    MIDP(t - 1)
                if t < 16 and not odd:
                    GATES(t)

    def phase3b(self, st, io):
        nc, P = self.nc, self.P
        Bg = self.Bg
        hTo = self.hTo
        yaT = self.sb(st, "yaT", [128, 8, NOWN], BF16)
        with ExitStack() as s:
            sb = lambda name, shape, dt=F32: self.sb(s, name, shape, dt)
            w_ua = sb("w_ua", [128, 8, 1024], BF16)
            w_va = sb("w_va", [128, 8, 1024], BF16)
            w_za = sb("w_za", [128, 8, 1024], BF16)
            wsp = sb("wsp", [128, 8, 128], BF16)
            rlg = sb("rlg", [128, 1024])
            rlb = sb("rlb", [128, 1024])
            rbs = sb("rbs", [128, 1024])
            uz = sb("uz", [128, 8, 512], BF16)
            gvs = [sb(f"gv{i}", [128, 1024]) for i in range(3)]
            vlns = [sb(f"vln{i}", [128, 1024], BF16) for i in range(3)]
            self.load_w(w_va, "w_va", io["w_in"], OFF_VA, 1024)
            self.load_w(w_ua, "w_ua", io["w_in"], OFF_UA, 1024)
            self.load_w(w_za, "w_za", io["w_in"], OFF_ZA, 1024)
            P.dma(gvs[1][:], io["w_spT"], (), (("gv", 1),), grp="wspst")
            self.cp("dve", wsp[:].rearrange("p a b -> p (a b)"), gvs[1][:], (("gv", 1),), ("wsp",))
            P.dma(rlg[:], io["r_lng"], (), ("rlg",), grp="rlg")
            P.dma(rlb[:], io["r_lnb"], (), ("rlb",), grp="rlb")
            P.dma(rbs[:], io["r_bsp"], (), ("rbs",), grp="rbs")
            sm = self.sml
            tu = self.tmpA[:, 0:512]
            tz = self.tmpB[:, 0:512]
            stmp = self.tmpA[:, 512:1024]

            def U(blk):
                hk = ("hTo", blk)
                for cbk in range(8):
                    pu, ku = self.ps()
                    for kc in range(8):
                        self.mm(pu[:], w_ua[:, kc, cbk * 128:(cbk + 1) * 128], hTo[:, kc, blk * 512:(blk + 1) * 512],
                                kc == 0, kc == 7, (hk,) + self.wk("w_ua", cbk * 128, cbk * 128 + 128), (ku,))
                    pz, kz = self.ps()
                    for kc in range(8):
                        self.mm(pz[:], w_za[:, kc, cbk * 128:(cbk + 1) * 128], hTo[:, kc, blk * 512:(blk + 1) * 512],
                                kc == 0, kc == 7, (hk,) + self.wk("w_za", cbk * 128, cbk * 128 + 128), (kz,))
                    if cbk % 2 == 0:
                        self.act(tu, pu[:], AF.Gelu_apprx_tanh, (ku,), ("tu",))
                        self.act(tz, pz[:], AF.Silu, (kz,), ("tz",))
                    else:
                        self.act(tz, pz[:], AF.Silu, (kz,), ("tz",))
                        self.act(tu, pu[:], AF.Gelu_apprx_tanh, (ku,), ("tu",))
                    self.tt("pool", uz[:, cbk, :], tu, tz, ALU.mult, ("tu", "tz"), ("uz",))

            def V(t):
                blk = t // 4
                hk = ("hTo", blk)
                q = t % 3
                gv, vln = gvs[q], vlns[q]
                gk, vk = ("gv", q), ("vln", q)
                c0 = 24 + 8 * q
                K = lambda j: ("sml", c0 + j)
                C = lambda j: sm[:, c0 + j:c0 + j + 1]
                for half in range(2):
                    pv, kv = self.ps()
                    for kc in range(8):
                        self.mm(pv[:], hTo[:, kc, t * 128:(t + 1) * 128], w_va[:, kc, half * 512:(half + 1) * 512],
                                kc == 0, kc == 7, (hk,) + self.wk("w_va", half * 512, half * 512 + 512), (kv,))
                    self.act(gv[:, half * 512:(half + 1) * 512], pv[:], AF.Gelu_apprx_tanh, (kv,), (gk, K(half)), accum=C(half))
                self.act(yaT[:, :, t * 128:(t + 1) * 128], gv[:].rearrange("p (a b) -> p a b", b=128), AF.Square,
                         (gk,), (("yaT", blk), K(2)), accum=C(2))
                self.tt("dve", C(3), C(0), C(1), ALU.add, (K(0), K(1)), (K(3),))
                self.ts("dve", C(3), C(3), 1.0 / 1024, None, ALU.mult, None, (K(3),), (K(3),))
                self.tt("dve", C(5), C(3), C(3), ALU.mult, (K(3),), (K(5),))
                self.ts("dve", C(4), C(2), 1.0 / 1024, None, ALU.mult, None, (K(2),), (K(4),))
                self.tt("dve", C(4), C(4), C(5), ALU.subtract, (K(4), K(5)), (K(4),))
                self.rsqrt_small(C(4), C(4), 1.0, (K(4),), (K(4),))
                self.ts("dve", gv[:], gv[:], C(3), C(4), ALU.subtract, ALU.mult, (gk, K(3), K(4)), (gk,))
                self.tt("pool", gv[:], gv[:], rlg[:], ALU.mult, (gk, "rlg"), (gk,))
                self.tt("pool", vln[:], gv[:], rlb[:], ALU.add, (gk, "rlb"), (vk,))

            def S(t):
                blk, tt_ = divmod(t, 4)
                q = t % 3
                vln, vk = vlns[q], ("vln", q)
                for gh in range(2):
                    pS, kS = self.ps()
                    for gi in range(4):
                        g_ = gh * 4 + gi
                        self.mm(pS[:, gi * 128:(gi + 1) * 128], vln[:, g_ * 128:(g_ + 1) * 128], wsp[:, g_, :], True, True,
                                (vk, "wsp"), (kS,))
                    self.tt("dve", stmp, pS[:], rbs[:, gh * 512:(gh + 1) * 512], ALU.add, (kS, "rbs"), ("stmp",))
                    self.tt("pool", yaT[:, gh * 4:(gh + 1) * 4, t * 128:(t + 1) * 128],
                            stmp.rearrange("p (a b) -> p a b", b=128),
                            uz[:, gh * 4:(gh + 1) * 4, tt_ * 128:(tt_ + 1) * 128], ALU.mult, ("stmp", "uz"), (("yaT", blk),))

            V(0)
            V(1)
            for blk in range(4):
                U(blk)
                for tt_ in range(4):
                    t = blk * 4 + tt_
                    if t + 2 < 16:
                        V(t + 2)
                    S(t)
        P.barrier()
        self.dump("yaT", yaT[:], tuple(("yaT", b) for b in range(4)))
        with ExitStack() as s:
            sb = lambda name, shape, dt=F32: self.sb(s, name, shape, dt)
            w_ga = sb("w_ga", [128, 8, 1024], BF16)
            w_pa = sb("w_pa", [128, 8, 1024], BF16)
            w_out = sb("w_out", [128, 8, 1024], BF16)
            mT = sb("mT", [128, 8, 512], BF16)
            ob = [sb(f"ob{i}", [128, 1024]) for i in range(2)]
            self.alloc_x(s)
            self.load_w(w_ga, "w_ga", io["w_in"], OFF_G, 1024)
            self.load_w(w_pa, "w_pa", io["w_pa"], 0, 1024)
            self.load_w(w_out, "w_out", io["w_out"], 0, 1024)
            sm = self.sml
            for blk in range(4):
                hk = ("hTo", blk)
                for mb in range(8):
                    pg, kg = self.ps()
                    for kc in range(8):
                        self.mm(pg[:], w_ga[:, kc, mb * 128:(mb + 1) * 128], hTo[:, kc, blk * 512:(blk + 1) * 512],
                                kc == 0, kc == 7, (hk,) + self.wk("w_ga", mb * 128, mb * 128 + 128), (kg,))
                    self.act(self.tmpB[:, 0:512], pg[:], AF.Sigmoid, (kg,), ("tmpB",))
                    pA, kA = self.ps()
                    for kc in range(8):
                        self.mm(pA[:], w_pa[:, kc, mb * 128:(mb + 1) * 128], yaT[:, kc, blk * 512:(blk + 1) * 512],
                                kc == 0, kc == 7, (("yaT", blk),) + self.wk("w_pa", mb * 128, mb * 128 + 128), (kA,))
                    self.tt("dve", self.tmpB[:, 512:1024], pA[:], self.tmpB[:, 0:512], ALU.mult, (kA, "tmpB"), ("tmpB2",))
                    self.tt("pool", mT[:, mb, :], self.tmpB[:, 512:1024], Bg[:, mb, blk * 512:(blk + 1) * 512], ALU.add,
                            ("tmpB2", ("Bg", blk)), ("mT",))
                for tt_ in range(4):
                    t = blk * 4 + tt_
                    i = self.xi % self.nx
                    self.xi += 1
                    xs = self.xst[i]
                    P.dma(xs[:], io["x_own"][t * 128:(t + 1) * 128, :], (), (("xst", i),), grp=f"xst{i}")
                    pos = []
                    for half in range(2):
                        po, ko = self.ps()
                        for kc in range(8):
                            self.mm(po[:], mT[:, kc, tt_ * 128:(tt_ + 1) * 128], w_out[:, kc, half * 512:(half + 1) * 512],
                                    kc == 0, kc == 7, ("mT",) + self.wk("w_out", half * 512, half * 512 + 512), (ko,))
                        self.act(self.tmpA[:, half * 512:(half + 1) * 512], po[:], AF.Square, (ko,),
                                 ("tmpA", ("sml", 32 + half)), accum=sm[:, 32 + half:33 + half])
                        pos.append((po, ko))
                    self.tt("dve", sm[:, 34:35], sm[:, 32:33], sm[:, 33:34], ALU.add, (("sml", 32), ("sml", 33)), (("sml", 34),))
                    self.rsqrt_small(sm[:, 34:35], sm[:, 34:35], 1.0 / 1024, (("sml", 34),), (("sml", 34),))
                    o_ = ob[t % 2]
                    okey = ("ob", t % 2)
                    for half in range(2):
                        po, ko = pos[half]
                        self.stt(o_[:, half * 512:(half + 1) * 512], po[:], sm[:, 34:35],
                                 self.gate_bc[:, half * 512:(half + 1) * 512], ALU.mult, ALU.mult,
                                 (ko, ("sml", 34), "gate_bc"), (okey,))
                    self.tt("pool", o_[:], o_[:], xs[:], ALU.add, (okey, ("xst", i)), (okey,))
                    P.dma(io["y"][t * 128:(t + 1) * 128, :], o_[:], (okey,), (), grp=f"yout{t % 2}")


IN_SPECS = [
    ("x_own", [NOWN, D]), ("x_oth", [NOWN, D]), ("ctx", [NCTX, D]), ("cvec", [128, 16]),
    ("w_mod", [D, 3 * D]), ("w_in", [D, 9248]), ("w_ab", [D, 32]),
    ("w_pa", [D, D]), ("w_pb", [D, D]), ("w_out", [D, D]), ("w_spT", [128, 1024]),
    ("fparams", [128, 104]), ("r_dtb", [128, 16]), ("r_alog", [128, 16]),
    ("r_gpost", [128, D]), ("r_bmodg", [128, D]), ("r_lng", [128, D]), ("r_lnb", [128, D]),
    ("r_bsp", [128, D]), ("r_gon", [128, 128]), ("consts", [128, NCONST]),
]


def _program(nc, plan, debug=None):
    io = {}
    for name, shape in IN_SPECS:
        io[name] = nc.dram_tensor(name, shape, F32, kind="ExternalInput").ap()
    io["y"] = nc.dram_tensor("y", [NOWN, D], F32, kind="ExternalOutput").ap()
    P = Prog(nc, plan)
    with ExitStack() as st:
        if plan is not None:
            P.setup_sems(st)
        Builder(nc, P, debug).build(io)
    return P


def build_nc(debug=None):
    nc0 = bass.Bass("TRN2", target_bir_lowering=False)
    P0 = _program(nc0, None, debug)
    plan = P0.analyze()
    nc = bass.Bass("TRN2", target_bir_lowering=False)
    P1 = _program(nc, plan, debug)
    assert P1.idx == len(plan["ops"])
    return nc, plan


def make_consts():
    c = np.zeros((128, NCONST), np.float32)
    i = np.arange(128)
    p, f = i[:, None], i[None, :]
    c[:, C_ID:C_ID + 128] = (p == f)
    c[:, C_LE:C_LE + 128] = (p <= f)
    c[:, C_GT:C_GT + 128] = (p > f)
    c[:, C_GE:C_GE + 128] = (p >= f)
    c[:, C_LT:C_LT + 128] = (p < f)
    c[:, C_NEGU:C_NEGU + 512] = np.tile(np.where(p < f, NEGV, 0.0), (1, 4))
    c[:, C_NEGL:C_NEGL + 512] = np.tile(np.where(p > f, NEGV, 0.0), (1, 4))
    for k in range(7):
        b = 1 << k
        blk = i // (2 * b)
        half = (i // b) % 2
        same = blk[:, None] == blk[None, :]
        mF = same & (half[:, None] == 0) & (half[None, :] == 1)
        mB = same & (half[:, None] == 1) & (half[None, :] == 0)
        c[:, C_NM + k * 128:C_NM + (k + 1) * 128] = -mF.astype(np.float32) + (p == f)
        c[:, C_NM + (7 + k) * 128:C_NM + (8 + k) * 128] = -mB.astype(np.float32) + (p == f)
    return c


def fm(v, n):
    return np.ascontiguousarray(np.asarray(v, np.float32).reshape(n, 128).T)


def rows(v):
    v = np.asarray(v, np.float32).reshape(1, -1)
    return np.ascontiguousarray(np.broadcast_to(v, (128, v.shape[1])))


def make_in_maps(x, c, ctx, c_ctx, w_mod, b_mod, g_pre, g_post, w_in, w_conv, a_log, dt_bias,
                 g_onorm, gm_ln_g, gm_ln_b, w_sp, b_sp, w_pa, w_pb, w_out):
    f32 = lambda a: np.ascontiguousarray(np.asarray(a, np.float32))
    w_mod0, b_mod0, w_in0 = f32(w_mod[0]), f32(b_mod[0]), f32(w_in[0])
    w_pa0, w_pb0, w_out0 = f32(w_pa[0]), f32(w_pb[0]), f32(w_out[0])
    consts = make_consts()
    maps = []
    for r in range(8):
        b, half = r // 2, r % 2
        rev = half == 1
        xs = np.asarray(x[b], np.float32)
        cx = np.asarray(ctx[b], np.float32)
        if rev:
            xs = xs[::-1]
            cx = cx[::-1]
        wc = np.asarray(w_conv[0], np.float32)
        al = np.asarray(a_log[0], np.float32)
        db = np.asarray(dt_bias[0], np.float32)
        wsp = np.asarray(w_sp[0], np.float32)
        bsp = np.asarray(b_sp[0], np.float32)
        wab = w_in0[:, OFF_A:OFF_ZB]
        if rev:
            wc = wc[::-1]
            al = al[::-1]
            db = db[::-1]
            wsp = wsp[:, ::-1, ::-1]
            bsp = bsp[:, ::-1]
            wab = wab[:, [8, 9, 10, 11, 12, 13, 14, 15, 0, 1, 2, 3, 4, 5, 6, 7,
                          24, 25, 26, 27, 28, 29, 30, 31, 16, 17, 18, 19, 20, 21, 22, 23]]
        cvec = np.zeros((128, 16), np.float32)
        cb_fm = fm(c[b], 8)
        cc_fm = fm(c_ctx, 8)
        cvec[:, 0::2] = cb_fm
        cvec[:, 1::2] = cc_fm
        fpar = np.zeros((128, 104), np.float32)
        fpar[:, 0:8] = fm(g_pre[0], 8)
        fpar[:, 8:32] = fm(b_mod0, 24)
        for tap in range(3):
            fpar[:, 32 + tap * 24:32 + (tap + 1) * 24] = fm(wc[tap], 24)
        m = {
            "x_own": f32(xs[:NOWN]), "x_oth": f32(xs[NOWN:]), "ctx": f32(cx), "cvec": cvec,
            "w_mod": w_mod0, "w_in": w_in0, "w_ab": f32(wab), "w_pa": w_pa0, "w_pb": w_pb0, "w_out": w_out0,
            "w_spT": f32(np.transpose(wsp, (2, 0, 1)).reshape(128, 1024)),
            "fparams": fpar, "r_dtb": rows(db.reshape(-1)), "r_alog": rows(al.reshape(-1)),
            "r_gpost": rows(g_post[0]), "r_bmodg": rows(b_mod0[2048:]), "r_lng": rows(gm_ln_g[0]),
            "r_lnb": rows(gm_ln_b[0]), "r_bsp": rows(bsp.reshape(-1)), "r_gon": rows(g_onorm[0]),
            "consts": consts,
        }
        maps.append(m)
    return maps


_CACHE = {}


def kernel(**inputs):
    if "nc" not in _CACHE:
        _CACHE["nc"] = build_nc()[0]
    nc = _CACHE["nc"]
    maps = make_in_maps(**inputs)
    res = run_bass_kernel_spmd(nc, maps, core_ids=list(range(8)))
    out = np.empty((4, 4096, D), np.float32)
    for r in range(8):
        b, half = r // 2, r % 2
        y = np.asarray(res.results[r]["y"], np.float32)
        if half == 0:
            out[b, :NOWN] = y
        else:
            out[b, ::-1][:NOWN] = y
    return out
```
